# Optimizing a Trainium2 kernel written in Bass

```python
import math
import jax, jax.numpy as jnp
from jax import lax
import numpy as np

D_MODEL = 1024
BATCH = 16
SEQ = 256
DEPTH = 2
DEC_BATCH = 2
DEC_SEQ = 2048
PAST_LEN = 256

GRID_W = 64
CHUNK = 64
EPS = 1e-6
FFN_DIM = 2816
N_MOD = 9
GLA_HEADS = 4
GLA_DK = 32
GLA_DV = 64
GLA_RANK = 16
GLA_GATE_NORM = 16.0
GDN_HEADS = 4
GDN_DK = 128
GDN_DV = 128
GDN_CONV = 5
S5_GROUPS = 16
S5_CH = 16
S5_P = 64
GLA_W = GLA_HEADS * GLA_DV
GDN_W = GDN_HEADS * GDN_DV
S5_W = S5_GROUPS * S5_CH
MIX_W = GLA_W + GDN_W + S5_W
IN_SIZES = (GLA_HEADS * GLA_DK, GLA_HEADS * GLA_DK, GLA_W, GLA_RANK, GLA_RANK, GLA_W,
            GDN_HEADS * GDN_DK, GDN_HEADS * GDN_DK, GDN_W, GDN_HEADS, GDN_HEADS, GDN_HEADS, GDN_HEADS, GDN_W,
            S5_W)
IN_OFFSETS = tuple(sum(IN_SIZES[:i + 1]) for i in range(len(IN_SIZES) - 1))
IN_W = sum(IN_SIZES)

kernel_name = 'hymba_gla_gdn_s5_diffusion_step'


def _f32(a):
    return a.astype(jnp.float32)


def _rmsnorm(x, w):
    x32 = _f32(x)
    y = x32 * lax.rsqrt(jnp.mean(x32 * x32, axis=-1, keepdims=True) + EPS)
    return y * _f32(w)


def _l2norm(x):
    return x * lax.rsqrt(jnp.sum(x * x, axis=-1, keepdims=True) + EPS)


def _flip(t):
    return jnp.flip(t, axis=1)


def _chunk(t):
    b, n, h, d = t.shape
    return t.reshape(b, n // CHUNK, CHUNK, h, d).transpose(1, 0, 3, 2, 4)


def _unchunk(t):
    n, b, h, c, d = t.shape
    return t.transpose(1, 0, 3, 2, 4).reshape(b, n * c, h, d)


def _swiglu(h, w_gate, w_up, w_down):
    return (jax.nn.silu(h @ _f32(w_gate)) * (h @ _f32(w_up))) @ _f32(w_down)


def _short_conv(x, w, rows):
    b, n, ch = x.shape
    xr = x.reshape(b * rows, n // rows, ch)
    y = lax.conv_general_dilated(xr, _f32(w)[:, None, :], window_strides=(1,),
                                 padding=[(GDN_CONV // 2, GDN_CONV // 2)],
                                 dimension_numbers=('NWC', 'WIO', 'NWC'), feature_group_count=ch)
    return y.reshape(b, n, ch)


def _gla_chunked(q, k, v, log_g, s0):
    qc, kc, vc, gc = (_chunk(t) for t in (q, k, v, log_g))
    bcum = jnp.cumsum(gc, axis=-2)
    incl = jnp.tril(jnp.ones((CHUNK, CHUNK), bool))
    rel = jnp.where(incl[:, :, None], bcum[..., :, None, :] - bcum[..., None, :, :], -jnp.inf)
    att = jnp.einsum('nbhtd,nbhsd,nbhtsd->nbhts', qc, kc, jnp.exp(rel))
    o_intra = jnp.einsum('nbhts,nbhse->nbhte', att, vc)
    q_dec = qc * jnp.exp(bcum)
    k_dec = kc * jnp.exp(bcum[..., -1:, :] - bcum)
    dec_last = jnp.exp(bcum[..., -1, :])

    def step(s, inp):
        qd, kd, vv, dl = inp
        o = jnp.einsum('bhtd,bhde->bhte', qd, s)
        s = dl[..., None] * s + jnp.einsum('bhtd,bhte->bhde', kd, vv)
        return s, o

    s_fin, o_inter = lax.scan(step, s0, (q_dec, k_dec, vc, dec_last))
    return _unchunk(o_inter + o_intra), s_fin


def _gdn_chunked(q, k, v, log_a, beta, s0):
    qc, kc, vc = (_chunk(t) for t in (q, k, v))
    g = jnp.cumsum(_chunk(log_a[..., None])[..., 0], axis=-1)
    bt = _chunk(beta[..., None])[..., 0]
    incl = jnp.tril(jnp.ones((CHUNK, CHUNK), bool))
    strict = jnp.tril(jnp.ones((CHUNK, CHUNK), bool), -1)
    decay = jnp.exp(jnp.where(incl, g[..., :, None] - g[..., None, :], -jnp.inf))
    lmat = jnp.where(strict, decay, 0.0) * jnp.einsum('nbhtd,nbhsd->nbhts', kc, kc) * bt[..., :, None]
    eg = jnp.exp(g)
    rhs = jnp.concatenate([(bt * eg)[..., None] * kc, bt[..., None] * vc], axis=-1)
    sol = lax.linalg.triangular_solve(jnp.eye(CHUNK, dtype=lmat.dtype) + lmat, rhs,
                                      left_side=True, lower=True)
    w, u0 = jnp.split(sol, [kc.shape[-1]], axis=-1)
    qk = jnp.einsum('nbhtd,nbhsd->nbhts', qc, kc) * decay
    k_dec = kc * jnp.exp(g[..., -1:] - g)[..., None]
    eg_last = eg[..., -1]

    def step(s, inp):
        w_c, u0_c, q_c, kd_c, qk_c, eg_c, egl_c = inp
        u = u0_c - jnp.einsum('bhtd,bhde->bhte', w_c, s)
        o = eg_c[..., None] * jnp.einsum('bhtd,bhde->bhte', q_c, s) + jnp.einsum('bhts,bhse->bhte', qk_c, u)
        s = egl_c[..., None, None] * s + jnp.einsum('bhtd,bhte->bhde', kd_c, u)
        return s, o

    s_fin, o = lax.scan(step, s0, (w, u0, qc, k_dec, qk, eg, eg_last))
    return _unchunk(o), s_fin


def _s5_scan(u, lam_re, lam_im, log_step, b_re, b_im, c_re, c_im, h0_re, h0_im):
    lam = lax.complex(_f32(lam_re), _f32(lam_im))
    lam_dt = lam * jnp.exp(_f32(log_step))[:, None]
    lam_bar = jnp.exp(lam_dt)
    b_bar = ((lam_bar - 1.0) / lam)[..., None] * lax.complex(_f32(b_re), _f32(b_im))
    c_mat = lax.complex(_f32(c_re), _f32(c_im))
    bu = jnp.einsum('gpc,bngc->bngp', b_bar, u.astype(jnp.complex64))
    a = jnp.broadcast_to(lam_bar, bu.shape)

    def combine(left, right):
        a_l, b_l = left
        a_r, b_r = right
        return a_r * a_l, a_r * b_l + b_r

    _, h = lax.associative_scan(combine, (a, bu), axis=1)
    n = u.shape[1]
    pos = jnp.arange(1, n + 1, dtype=jnp.float32)[:, None, None]
    h = h + jnp.exp(lam_dt[None] * pos)[None] * lax.complex(_f32(h0_re), _f32(h0_im))[:, None]
    y = jnp.real(jnp.einsum('gcp,bngp->bngc', c_mat, h))
    return y, h[:, -1]


def _token_mix(h, rows, states, p, l):
    st_gla, st_gdn, st_s5_re, st_s5_im = (_f32(s) for s in states)
    b, n, _ = h.shape
    (gq, gk, gv, glr_f, glr_b, gog, dq, dk, dv, da_f, da_b, db_f, db_b, dog, su) = jnp.split(
        h @ _f32(p['w_in'][l]), IN_OFFSETS, axis=-1)

    q = gq.reshape(b, n, GLA_HEADS, GLA_DK) * GLA_DK ** -0.5
    k = gk.reshape(b, n, GLA_HEADS, GLA_DK)
    v = gv.reshape(b, n, GLA_HEADS, GLA_DV)
    o_gla = jnp.zeros((b, n, GLA_HEADS, GLA_DV), jnp.float32)
    fin_gla = []
    for d, lowrank in enumerate((glr_f, glr_b)):
        lg = jax.nn.log_sigmoid(lowrank @ _f32(p['gla_gk_up'][l, d]) + _f32(p['gla_gk_bias'][l, d])) / GLA_GATE_NORM
        lg = lg.reshape(b, n, GLA_HEADS, GLA_DK)
        args = (q, k, v, lg) if d == 0 else tuple(_flip(t) for t in (q, k, v, lg))
        o, s = _gla_chunked(*args, st_gla[:, d])
        o_gla = o_gla + (o if d == 0 else _flip(o))
        fin_gla.append(s)
    o_gla = _rmsnorm(o_gla, p['gla_norm_w'][l]) * jax.nn.silu(gog.reshape(b, n, GLA_HEADS, GLA_DV))
    o_gla = o_gla.reshape(b, n, GLA_W)

    qkv = jax.nn.silu(_short_conv(jnp.concatenate([dq, dk, dv], axis=-1), p['gdn_conv_w'][l], rows))
    q, k, v = jnp.split(qkv, [GDN_HEADS * GDN_DK, 2 * GDN_HEADS * GDN_DK], axis=-1)
    q = _l2norm(q.reshape(b, n, GDN_HEADS, GDN_DK)) * GDN_DK ** -0.5
    k = _l2norm(k.reshape(b, n, GDN_HEADS, GDN_DK))
    v = v.reshape(b, n, GDN_HEADS, GDN_DV)
    o_gdn = jnp.zeros((b, n, GDN_HEADS, GDN_DV), jnp.float32)
    fin_gdn = []
    for d, (a_raw, b_raw) in enumerate(((da_f, db_f), (da_b, db_b))):
        log_a = -jnp.exp(_f32(p['gdn_a_log'][l, d])) * jax.nn.softplus(a_raw + _f32(p['gdn_dt_bias'][l, d]))
        beta = jax.nn.sigmoid(b_raw)
        args = (q, k, v, log_a, beta) if d == 0 else tuple(_flip(t) for t in (q, k, v, log_a, beta))
        o, s = _gdn_chunked(*args, st_gdn[:, d])
        o_gdn = o_gdn + (o if d == 0 else _flip(o))
        fin_gdn.append(s)
    o_gdn = _rmsnorm(o_gdn, p['gdn_norm_w'][l]) * jax.nn.silu(dog.reshape(b, n, GDN_HEADS, GDN_DV))
    o_gdn = o_gdn.reshape(b, n, GDN_W)

    u = su.reshape(b, n, S5_GROUPS, S5_CH)
    y = u * _f32(p['s5_d'][l]).reshape(S5_GROUPS, S5_CH)
    fin_s5 = []
    for d in range(2):
        ud = u if d == 0 else _flip(u)
        yd, hf = _s5_scan(ud, p['s5_lam_re'][l, d], p['s5_lam_im'][l, d], p['s5_log_step'][l, d],
                          p['s5_b_re'][l, d], p['s5_b_im'][l, d], p['s5_c_re'][l, d], p['s5_c_im'][l, d],
                          st_s5_re[:, d], st_s5_im[:, d])
        y = y + (yd if d == 0 else _flip(yd))
        fin_s5.append(hf)
    g = jax.nn.gelu(y.reshape(b, n, S5_W))
    o_s5 = g * jax.nn.sigmoid(g @ _f32(p['s5_w_glu'][l]) + _f32(p['s5_b_glu'][l]))

    out = jnp.concatenate([o_gla, o_gdn, o_s5], axis=-1) @ _f32(p['w_out'][l])
    h_s5 = jnp.stack(fin_s5, axis=1)
    new = (jnp.stack(fin_gla, axis=1), jnp.stack(fin_gdn, axis=1), jnp.real(h_s5), jnp.imag(h_s5))
    return out, new


def _trunk(x, cond, rows, states, p):
    act = x.dtype
    sc = jax.nn.silu(_f32(cond))
    finals = []
    for l in range(DEPTH):
        mod = (sc @ _f32(p['w_ada'][l]) + _f32(p['b_ada'][l]))[:, None, :]
        sh1, sc1, g1, sh2, sc2, g2, sh3, sc3, g3 = jnp.split(mod, N_MOD, axis=-1)
        h = _rmsnorm(x, p['norm_w'][l, 0]) * (1.0 + sc1) + sh1
        x = x + (0.5 * g1 * _swiglu(h, p['ffn_w_gate'][l, 0], p['ffn_w_up'][l, 0], p['ffn_w_down'][l, 0])).astype(act)
        h = _rmsnorm(x, p['norm_w'][l, 1]) * (1.0 + sc2) + sh2
        y, st = _token_mix(h, rows, tuple(s[:, l] for s in states), p, l)
        x = x + (g2 * y).astype(act)
        h = _rmsnorm(x, p['norm_w'][l, 2]) * (1.0 + sc3) + sh3
        x = x + (0.5 * g3 * _swiglu(h, p['ffn_w_gate'][l, 1], p['ffn_w_up'][l, 1], p['ffn_w_down'][l, 1])).astype(act)
        finals.append(st)
    return _rmsnorm(x, p['final_norm_w']).astype(act), finals


def setup_inputs(seed: int = 0) -> dict:
    key = jax.random.key(seed)
    keys = jax.random.split(key, 40)

    def nrm(i, shape, s):
        return jax.random.normal(keys[i], shape, jnp.float32) * s

    def unif(i, shape, lo, hi):
        return jax.random.uniform(keys[i], shape, jnp.float32, minval=lo, maxval=hi)

    L = DEPTH
    dt = jnp.exp(unif(14, (L, 2, GDN_HEADS), math.log(1e-3), math.log(1e-1)))
    n_idx = jnp.arange(S5_P, dtype=jnp.float32)
    return {
        'x_prompt': nrm(0, (BATCH, SEQ, D_MODEL), 1.0),
        'x_sample': nrm(1, (DEC_BATCH, DEC_SEQ, D_MODEL), 1.0),
        'c': nrm(2, (DEC_BATCH, D_MODEL), 1.0),
        'state_gla': nrm(3, (DEC_BATCH, L, 2, GLA_HEADS, GLA_DK, GLA_DV), 0.1),
        'state_gdn': nrm(4, (DEC_BATCH, L, 2, GDN_HEADS, GDN_DK, GDN_DV), 0.1),
        'state_s5_re': nrm(5, (DEC_BATCH, L, 2, S5_GROUPS, S5_P), 0.5),
        'state_s5_im': nrm(6, (DEC_BATCH, L, 2, S5_GROUPS, S5_P), 0.5),
        'c_ctx': nrm(7, (D_MODEL,), 1.0),
        'w_ada': nrm(8, (L, D_MODEL, N_MOD * D_MODEL), D_MODEL ** -0.5),
        'b_ada': nrm(9, (L, N_MOD * D_MODEL), 0.02),
        'norm_w': 1.0 + nrm(10, (L, 3, D_MODEL), 0.05),
        'ffn_w_gate': nrm(11, (L, 2, D_MODEL, FFN_DIM), D_MODEL ** -0.5),
        'ffn_w_up': nrm(12, (L, 2, D_MODEL, FFN_DIM), D_MODEL ** -0.5),
        'ffn_w_down': nrm(13, (L, 2, FFN_DIM, D_MODEL), FFN_DIM ** -0.5),
        'w_in': nrm(15, (L, D_MODEL, IN_W), D_MODEL ** -0.5),
        'gla_gk_up': nrm(16, (L, 2, GLA_RANK, GLA_HEADS * GLA_DK), GLA_RANK ** -0.5),
        'gla_gk_bias': nrm(17, (L, 2, GLA_HEADS * GLA_DK), 0.1),
        'gla_norm_w': 1.0 + nrm(18, (L, GLA_DV), 0.05),
        'gdn_conv_w': nrm(19, (L, GDN_CONV, 2 * GDN_HEADS * GDN_DK + GDN_W), GDN_CONV ** -0.5),
        'gdn_a_log': jnp.log(unif(20, (L, 2, GDN_HEADS), 1.0, 16.0)),
        'gdn_dt_bias': dt + jnp.log(-jnp.expm1(-dt)),
        'gdn_norm_w': 1.0 + nrm(21, (L, GDN_DV), 0.05),
        's5_lam_re': -0.5 + nrm(22, (L, 2, S5_GROUPS, S5_P), 0.01),
        's5_lam_im': math.pi * n_idx + nrm(23, (L, 2, S5_GROUPS, S5_P), 0.01),
        's5_log_step': unif(24, (L, 2, S5_GROUPS), math.log(1e-3), math.log(1e-1)),
        's5_b_re': nrm(25, (L, 2, S5_GROUPS, S5_P, S5_CH), (2 * S5_CH) ** -0.5),
        's5_b_im': nrm(26, (L, 2, S5_GROUPS, S5_P, S5_CH), (2 * S5_CH) ** -0.5),
        's5_c_re': nrm(27, (L, 2, S5_GROUPS, S5_CH, S5_P), (2 * S5_P) ** -0.5),
        's5_c_im': nrm(28, (L, 2, S5_GROUPS, S5_CH, S5_P), (2 * S5_P) ** -0.5),
        's5_d': nrm(29, (L, S5_W), 1.0),
        's5_w_glu': nrm(30, (L, S5_W, S5_W), S5_W ** -0.5),
        's5_b_glu': nrm(31, (L, S5_W), 0.02),
        'w_out': nrm(32, (L, MIX_W, D_MODEL), MIX_W ** -0.5),
        'final_norm_w': 1.0 + nrm(33, (D_MODEL,), 0.05),
    }


def reference(x_prompt, x_sample, c, state_gla, state_gdn, state_s5_re, state_s5_im, c_ctx,
              w_ada, b_ada, norm_w, ffn_w_gate, ffn_w_up, ffn_w_down, w_in,
              gla_gk_up, gla_gk_bias, gla_norm_w, gdn_conv_w, gdn_a_log, gdn_dt_bias, gdn_norm_w,
              s5_lam_re, s5_lam_im, s5_log_step, s5_b_re, s5_b_im, s5_c_re, s5_c_im, s5_d,
              s5_w_glu, s5_b_glu, w_out, final_norm_w):
    p = dict(w_ada=w_ada, b_ada=b_ada, norm_w=norm_w, ffn_w_gate=ffn_w_gate, ffn_w_up=ffn_w_up,
             ffn_w_down=ffn_w_down, w_in=w_in, gla_gk_up=gla_gk_up, gla_gk_bias=gla_gk_bias,
             gla_norm_w=gla_norm_w, gdn_conv_w=gdn_conv_w, gdn_a_log=gdn_a_log, gdn_dt_bias=gdn_dt_bias,
             gdn_norm_w=gdn_norm_w, s5_lam_re=s5_lam_re, s5_lam_im=s5_lam_im, s5_log_step=s5_log_step,
             s5_b_re=s5_b_re, s5_b_im=s5_b_im, s5_c_re=s5_c_re, s5_c_im=s5_c_im, s5_d=s5_d,
             s5_w_glu=s5_w_glu, s5_b_glu=s5_b_glu, w_out=w_out, final_norm_w=final_norm_w)

    b_ctx = x_prompt.shape[0]
    zero_states = (jnp.zeros((b_ctx, DEPTH, 2, GLA_HEADS, GLA_DK, GLA_DV), jnp.float32),
                   jnp.zeros((b_ctx, DEPTH, 2, GDN_HEADS, GDN_DK, GDN_DV), jnp.float32),
                   jnp.zeros((b_ctx, DEPTH, 2, S5_GROUPS, S5_P), jnp.float32),
                   jnp.zeros((b_ctx, DEPTH, 2, S5_GROUPS, S5_P), jnp.float32))
    y_prompt, ctx_finals = _trunk(x_prompt, c_ctx[None, :], 1, zero_states, p)
    new_state_gla = jnp.stack([f[0] for f in ctx_finals], axis=1).astype(state_gla.dtype)
    new_state_gdn = jnp.stack([f[1] for f in ctx_finals], axis=1).astype(state_gdn.dtype)
    new_state_s5_re = jnp.stack([f[2] for f in ctx_finals], axis=1).astype(state_s5_re.dtype)
    new_state_s5_im = jnp.stack([f[3] for f in ctx_finals], axis=1).astype(state_s5_im.dtype)

    rows = x_sample.shape[1] // GRID_W
    y_sample, _ = _trunk(x_sample, c, rows, (state_gla, state_gdn, state_s5_re, state_s5_im), p)
    return (y_prompt, y_sample, new_state_gla, new_state_gdn, new_state_s5_re, new_state_s5_im)
```

```python
import numpy as np
from contextlib import ExitStack
import concourse.bass as bass
import concourse.mybir as mybir
from concourse.bass_utils import run_bass_kernel_spmd

F32 = mybir.dt.float32
BF16 = mybir.dt.bfloat16
AF = mybir.ActivationFunctionType
ALU = mybir.AluOpType

D = 1024
T = 2048
NS = 8
SL = 256
DEPTH = 2
FFN = 2816
NFC = 22
INW = 3120
TT = 512
NTT = T // TT
EPS = 1e-6
CT = 128
NCH = T // CT

O_GQ, O_GK, O_GV, O_GLF, O_GLB, O_GOG = 0, 128, 256, 512, 528, 544
O_DQ, O_DK, O_DV = 800, 1312, 1824
O_DAF, O_DAB, O_DBF, O_DBB, O_DOG, O_SU = 2336, 2340, 2344, 2348, 2352, 2864


class Buf:
    __slots__ = ("name", "w", "r")

    def __init__(self, name):
        self.name = name
        self.w = None
        self.r = []


class Eng:
    def __init__(self, name, is_pe=False):
        self.name = name
        self.ops = []
        self.count = 0
        self.seen = {}
        self.is_pe = is_pe


class Sched:
    NDMA = 6

    def __init__(self, nc):
        self.nc = nc
        self.engs = {n: Eng(n, n == "pe") for n in ("pe", "act", "dve", "pool", "sp")}
        self.dma_next = {"sp": 0, "pool": 0}
        self.dma_cnt = {}
        self.sem_keys = [("c", n) for n in ("pe", "act", "dve", "pool")]
        for q in ("sp", "pool"):
            for i in range(self.NDMA):
                self.sem_keys.append(("d", q, i))
                self.dma_cnt[("d", q, i)] = 0
        self.out_events = []
        self.nops = 0
        import os
        self.strict = os.environ.get("KSTRICT", "") == "1"

    def _deps(self, eng, reads, writes):
        deps = {}

        def add(ev, same_ok):
            if ev is None:
                return
            key, val, src = ev
            if src == eng.name and key[0] == "c":
                if eng.is_pe or same_ok:
                    return
            if eng.seen.get(key, 0) >= val:
                return
            if deps.get(key, 0) < val:
                deps[key] = val

        for b in reads:
            add(b.w, False)
        for b in writes:
            add(b.w, False)
            for ev in b.r:
                add(ev, False)
        for k, v in deps.items():
            eng.seen[k] = v
        return list(deps.items())

    def op(self, engname, fn, reads=(), writes=()):
        eng = self.engs[engname]
        waits = self._deps(eng, reads, writes)
        if self.strict:
            for m, e2 in self.engs.items():
                if m in ("sp",) or e2.count == 0:
                    continue
                key = ("c", m)
                if eng.seen.get(key, 0) < e2.count and not (m == engname and eng.is_pe):
                    waits.append((key, e2.count))
                    eng.seen[key] = e2.count
        eng.count += 1
        self.nops += 1
        ev = (("c", engname), eng.count, engname)
        eng.ops.append((waits, fn, "c", ev))
        for b in reads:
            b.r.append(ev)
        for b in writes:
            b.w = ev
            b.r = []
        return ev

    def dma(self, q, fn, reads=(), writes=(), is_out=False):
        eng = self.engs[q]
        i = self.dma_next[q]
        self.dma_next[q] = (i + 1) % self.NDMA
        key = ("d", q, i)
        waits = self._deps(eng, reads, writes)
        prev = self.dma_cnt[key] * 16
        if prev > 0 and eng.seen.get(key, 0) < prev:
            waits.append((key, prev))
            eng.seen[key] = prev
        self.dma_cnt[key] += 1
        self.nops += 1
        ev = (key, self.dma_cnt[key] * 16, "dma:" + q)
        eng.ops.append((waits, fn, "d", ev))
        for b in reads:
            b.r.append(ev)
        for b in writes:
            b.w = ev
            b.r = []
        if is_out:
            self.out_events.append(ev)
        return ev

    def barrier(self):
        last = {n: e.count for n, e in self.engs.items() if n != "sp"}
        dl = {k: c * 16 for k, c in self.dma_cnt.items() if c}
        for n, e in self.engs.items():
            waits = []
            for m, c in last.items():
                key = ("c", m)
                if m == n or c == 0:
                    continue
                if e.seen.get(key, 0) < c:
                    waits.append((key, c))
                    e.seen[key] = c
            for key, v in dl.items():
                if e.seen.get(key, 0) < v:
                    waits.append((key, v))
                    e.seen[key] = v
            if waits:
                e.ops.append((waits, None, "w", None))

    def emit(self, stack):
        nc = self.nc
        sems = {}
        for k in self.sem_keys:
            sems[k] = stack.enter_context(nc.semaphore("s_" + "_".join(str(x) for x in k)))
        sp = self.engs["sp"]
        fin = {}
        for n in ("pe", "act", "dve", "pool"):
            c = self.engs[n].count
            if c:
                fin[("c", n)] = c
        for key, cnt in self.dma_cnt.items():
            if cnt:
                fin[key] = cnt * 16
        sp.ops.append((list(fin.items()), None, "w", None))
        block = stack.enter_context(nc.Block())

        def replay(engname):
            def body(e):
                for waits, fn, kind, ev in self.engs[engname].ops:
                    for key, val in waits:
                        e.wait_ge(sems[key], val)
                    if fn is None:
                        continue
                    ins = getattr(e, fn[0])(**fn[1]) if isinstance(fn, tuple) else fn(e)
                    ins.then_inc(sems[ev[0]], 16 if kind == "d" else 1)
            return body

        block.tensor(replay("pe"))
        block.scalar(replay("act"))
        block.vector(replay("dve"))
        block.gpsimd(replay("pool"))
        block.sync(replay("sp"))


def build_program(enable_mix=True, dbg=False):
    nc = bass.Bass("TRN2", target_bir_lowering=False)

    def din(name, shape):
        return nc.dram_tensor(name, list(shape), F32, kind="ExternalInput").ap()

    def dout(name, shape):
        return nc.dram_tensor(name, list(shape), F32, kind="ExternalOutput").ap()

    xT_in = din("xT", [128, 8, T])
    cond_in = din("cond", [128, 8])
    w_ada = din("w_ada", [DEPTH, D, 9 * D])
    b_ada = din("b_ada", [DEPTH, 128, 72])
    norm_w = din("norm_w", [128, DEPTH * 3 * 8])
    fnorm_w = din("fnorm_w", [128, 8])
    import os
    dbg_mode = os.environ.get("KDBG", "")
    if dbg_mode != "noffn":
        w_gate = din("ffn_w_gate", [DEPTH, 2, D, FFN])
        w_up = din("ffn_w_up", [DEPTH, 2, D, FFN])
        w_down = din("ffn_w_down", [DEPTH, 2, FFN, D])
    yT_out = dout("yT", [128, 8, T])
    w_in = din("w_in", [DEPTH, D, INW])
    w_out = din("w_out", [DEPTH, D, D])
    gla_up = din("gla_up", [DEPTH, 16, 256])
    gla_bias = din("gla_bias", [128, DEPTH * 2])
    gla_nw = din("gla_nw", [DEPTH, 128, 256])
    st_gla = din("st_gla", [DEPTH, 2, 128, 64])
    o_stgla = dout("o_stgla", [DEPTH, 2, NS, 128, 64])
    gdn_convw = din("gdn_convw", [128, DEPTH * 60])
    gdn_dtb = din("gdn_dtb", [128, DEPTH * 8])
    gdn_alog = din("gdn_alog", [128, DEPTH * 8])
    gdn_nw = din("gdn_nw", [DEPTH, 128, 512])
    st_gdn = din("st_gdn", [DEPTH, 2, 128, 512])
    o_stgdn = dout("o_stgdn", [DEPTH, 2, NS, 128, 512])
    c_cmask = din("c_cmask", [128, 4, T])
    c_mkfs = din("c_mkfs", [128, 128])
    c_blk = din("c_blk", [128, 7, 128])
    s5_lr_p = din("s5_lr_p", [128, DEPTH * 16])
    s5_li_p = din("s5_li_p", [128, DEPTH * 16])
    s5_ls_p = din("s5_ls_p", [128, DEPTH * 16])
    s5_h0re = din("s5_h0re", [128, DEPTH * 16])
    s5_h0im = din("s5_h0im", [128, DEPTH * 16])
    s5_lr_row = din("s5_lr_row", [128, DEPTH * 2048])
    s5_li_row = din("s5_li_row", [128, DEPTH * 2048])
    s5_ls_row = din("s5_ls_row", [128, DEPTH * 2048])
    s5_Bre = din("s5_Bre", [128, DEPTH * 2048])
    s5_Bim = din("s5_Bim", [128, DEPTH * 2048])
    s5_Cre = din("s5_Cre", [128, DEPTH * 2048])
    s5_Cim = din("s5_Cim", [128, DEPTH * 2048])
    s5_D = din("s5_D", [128, DEPTH * 2])
    s5_bglu = din("s5_bglu", [128, DEPTH * 2])
    s5_wglu = din("s5_wglu", [DEPTH, 256, 256])
    c_tix = din("c_tix", [128, 512])
    c_cm5 = din("c_cm5", [128, 2, 512])
    c_negpi = din("c_negpi", [128, 1])
    o_s5 = dout("o_s5", [DEPTH, 128, 256])
    c_mkbs = din("c_mkbs", [128, 128])
    zq_t = nc.dram_tensor("zq_scr", [NCH, 128, 1536], F32).ap()
    ogs_t = nc.dram_tensor("ogs_scr", [NCH, 128, 512], F32).ap()
    mixers = os.environ.get("KMIX", "gla,gdn,s5").split(",")
    c_ident = din("c_ident", [128, 128])
    c_mkf = din("c_mkf", [128, 128])
    c_mkb = din("c_mkb", [128, 128])
    c_hmask = din("c_hmask", [128, 4])
    c_carry = din("c_carry", [128, 1])

    S = Sched(nc)
    with ExitStack() as st:
        uid = [0]

        def sb(name, shape, dt=F32, stack=st):
            uid[0] += 1
            t_ = stack.enter_context(nc.sbuf_tensor(f"{name}_{uid[0]}", list(shape), dt))
            if os.environ.get("KALLOC", ""):
                a_ = nc.lookup_mloc(t_).addr
                nb_ = int(np.prod(shape[1:])) * (2 if dt == BF16 else 4)
                print("ALLOC", f"{name}_{uid[0]}", a_, a_ + nb_)
            return t_

        x = sb("x", [128, 8, T])
        xB = [[Buf(f"x{dc}_{tt}") for tt in range(NTT)] for dc in range(8)]
        ones = sb("ones", [128, 128])
        b_ones = Buf("ones")
        prm = sb("prm", [128, DEPTH * 72])
        b_prm = Buf("prm")
        nwt = sb("nwt", [128, DEPTH * 24])
        b_nwt = Buf("nwt")
        fnw = sb("fnw", [128, 8])
        b_fnw = Buf("fnw")
        coefA = sb("coefA", [128, DEPTH * 24])
        b_coefA = Buf("coefA")
        coefG = sb("coefG", [128, DEPTH * 24])
        b_coefG = Buf("coefG")
        zero8 = sb("zero8", [128, 8])
        b_zero8 = Buf("zero8")
        psb = [st.enter_context(nc.psum_tensor(f"ps{i}", [128, 512], F32)) for i in range(8)]
        psB = [Buf(f"ps{i}") for i in range(8)]
        ps_i = [0]
        ps_lo = [0]

        def psum():
            i = ps_i[0]
            if i < ps_lo[0]:
                i = ps_lo[0]
            nxt = i + 1
            if nxt >= 8:
                nxt = ps_lo[0]
            ps_i[0] = nxt
            return psb[i], psB[i]

        def OP(eng, meth, r, w, **kw):
            return S.op(eng, (meth, kw), r, w)

        def DMA(q, out, in_, r, w, is_out=False):
            return S.dma(q, ("dma_start", dict(out=out, in_=in_)), r, w, is_out)

        ident = sb("ident", [128, 128])
        mk_f = sb("mk_f", [128, 128])
        mk_b = sb("mk_b", [128, 128])
        hmask = sb("hmask", [128, 4])
        carry = sb("carry", [128, 1])
        b_cst = Buf("cst")
        mk_fs = sb("mk_fs", [128, 128])
        blkm = sb("blkm", [128, 7, 128])
        negpi = sb("negpi", [128, 1])
        bd16 = blkm[:, 0, :]
        MLm = [blkm[:, 1 + i, :] for i in range(3)]
        MUm = [blkm[:, 4 + i, :] for i in range(3)]
        mk_bs = sb("mk_bs", [128, 128])
        zq = [zq_t[c] for c in range(NCH)]
        ogs = [ogs_t[c] for c in range(NCH)]
        b_zq = [Buf(f"zq{c}") for c in range(NCH)]
        b_ogs = [Buf(f"ogs{c}") for c in range(NCH)]
        for dc in range(8):
            DMA("sp", x[:, dc, :], xT_in[:, dc, :], [], [xB[dc][tt] for tt in range(NTT)])
        OP("dve", "memset", [], [b_ones], ap=ones[:], constant=1.0)
        OP("dve", "memset", [], [b_zero8], ap=zero8[:], constant=0.0)
        DMA("sp", nwt[:], norm_w, [], [b_nwt])
        DMA("sp", ident[:], c_ident, [], [b_cst])
        DMA("sp", mk_f[:], c_mkf, [], [b_cst])
        DMA("sp", mk_b[:], c_mkb, [], [b_cst])
        DMA("sp", hmask[:], c_hmask, [], [b_cst])
        DMA("sp", carry[:], c_carry, [], [b_cst])
        DMA("sp", mk_fs[:], c_mkfs, [], [b_cst])
        DMA("sp", blkm[:], c_blk, [], [b_cst])
        DMA("sp", negpi[:], c_negpi, [], [b_cst])
        DMA("sp", mk_bs[:], c_mkbs, [], [b_cst])
        DMA("sp", fnw[:], fnorm_w, [], [b_fnw])

        with ExitStack() as st_ada:
            cnd = sb("cnd", [128, 8], stack=st_ada)
            b_cnd = Buf("cnd")
            sc = sb("sc", [128, 8], stack=st_ada)
            b_sc = Buf("sc")
            bada = sb("bada", [128, DEPTH * 72], stack=st_ada)
            b_bada = Buf("bada")
            wa = [sb(f"wa{i}", [128, 8, 256], stack=st_ada) for i in range(2)]
            b_wa = [Buf(f"wa{i}") for i in range(2)]
            DMA("sp", cnd[:], cond_in, [], [b_cnd])
            for l in range(DEPTH):
                DMA("sp", bada[:, l * 72:(l + 1) * 72], b_ada[l], [], [b_bada])
            OP("act", "activation", [b_cnd], [b_sc], out=sc[:], in_=cnd[:], func=AF.Silu)
            for l in range(DEPTH):
                pm, bpm = psum()
                wsrc = w_ada[l].rearrange("(kc p) n -> p kc n", p=128)
                for blk in range(36):
                    wt, bw = wa[blk % 2], b_wa[blk % 2]
                    DMA("sp", wt[:], wsrc[:, :, blk * 256:(blk + 1) * 256], [], [bw])
                    for m in range(2):
                        j = blk * 2 + m
                        for kc in range(8):
                            OP("pe", "matmul", [bw, b_sc], [bpm], out=pm[:, j:j + 1], lhsT=wt[:, kc, m * 128:(m + 1) * 128], rhs=sc[:, kc:kc + 1],
                               start=(kc == 0), stop=(kc == 7))
                OP("dve", "tensor_tensor", [bpm, b_bada], [b_prm], out=prm[:, l * 72:(l + 1) * 72], in0=pm[:, 0:72], in1=bada[:, l * 72:(l + 1) * 72], op=ALU.add)
                for n in range(3):
                    o = l * 72 + (3 * n + 1) * 8
                    c0 = l * 24 + n * 8
                    OP("dve", "scalar_tensor_tensor", [b_prm, b_nwt], [b_coefA], out=coefA[:, c0:c0 + 8], in0=prm[:, o:o + 8], scalar=1.0, in1=nwt[:, c0:c0 + 8],
                       op0=ALU.add, op1=ALU.mult)
                    og = l * 72 + (3 * n + 2) * 8
                    gs = 1.0 if n == 1 else 0.5
                    OP("dve", "tensor_scalar", [b_prm], [b_coefG], out=coefG[:, c0:c0 + 8], in0=prm[:, og:og + 8], scalar1=gs, scalar2=None, op0=ALU.mult)
            S.barrier()

        sq = [sb(f"sq{i}", [128, TT]) for i in range(2)]
        b_sq = [Buf(f"sq{i}") for i in range(2)]
        rstd = sb("rstd", [128, TT])
        b_rstd = Buf("rstd")
        tn = [sb(f"tn{i}", [128, TT]) for i in range(2)]
        b_tn = [Buf(f"tn{i}") for i in range(2)]
        cnt = {"sq": 0, "tn": 0}

        def norm_stats(tt):
            ts = slice(tt * TT, (tt + 1) * TT)
            pss, bps = psum()
            for dc in range(8):
                i = cnt["sq"] % 2
                cnt["sq"] += 1
                OP("act", "activation", [xB[dc][tt]], [b_sq[i]], out=sq[i][:], in_=x[:, dc, ts], func=AF.Square)
                OP("pe", "matmul", [b_sq[i], b_ones], [bps], out=pss[:, 0:TT], lhsT=ones[:], rhs=sq[i][:], start=(dc == 0), stop=(dc == 7))
            OP("act", "activation", [bps], [b_rstd], out=rstd[:], in_=pss[:, 0:TT], func=AF.Sqrt, scale=1.0 / D, bias=EPS)
            OP("dve", "reciprocal", [b_rstd], [b_rstd], out=rstd[:], in_=rstd[:])

        def norm_tile(tt, A_ap, sh_ap, reads_coef, dst_fn, dst_bufs):
            ts = slice(tt * TT, (tt + 1) * TT)
            norm_stats(tt)
            for dc in range(8):
                i = cnt["tn"] % 2
                cnt["tn"] += 1
                OP("dve", "tensor_tensor", [xB[dc][tt], b_rstd], [b_tn[i]], out=tn[i][:], in0=x[:, dc, ts], in1=rstd[:], op=ALU.mult)
                OP("act", "activation", [b_tn[i]] + reads_coef, dst_bufs(dc), out=dst_fn(dc), in_=tn[i][:], func=AF.Identity,
                   scale=A_ap[:, dc:dc + 1], bias=sh_ap[:, dc:dc + 1])

        def mark(name):
            if os.environ.get("KPHASE", ""):
                print("PHASE", name, S.engs["pe"].count)

        def ffn_phase(l, i, n):
            mark(f"ffn{l}_{i}")
            with ExitStack() as sf:
                hT = sb("hT", [128, 8, TT], BF16, stack=sf)
                b_hT = Buf("hT")
                act = sb("act", [128, NFC, TT], BF16, stack=sf)
                b_act = [Buf(f"act{fc}") for fc in range(NFC)]
                wg = [sb(f"wg{k}", [128, 8, 256], BF16, stack=sf) for k in range(2)]
                wu = [sb(f"wu{k}", [128, 8, 256], BF16, stack=sf) for k in range(2)]
                wd = [sb(f"wd{k}", [128, NFC, 128], BF16, stack=sf) for k in range(2)]
                b_wg = [Buf(f"wg{k}") for k in range(2)]
                b_wu = [Buf(f"wu{k}") for k in range(2)]
                b_wd = [Buf(f"wd{k}") for k in range(2)]
                sg = [sb(f"sg{k}", [128, TT], stack=sf) for k in range(2)]
                b_sg = [Buf(f"sg{k}") for k in range(2)]
                gsrc = w_gate[l, i].rearrange("(kc p) n -> p kc n", p=128)
                usrc = w_up[l, i].rearrange("(kc p) n -> p kc n", p=128)
                dsrc = w_down[l, i].rearrange("(fc p) n -> p fc n", p=128)
                c0 = l * 24 + n * 8
                osh = l * 72 + (3 * n) * 8
                A_ap = coefA[:, c0:c0 + 8]
                sh_ap = prm[:, osh:osh + 8]
                G_ap = coefG[:, c0:c0 + 8]
                wcnt = 0
                dcnt = 0
                scnt = 0
                for tt in range(NTT):
                    ts = slice(tt * TT, (tt + 1) * TT)
                    norm_tile(tt, A_ap, sh_ap, [b_coefA, b_prm], lambda dc: hT[:, dc, :], lambda dc: [b_hT])
                    for fb in range(11):
                        k = wcnt % 2
                        wcnt += 1
                        DMA("pool", wg[k][:], gsrc[:, :, fb * 256:(fb + 1) * 256], [], [b_wg[k]])
                        DMA("pool", wu[k][:], usrc[:, :, fb * 256:(fb + 1) * 256], [], [b_wu[k]])
                        for sub in range(2):
                            fc = fb * 2 + sub
                            pg, bpg = psum()
                            pu, bpu = psum()
                            for kc in range(8):
                                OP("pe", "matmul", [b_wg[k], b_hT], [bpg], out=pg[:, 0:TT], lhsT=wg[k][:, kc, sub * 128:(sub + 1) * 128], rhs=hT[:, kc, :],
                                   start=(kc == 0), stop=(kc == 7))
                            for kc in range(8):
                                OP("pe", "matmul", [b_wu[k], b_hT], [bpu], out=pu[:, 0:TT], lhsT=wu[k][:, kc, sub * 128:(sub + 1) * 128], rhs=hT[:, kc, :],
                                   start=(kc == 0), stop=(kc == 7))
                            s_ = scnt % 2
                            scnt += 1
                            OP("act", "activation", [bpg], [b_sg[s_]], out=sg[s_][:], in_=pg[:, 0:TT], func=AF.Silu)
                            OP("dve", "tensor_tensor", [b_sg[s_], bpu], [b_act[fc]], out=act[:, fc, :], in0=sg[s_][:], in1=pu[:, 0:TT], op=ALU.mult)
                    for dc in range(8):
                        k = dcnt % 2
                        dcnt += 1
                        DMA("pool", wd[k][:], dsrc[:, :, dc * 128:(dc + 1) * 128], [], [b_wd[k]])
                        po, bpo = psum()
                        for fc in range(NFC):
                            OP("pe", "matmul", [b_wd[k], b_act[fc]], [bpo], out=po[:, 0:TT], lhsT=wd[k][:, fc, :], rhs=act[:, fc, :],
                               start=(fc == 0), stop=(fc == NFC - 1))
                        OP("dve", "scalar_tensor_tensor", [bpo, b_coefG, xB[dc][tt]], [xB[dc][tt]], out=x[:, dc, ts], in0=po[:, 0:TT], scalar=G_ap[:, dc:dc + 1],
                           in1=x[:, dc, ts], op0=ALU.mult, op1=ALU.add)
                S.barrier()

        def wout_part(l, mc0, nmc, om, b_om):
            with ExitStack() as sw:
                Wo = sb("Wo", [128, nmc, D], BF16, stack=sw)
                b_Wo = Buf("Wo")
                osrc = w_out[l].rearrange("(mc p) n -> p mc n", p=128)
                DMA("pool", Wo[:], osrc[:, mc0:mc0 + nmc, :], [], [b_Wo])
                c0 = l * 24 + 8
                for tt in range(NTT):
                    ts = slice(tt * TT, (tt + 1) * TT)
                    for dc in range(8):
                        pW, bW = psum()
                        for mc in range(nmc):
                            OP("pe", "matmul", [b_Wo] + b_om[tt * 4:(tt + 1) * 4], [bW], out=pW[:, 0:TT], lhsT=Wo[:, mc, dc * 128:(dc + 1) * 128], rhs=om[:, mc, ts],
                               start=(mc == 0), stop=(mc == nmc - 1))
                        OP("dve", "scalar_tensor_tensor", [bW, b_coefG, xB[dc][tt]], [xB[dc][tt]], out=x[:, dc, ts], in0=pW[:, 0:TT], scalar=coefG[:, c0 + dc:c0 + dc + 1],
                           in1=x[:, dc, ts], op0=ALU.mult, op1=ALU.add)
                S.barrier()

        def mixer_phase(l):
            with ExitStack() as sm:
                U = sb("U", [128, 2, T], BF16, stack=sm)
                b_U = Buf("U")
                b_om = [Buf(f"om{c}") for c in range(NCH)]
                wsrc = w_in[l].rearrange("(kc p) n -> p kc n", p=128)
                sh = ExitStack()
                hTm = sb("hTm", [128, 8, T], BF16, stack=sh)
                b_hTm = [Buf(f"hTm{tt}") for tt in range(NTT)]
                c0 = l * 24 + 8
                osh = l * 72 + 3 * 8
                for tt in range(NTT):
                    ts = slice(tt * TT, (tt + 1) * TT)
                    norm_tile(tt, coefA[:, c0:c0 + 8], prm[:, osh:osh + 8], [b_coefA, b_prm], lambda dc: hTm[:, dc, ts], lambda dc: [b_hTm[tt]])
                with ExitStack() as su:
                    Ws5 = sb("Ws5", [128, 8, 256], BF16, stack=su)
                    b_Ws = Buf("Ws5")
                    DMA("pool", Ws5[:], wsrc[:, :, O_SU:O_SU + 256], [], [b_Ws])
                    for gh in range(2):
                        for tb in range(4):
                            ts = slice(tb * 512, (tb + 1) * 512)
                            pu_, bpu_ = psum()
                            for kc in range(8):
                                OP("pe", "matmul", [b_Ws, b_hTm[tb]], [bpu_], out=pu_[:, 0:512], lhsT=Ws5[:, kc, gh * 128:(gh + 1) * 128], rhs=hTm[:, kc, ts], start=(kc == 0), stop=(kc == 7))
                            OP("act", "copy", [bpu_], [b_U], out=U[:, gh, ts], in_=pu_[:, 0:512])
                S.barrier()
                mark(f"gla{l}")
                with ExitStack() as sg_:
                    omG = sb("omG", [128, 2, T], BF16, stack=sg_)
                    Wqk = sb("Wqk", [128, 8, 256], BF16, stack=sg_)
                    Wv = sb("Wv", [128, 8, 256], BF16, stack=sg_)
                    Wlr = sb("Wlr", [128, 8, 32], BF16, stack=sg_)
                    Wog = sb("Wog", [128, 8, 256], BF16, stack=sg_)
                    b_W = Buf("glaW")
                    DMA("pool", Wqk[:], wsrc[:, :, O_GQ:O_GQ + 256], [], [b_W])
                    DMA("pool", Wv[:], wsrc[:, :, O_GV:O_GV + 256], [], [b_W])
                    DMA("pool", Wlr[:], wsrc[:, :, O_GLF:O_GLF + 32], [], [b_W])
                    DMA("pool", Wog[:], wsrc[:, :, O_GOG:O_GOG + 256], [], [b_W])
                    up = sb("up", [16, 256], stack=sg_)
                    nb = sb("nb", [128, 2], stack=sg_)
                    nwr = sb("nwr", [128, 256], stack=sg_)
                    b_gp = Buf("glap")
                    DMA("sp", up[:], gla_up[l], [], [b_gp])
                    DMA("sp", nb[:], gla_bias[:, l * 2:l * 2 + 2], [], [b_gp])
                    DMA("sp", nwr[:], gla_nw[l], [], [b_gp])
                    OP("dve", "tensor_scalar", [b_gp], [b_gp], out=nb[:], in0=nb[:], scalar1=-1.0, scalar2=None, op0=ALU.mult)
                    ogla = sb("ogla", [128, NCH, 256], stack=sg_)
                    b_og = [Buf(f"og{c}") for c in range(NCH)]
                    S0 = sb("S0", [128, 2, 64], stack=sg_)
                    b_S0 = Buf("S0")
                    DMA("sp", S0[:, 0, :], st_gla[l, 0], [], [b_S0])
                    DMA("sp", S0[:, 1, :], st_gla[l, 1], [], [b_S0])
                    gsets = []
                    for si in range(2):
                        t2 = lambda n_, shp: sb(f"g{si}{n_}", shp, stack=sg_)
                        Sg = t2("Sg", [128, 64]); b_S = Buf(f"Sg{si}")
                        lr_sb = t2("lr", [16, 128]); b_lr = Buf(f"lr{si}")
                        spt = t2("spt", [128, 128]); cst = t2("cst", [128, 128]); eb = t2("eb", [128, 128]); einv = t2("einv", [128, 128]); edec = t2("edec", [128, 128])
                        b_e = Buf(f"e{si}")
                        sm1 = t2("sm1", [128, 2]); b_sm1 = Buf(f"sm1{si}")
                        qd = t2("qd", [128, 128]); ki = t2("ki", [128, 128]); kd = t2("kd", [128, 128]); b_qk = Buf(f"qk{si}")
                        rb = t2("rb", [128, 4, 128]); b_rb = Buf(f"rb{si}")
                        attm = t2("attm", [128, 4, 128]); b_att = Buf(f"att{si}")
                        v_sb = t2("v_sb", [128, 256]); b_v = Buf(f"v{si}")
                        kdm = t2("kdm", [128, 4, 128]); b_kdm = Buf(f"kdm{si}")
                        OP("pool", "memset", [], [b_kdm], ap=kdm[:], constant=0.0)
                        gsets.append((Sg, b_S, lr_sb, b_lr, spt, cst, eb, einv, edec, b_e, sm1, b_sm1, qd, ki, kd, b_qk, rb, b_rb, attm, b_att, v_sb, b_v, kdm, b_kdm))
                    sto = [sb(f"sto{k}", [128, 64], stack=sg_) for k in range(2)]
                    b_sto = [Buf(f"sto{k}") for k in range(2)]
                    stc = [0]
                    ogw = [False] * NCH

                    def gla_dir(d, B):
                        (Sg, b_S, lr_sb, b_lr, spt, cst, eb, einv, edec, b_e, sm1, b_sm1, qd, ki, kd, b_qk, rb, b_rb, attm, b_att, v_sb, b_v, kdm, b_kdm) = B
                        order = list(range(NCH)) if d == 0 else list(range(NCH - 1, -1, -1))
                        mk = mk_f if d == 0 else mk_b
                        for ci, c in enumerate(order):
                            cs = slice(c * CT, (c + 1) * CT)
                            tt = c // 4
                            slot_first = (c % 2 == 0) if d == 0 else (c % 2 == 1)
                            slot_last = not slot_first
                            slot = c // 2
                            pA, bA = psum()
                            for kc in range(8):
                                OP("pe", "matmul", [b_W, b_hTm[tt]], [bA], out=pA[:, 0:128], lhsT=Wqk[:, kc, 0:128], rhs=hTm[:, kc, cs], start=(kc == 0), stop=(kc == 7))
                            for kc in range(8):
                                OP("pe", "matmul", [b_W, b_hTm[tt]], [bA], out=pA[:, 128:256], lhsT=Wqk[:, kc, 128:256], rhs=hTm[:, kc, cs], start=(kc == 0), stop=(kc == 7))
                            for kc in range(8):
                                OP("pe", "matmul", [b_W, b_hTm[tt]], [bA], out=pA[0:16, 256:384], lhsT=Wlr[:, kc, 16 * d:16 * d + 16], rhs=hTm[:, kc, cs], start=(kc == 0), stop=(kc == 7))
                            OP("act", "copy", [bA], [b_lr], out=lr_sb[:], in_=pA[0:16, 256:384])
                            yield
                            pZ, bZ = psum()
                            OP("pe", "matmul", [b_lr, b_gp], [bZ], out=pZ[:, 0:128], lhsT=up[:, d * 128:(d + 1) * 128], rhs=lr_sb[:], start=True, stop=True)
                            OP("act", "activation", [bZ, b_gp], [b_e], out=spt[:], in_=pZ[:, 0:128], func=AF.Exp, scale=-1.0, bias=nb[:, d:d + 1])
                            OP("act", "activation", [b_e], [b_e], out=spt[:], in_=spt[:], func=AF.Ln, bias=1.0)
                            if d == 0:
                                OP("dve", "tensor_tensor_scan", [b_e, b_ones], [b_e], out=cst[:], data0=ones[:], data1=spt[:], initial=0.0, op0=ALU.mult, op1=ALU.add)
                                last = cst[:, 127:128]
                            else:
                                OP("dve", "tensor_tensor_scan", [b_e, b_ones], [b_e], out=cst[:, ::-1], data0=ones[:], data1=spt[:, ::-1], initial=0.0, op0=ALU.mult, op1=ALU.add)
                                last = cst[:, 0:1]
                            OP("dve", "tensor_scalar", [b_e], [b_sm1], out=sm1[:, 0:1], in0=last, scalar1=-1.0 / 16.0, scalar2=None, op0=ALU.mult)
                            OP("act", "activation", [b_sm1], [b_sm1], out=sm1[:, 1:2], in_=sm1[:, 0:1], func=AF.Exp)
                            OP("act", "activation", [b_e], [b_e], out=eb[:], in_=cst[:], func=AF.Exp, scale=-1.0 / 16.0)
                            OP("act", "activation", [b_e], [b_e], out=einv[:], in_=cst[:], func=AF.Exp, scale=1.0 / 16.0)
                            OP("act", "activation", [b_e, b_sm1], [b_e], out=edec[:], in_=cst[:], func=AF.Exp, scale=1.0 / 16.0, bias=sm1[:, 0:1])
                            OP("dve", "scalar_tensor_tensor", [bA, b_e], [b_qk], out=qd[:], in0=pA[:, 0:128], scalar=32.0 ** -0.5, in1=eb[:], op0=ALU.mult, op1=ALU.mult)
                            OP("dve", "tensor_tensor", [bA, b_e], [b_qk], out=ki[:], in0=pA[:, 128:256], in1=einv[:], op=ALU.mult)
                            OP("dve", "tensor_tensor", [bA, b_e], [b_qk], out=kd[:], in0=pA[:, 128:256], in1=edec[:], op=ALU.mult)
                            OP("dve", "tensor_tensor", [b_qk, b_cst], [b_rb], out=rb[:], in0=qd[:].unsqueeze(1).to_broadcast([128, 4, 128]),
                               in1=hmask[:].unsqueeze(2).to_broadcast([128, 4, 128]), op=ALU.mult)
                            yield
                            pT, bT = psum()
                            OP("pe", "matmul", [b_qk, b_rb], [bT], out=pT[:, 0:512], lhsT=ki[:], rhs=rb[:].rearrange("p h t -> p (h t)"), start=True, stop=True)
                            OP("dve", "tensor_tensor", [bT, b_cst], [b_att], out=attm[:], in0=pT[:, 0:512].rearrange("p (h t) -> p h t", h=4),
                               in1=mk[:].unsqueeze(1).to_broadcast([128, 4, 128]), op=ALU.mult)
                            yield
                            pV, bV = psum()
                            for kc in range(8):
                                OP("pe", "matmul", [b_W, b_hTm[tt]], [bV], out=pV[:, 0:256], lhsT=hTm[:, kc, cs], rhs=Wv[:, kc, :], start=(kc == 0), stop=(kc == 7))
                            OP("act", "copy", [bV], [b_v], out=v_sb[:], in_=pV[:, 0:256])
                            OP("pe", "transpose", [b_qk, b_cst], [bV], out=pV[:, 256:384], in_=kd[:], identity=ident[:])
                            for h in range(4):
                                OP("act", "copy", [bV], [b_kdm], out=kdm[:, h, h * 32:(h + 1) * 32], in_=pV[:, 256 + h * 32:256 + (h + 1) * 32])
                            yield
                            if ci == 0:
                                OP("dve", "tensor_copy", [b_S0], [b_S], out=Sg[:], in_=S0[:, d, :])
                            elif slot_first:
                                OP("dve", "tensor_scalar", [b_S, b_cst], [b_S], out=Sg[:], in0=Sg[:], scalar1=carry[:, 0:1], scalar2=None, op0=ALU.mult)
                            pO, bO = psum()
                            for h in range(4):
                                OP("pe", "matmul", [b_att, b_v], [bO], out=pO[:, h * 64:(h + 1) * 64], lhsT=attm[:, h, :], rhs=v_sb[:, h * 64:(h + 1) * 64], start=True, stop=False)
                                OP("pe", "matmul", [b_rb, b_S], [bO], out=pO[:, h * 64:(h + 1) * 64], lhsT=rb[:, h, :], rhs=Sg[:], start=False, stop=True)
                            if not ogw[c]:
                                ogw[c] = True
                                OP("act", "copy", [bO], [b_og[c]], out=ogla[:, c, :], in_=pO[:, 0:256])
                            else:
                                OP("dve", "tensor_tensor", [bO, b_og[c]], [b_og[c]], out=ogla[:, c, :], in0=pO[:, 0:256], in1=ogla[:, c, :], op=ALU.add)
                            yield
                            pS, bS = psum()
                            for h in range(4):
                                OP("pe", "matmul", [b_kdm, b_v], [bS], out=pS[:, 0:64], lhsT=kdm[:, h, :], rhs=v_sb[:, h * 64:(h + 1) * 64], start=(h == 0), stop=(h == 3))
                            OP("dve", "scalar_tensor_tensor", [bS, b_S, b_sm1], [b_S], out=Sg[:], in0=Sg[:], scalar=sm1[:, 1:2], in1=pS[:, 0:64], op0=ALU.mult, op1=ALU.add)
                            if slot_last:
                                k = stc[0] % 2
                                stc[0] += 1
                                OP("act", "copy", [b_S], [b_sto[k]], out=sto[k][:], in_=Sg[:])
                                DMA("sp", o_stgla[l, d, slot], sto[k][:], [b_sto[k]], [], is_out=True)
                    gens = [gla_dir(0, gsets[0]), gla_dir(1, gsets[1])]
                    alive = list(gens)
                    while alive:
                        for g_ in list(alive):
                            try:
                                next(g_)
                            except StopIteration:
                                alive.remove(g_)
                    sqo = sb("sqo", [128, 256], stack=sg_)
                    ssq = sb("ssq", [128, 4], stack=sg_)
                    gts = sb("gts", [128, 256], stack=sg_)
                    b_fin = Buf("fin")
                    for c in range(NCH):
                        cs = slice(c * CT, (c + 1) * CT)
                        tt = c // 4
                        OP("dve", "tensor_tensor", [b_og[c]], [b_fin], out=sqo[:], in0=ogla[:, c, :], in1=ogla[:, c, :], op=ALU.mult)
                        OP("dve", "tensor_reduce", [b_fin], [b_fin], out=ssq[:], in_=sqo[:].rearrange("p (h e) -> p h e", h=4), axis=mybir.AxisListType.X, op=ALU.add)
                        OP("act", "activation", [b_fin], [b_fin], out=ssq[:], in_=ssq[:], func=AF.Sqrt, scale=1.0 / 64.0, bias=EPS)
                        OP("dve", "reciprocal", [b_fin], [b_fin], out=ssq[:], in_=ssq[:])
                        OP("dve", "tensor_tensor", [b_fin, b_og[c]], [b_fin], out=sqo[:].rearrange("p (h e) -> p h e", h=4), in0=ogla[:, c, :].rearrange("p (h e) -> p h e", h=4),
                           in1=ssq[:].unsqueeze(2).to_broadcast([128, 4, 64]), op=ALU.mult)
                        OP("dve", "tensor_tensor", [b_fin, b_gp], [b_fin], out=sqo[:], in0=sqo[:], in1=nwr[:], op=ALU.mult)
                        pG, bG = psum()
                        for kc in range(8):
                            OP("pe", "matmul", [b_W, b_hTm[tt]], [bG], out=pG[:, 0:256], lhsT=hTm[:, kc, cs], rhs=Wog[:, kc, :], start=(kc == 0), stop=(kc == 7))
                        OP("act", "activation", [bG], [b_fin], out=gts[:], in_=pG[:, 0:256], func=AF.Silu)
                        OP("dve", "tensor_tensor", [b_fin], [b_fin], out=sqo[:], in0=sqo[:], in1=gts[:], op=ALU.mult)
                        for m in range(2):
                            OP("pe", "transpose", [b_fin, b_cst], [bG], out=pG[:, 256 + m * 128:256 + (m + 1) * 128], in_=sqo[:, m * 128:(m + 1) * 128], identity=ident[:])
                            OP("act", "copy", [bG], [b_om[c]], out=omG[:, m, cs], in_=pG[:, 256 + m * 128:256 + (m + 1) * 128])
                    wout_part(l, 0, 2, omG, b_om)
                S.barrier()
                mark(f"gdn1_{l}")
                if "gdn" in mixers:
                  with ExitStack() as s1:
                    Wqkv = sb("Wqkv", [128, 8, 1536], BF16, stack=s1)
                    b_Wq = Buf("Wqkv")
                    for q3 in range(3):
                        DMA("pool", Wqkv[:, :, q3 * 512:(q3 + 1) * 512], wsrc[:, :, O_DQ + q3 * 512:O_DQ + (q3 + 1) * 512], [], [b_Wq])
                    cw = sb("cw", [128, 12, 5], stack=s1)
                    b_cw = Buf("cw")
                    DMA("sp", cw[:], gdn_convw[:, l * 60:(l + 1) * 60].rearrange("p (c j) -> p c j", j=5), [], [b_cw])
                    win = sb("win", [128, 12, 132], stack=s1)
                    b_win = Buf("win")
                    cmk = [sb(f"cmk{k}", [128, 4, 128], stack=s1) for k in range(2)]
                    b_cmk = [Buf(f"cmk{k}") for k in range(2)]
                    acc = sb("acc", [128, 12, 128], stack=s1)
                    tmpc = [sb(f"tmpc{k}", [128, 12, 128], stack=s1) for k in range(2)]
                    b_acc = Buf("acc")
                    b_tmpc = [Buf(f"tmpc{k}") for k in range(2)]
                    zs = [sb(f"zs{k}", [128, 12, 128], stack=s1) for k in range(2)]
                    b_zs = [Buf(f"zs{k}") for k in range(2)]
                    sq8 = sb("sq8", [128, 8, 128], stack=s1)
                    rs8 = sb("rs8", [128, 8, 128], stack=s1)
                    b_s8 = Buf("s8")
                    OP("pool", "memset", [], [b_win], ap=win[:], constant=0.0)
                    for c in range(NCH):
                        tt = c // 4
                        lo = max(c * CT - 2, 0)
                        hi = min(c * CT + CT + 2, T)
                        o0 = lo - (c * CT - 2)
                        n = hi - lo
                        k2 = c % 2
                        DMA("sp", cmk[k2][:], c_cmask[:, :, c * CT:(c + 1) * CT], [], [b_cmk[k2]])
                        if c == NCH - 1:
                            OP("pool", "memset", [], [b_win], ap=win[:], constant=0.0)
                        rd_h = [b_hTm[max(lo // TT, 0)], b_hTm[min((hi - 1) // TT, NTT - 1)]]
                        for g3 in range(4):
                            pw, bw = psum()
                            for j3 in range(3):
                                ct = g3 * 3 + j3
                                for kc in range(8):
                                    OP("pe", "matmul", [b_Wq] + rd_h, [bw], out=pw[:, j3 * 132:j3 * 132 + n], lhsT=Wqkv[:, kc, ct * 128:(ct + 1) * 128], rhs=hTm[:, kc, lo:hi],
                                       start=(kc == 0), stop=(kc == 7))
                            OP("act", "copy", [bw], [b_win], out=win[:, g3 * 3:(g3 + 1) * 3, o0:o0 + n], in_=pw[:, 0:396].rearrange("p (a b) -> p a b", a=3)[:, :, 0:n])
                        OP("dve", "tensor_tensor", [b_win, b_cw], [b_acc], out=acc[:], in0=win[:, :, 2:130], in1=cw[:, :, 2:3].to_broadcast([128, 12, 128]), op=ALU.mult)
                        for ji, j in enumerate((-2, -1, 1, 2)):
                            e_ = "pool" if ji % 2 == 0 else "dve"
                            k3 = ji % 2
                            OP(e_, "tensor_tensor", [b_win, b_cw], [b_tmpc[k3]], out=tmpc[k3][:], in0=win[:, :, 2 + j:130 + j],
                               in1=cw[:, :, 2 + j:3 + j].to_broadcast([128, 12, 128]), op=ALU.mult)
                            OP(e_, "tensor_tensor", [b_tmpc[k3], b_cmk[k2]], [b_tmpc[k3]], out=tmpc[k3][:], in0=tmpc[k3][:],
                               in1=cmk[k2][:, ji:ji + 1, :].to_broadcast([128, 12, 128]), op=ALU.mult)
                            OP("dve", "tensor_tensor", [b_tmpc[k3], b_acc], [b_acc], out=acc[:], in0=acc[:], in1=tmpc[k3][:], op=ALU.add)
                        z = zs[k2]
                        OP("act", "activation", [b_acc], [b_zs[k2]], out=z[:], in_=acc[:], func=AF.Silu)
                        OP("act", "activation", [b_zs[k2]], [b_s8], out=sq8[:], in_=z[:, 0:8, :], func=AF.Square)
                        for hf in range(2):
                            pn, bn = psum()
                            OP("pe", "matmul", [b_s8, b_ones], [bn], out=pn[:, 0:512], lhsT=ones[:], rhs=sq8[:, hf * 4:(hf + 1) * 4, :].rearrange("p a b -> p (a b)"), start=True, stop=True)
                            OP("act", "activation", [bn], [b_s8], out=rs8[:, hf * 4:(hf + 1) * 4, :].rearrange("p a b -> p (a b)"), in_=pn[:, 0:512], func=AF.Sqrt, bias=EPS)
                        OP("dve", "reciprocal", [b_s8], [b_s8], out=rs8[:], in_=rs8[:])
                        OP("dve", "scalar_tensor_tensor", [b_s8, b_zs[k2]], [b_zs[k2]], out=z[:, 0:4, :], in0=z[:, 0:4, :], scalar=128.0 ** -0.5, in1=rs8[:, 0:4, :], op0=ALU.mult, op1=ALU.mult)
                        OP("dve", "tensor_tensor", [b_s8, b_zs[k2]], [b_zs[k2]], out=z[:, 4:8, :], in0=z[:, 4:8, :], in1=rs8[:, 4:8, :], op=ALU.mult)
                        DMA("sp", zq[c], z[:].rearrange("p a b -> p (a b)"), [b_zs[k2]], [b_zq[c]])
                  S.barrier()
                  mark(f"gdn2_{l}")
                  gstep = float(os.environ.get("KGDN", "99"))
                  s_om = ExitStack()
                  omD = sb("omD", [128, 4, T], BF16, stack=s_om)
                  with ExitStack() as s2:
                    Wab = sb("Wab", [128, 8, 16], BF16, stack=s2)
                    Wdog = sb("Wdog", [128, 8, 512], BF16, stack=s2)
                    b_W2 = Buf("gdnW2")
                    DMA("pool", Wab[:], wsrc[:, :, O_DAF:O_DAF + 16], [], [b_W2])
                    DMA("pool", Wdog[:], wsrc[:, :, O_DOG:O_DOG + 512], [], [b_W2])
                    dtb = sb("dtb", [128, 8], stack=s2)
                    nA = sb("nA", [128, 8], stack=s2)
                    nwr = sb("nwrd", [128, 512], stack=s2)
                    b_gp = Buf("gdnp")
                    DMA("sp", dtb[:], gdn_dtb[:, l * 8:(l + 1) * 8], [], [b_gp])
                    DMA("sp", nA[:], gdn_alog[:, l * 8:(l + 1) * 8], [], [b_gp])
                    DMA("sp", nwr[:], gdn_nw[l], [], [b_gp])
                    OP("act", "activation", [b_gp], [b_gp], out=nA[:], in_=nA[:], func=AF.Exp)
                    OP("dve", "tensor_scalar", [b_gp], [b_gp], out=nA[:], in0=nA[:], scalar1=-1.0, scalar2=None, op0=ALU.mult)
                    TTb = sb("TTb", [128, 4, 128], stack=s2)
                    Sd = sb("Sd", [128, 4, 128], stack=s2)
                    b_S = Buf("Sd")
                    S0 = sb("S0d", [128, 512], stack=s2)
                    b_S0 = Buf("S0d")
                    zin = [sb(f"zin{k}", [128, 12, 128], stack=s2) for k in range(2)]
                    b_zin = [Buf(f"zin{k}") for k in range(2)]
                    gt = sb("gt", [128, 40], stack=s2)
                    b_gt = Buf("gt")
                    dg = sb("dg", [128, 4, 128], stack=s2)
                    b_dg = Buf("dg")
                    Eb = sb("Eb", [128, 4, 128], stack=s2)
                    ETb = sb("ETb", [128, 4, 128], stack=s2)
                    Egr = sb("Egr", [128, 4, 128], stack=s2)
                    b_E = Buf("E")
                    b_ET = Buf("ET")
                    b_Egr = Buf("Egr")
                    Ab = [sb(f"Ab{k}", [128, 4, 128], stack=s2) for k in range(2)]
                    Bb = [sb(f"Bb{k}", [128, 4, 128], stack=s2) for k in range(2)]
                    b_Ab = [Buf(f"Ab{k}") for k in range(2)]
                    b_Bb = [Buf(f"Bb{k}") for k in range(2)]
                    b_TT = Buf("TT")
                    Xb = sb("Xb", [128, 4, 128], stack=s2)
                    b_X = Buf("Xb")
                    rw = sb("rw", [128, 4, 128], stack=s2)
                    ru = sb("ru", [128, 4, 128], stack=s2)
                    kdc = sb("kdc", [128, 4, 128], stack=s2)
                    b_rk = Buf("rk")
                    wT = sb("wT", [128, 4, 128], stack=s2)
                    b_wT = Buf("wT")
                    u_sb = sb("u_sb", [128, 4, 128], stack=s2)
                    b_u = Buf("u")
                    qeg = sb("qeg", [128, 4, 128], stack=s2)
                    b_qeg = Buf("qeg")
                    qkm = sb("qkm", [128, 4, 128], stack=s2)
                    b_qkm = Buf("qkm")
                    ost = [sb(f"ost{k}", [128, 512], stack=s2) for k in range(1)] * 2
                    b_ost = [Buf(f"ost{k}") for k in range(1)] * 2
                    oin = sb("oin", [128, 512], stack=s2)
                    b_oin = Buf("oin")
                    fo = sb("fo", [128, 512], stack=s2)
                    fg = sb("fg", [128, 512], stack=s2)
                    fs = sb("fs", [128, 4], stack=s2)
                    b_f = Buf("fin")
                    sto = ost
                    b_sto = b_ost
                    stc = 0
                    oc = 0
                    zc = 0
                    bc4 = lambda ap: ap.unsqueeze(2).to_broadcast([128, 4, 128])
                    flat = lambda t_: t_[:].rearrange("p a b -> p (a b)")
                    for d in range(2 if gstep >= 2 else 0):
                        DMA("sp", S0[:], st_gdn[l, d], [], [b_S0])
                        order = list(range(NCH)) if d == 0 else list(range(NCH - 1, -1, -1))
                        mk_i = mk_f if d == 0 else mk_b
                        mk_s = mk_bs if d == 0 else mk_fs
                        for ci, c in enumerate(order):
                            cs = slice(c * CT, (c + 1) * CT)
                            tt = c // 4
                            slot_first = (c % 2 == 0) if d == 0 else (c % 2 == 1)
                            slot_last = not slot_first
                            slot = c // 2
                            zk = zc % 2
                            zc += 1
                            Z = zin[zk]
                            DMA("sp", flat(Z), zq[c], [b_zq[c]], [b_zin[zk]])
                            pg, bg = psum()
                            for kc in range(8):
                                OP("pe", "matmul", [b_W2, b_hTm[tt]], [bg], out=pg[:, 0:16], lhsT=hTm[:, kc, cs], rhs=Wab[:, kc, :], start=(kc == 0), stop=(kc == 7))
                            OP("dve", "tensor_tensor", [bg, b_gp], [b_gt], out=gt[:, 32:36], in0=pg[:, 4 * d:4 * d + 4], in1=dtb[:, 4 * d:4 * d + 4], op=ALU.add)
                            OP("act", "activation", [b_gt], [b_gt], out=gt[:, 32:36], in_=gt[:, 32:36], func=AF.Exp)
                            OP("act", "activation", [b_gt], [b_gt], out=gt[:, 32:36], in_=gt[:, 32:36], func=AF.Ln, bias=1.0)
                            OP("dve", "tensor_tensor", [b_gt, b_gp], [b_gt], out=gt[:, 0:4], in0=gt[:, 32:36], in1=nA[:, 4 * d:4 * d + 4], op=ALU.mult)
                            OP("act", "activation", [bg], [b_gt], out=gt[:, 4:8], in_=pg[:, 8 + 4 * d:12 + 4 * d], func=AF.Exp, scale=-1.0)
                            OP("dve", "tensor_scalar", [b_gt], [b_gt], out=gt[:, 4:8], in0=gt[:, 4:8], scalar1=1.0, scalar2=None, op0=ALU.add)
                            OP("dve", "reciprocal", [b_gt], [b_gt], out=gt[:, 4:8], in_=gt[:, 4:8])
                            pc, bpc = psum()
                            OP("pe", "matmul", [b_gt, b_cst], [bpc], out=pc[:, 0:4], lhsT=mk_i[:], rhs=gt[:, 0:4], start=True, stop=True)
                            OP("pe", "matmul", [b_gt, b_ones], [bpc], out=pc[:, 4:8], lhsT=ones[:], rhs=gt[:, 0:4], start=True, stop=True)
                            OP("act", "copy", [bpc], [b_gt], out=gt[:, 8:12], in_=pc[:, 0:4])
                            OP("act", "activation", [bpc], [b_gt], out=gt[:, 12:16], in_=pc[:, 0:4], func=AF.Exp)
                            OP("act", "activation", [bpc], [b_gt], out=gt[:, 24:28], in_=pc[:, 4:8], func=AF.Exp)
                            OP("dve", "tensor_tensor", [bpc, b_gt], [b_gt], out=gt[:, 20:24], in0=pc[:, 4:8], in1=gt[:, 8:12], op=ALU.subtract)
                            OP("act", "activation", [b_gt], [b_gt], out=gt[:, 20:24], in_=gt[:, 20:24], func=AF.Exp)
                            OP("dve", "scalar_tensor_tensor", [b_gt], [b_gt], out=gt[:, 28:32], in0=gt[:, 4:8], scalar=-1.0, in1=gt[:, 12:16], op0=ALU.mult, op1=ALU.mult)
                            if gstep < 3:
                                continue
                            OP("dve", "tensor_tensor", [b_gt, b_cst], [b_dg], out=dg[:], in0=ident[:].unsqueeze(1).to_broadcast([128, 4, 128]), in1=bc4(gt[:, 8:12]), op=ALU.mult)
                            pG, bG = psum()
                            OP("pe", "matmul", [b_dg, b_ones], [bG], out=pG[:, 0:512], lhsT=ones[:], rhs=flat(dg), start=True, stop=True)
                            OP("dve", "scalar_tensor_tensor", [bG, b_gt], [b_E], out=Eb[:], in0=pG[:, 0:512].rearrange("p (a b) -> p a b", a=4), scalar=-1.0, in1=bc4(gt[:, 8:12]),
                               op0=ALU.mult, op1=ALU.add)
                            OP("dve", "tensor_scalar", [b_E], [b_E], out=Eb[:], in0=Eb[:], scalar1=0.0, scalar2=None, op0=ALU.min)
                            OP("act", "activation", [b_E], [b_E], out=Eb[:], in_=Eb[:], func=AF.Exp)
                            OP("dve", "tensor_tensor", [bG, b_gt], [b_ET], out=ETb[:], in0=pG[:, 0:512].rearrange("p (a b) -> p a b", a=4), in1=bc4(gt[:, 8:12]), op=ALU.subtract)
                            OP("dve", "tensor_scalar", [b_ET], [b_ET], out=ETb[:], in0=ETb[:], scalar1=0.0, scalar2=None, op0=ALU.min)
                            OP("act", "activation", [b_ET], [b_ET], out=ETb[:], in_=ETb[:], func=AF.Exp)
                            OP("pool", "tensor_tensor", [b_ET, b_cst], [b_ET], out=ETb[:], in0=ETb[:], in1=mk_i[:].unsqueeze(1).to_broadcast([128, 4, 128]), op=ALU.mult)
                            OP("act", "activation", [bG], [b_Egr], out=flat(Egr), in_=pG[:, 0:512], func=AF.Exp)
                            if gstep < 3.2:
                                continue
                            pK, bK = psum()
                            pQ, bQ = psum()
                            for h in range(4):
                                OP("pe", "matmul", [b_zin[zk]], [bK], out=pK[:, h * 128:(h + 1) * 128], lhsT=Z[:, 4 + h, :], rhs=Z[:, 4 + h, :], start=True, stop=True)
                            for h in range(4):
                                OP("pe", "matmul", [b_zin[zk]], [bQ], out=pQ[:, h * 128:(h + 1) * 128], lhsT=Z[:, 4 + h, :], rhs=Z[:, h, :], start=True, stop=True)
                            if gstep < 3.5:
                                continue
                            A0 = Ab[0]
                            OP("pool", "tensor_tensor", [b_E, b_cst], [b_E], out=Eb[:], in0=Eb[:], in1=mk_s[:].unsqueeze(1).to_broadcast([128, 4, 128]), op=ALU.mult)
                            OP("dve", "scalar_tensor_tensor", [bK, b_E], [b_Ab[0]], out=flat(A0), in0=pK[:, 0:512], scalar=-1.0, in1=flat(Eb), op0=ALU.mult, op1=ALU.mult)
                            OP("dve", "tensor_tensor", [b_Ab[0], b_gt], [b_Ab[0]], out=A0[:], in0=A0[:], in1=bc4(gt[:, 4:8]), op=ALU.mult)
                            OP("dve", "tensor_tensor", [bQ, b_ET], [b_qkm], out=flat(qkm), in0=pQ[:, 0:512], in1=flat(ETb), op=ALU.mult)
                            OP("pool", "tensor_tensor", [b_zin[zk], b_Egr], [b_qeg], out=qeg[:], in0=Z[:, 0:4, :], in1=Egr[:], op=ALU.mult)
                            if gstep < 3.8:
                                continue
                            pB, bB = psum()
                            for h in range(4):
                                OP("pe", "matmul", [b_Ab[0], b_cst], [bB], out=pB[:, h * 128:(h + 1) * 128], lhsT=A0[:, h, :], rhs=ident[:], start=True, stop=True)
                            if os.environ.get("KX", "") != "1":
                                OP("act", "copy", [bB], [b_Bb[0]], out=flat(Bb[0]), in_=pB[:, 0:512])
                            if gstep < 5:
                                continue
                            Afull, Bfull = Ab[0], Bb[0]
                            Ak, Bk = Ab[1], Bb[1]
                            b_Ak, b_Bk = b_Ab[1], b_Bb[1]
                            mbc = lambda m_: m_[:].unsqueeze(1).to_broadcast([128, 4, 128])
                            OP("pool", "tensor_tensor", [b_Ab[0], b_cst], [b_Ak], out=Ak[:], in0=Afull[:], in1=mbc(bd16), op=ALU.mult)
                            OP("pool", "tensor_tensor", [b_Bb[0], b_cst], [b_Bk], out=Bk[:], in0=Bfull[:], in1=mbc(bd16), op=ALU.mult)
                            OP("dve", "tensor_tensor", [b_Ak, b_cst], [b_X], out=Xb[:], in0=Ak[:], in1=mbc(ident), op=ALU.add)
                            OP("dve", "tensor_tensor", [b_Bk, b_cst], [b_TT], out=TTb[:], in0=Bk[:], in1=mbc(ident), op=ALU.add)
                            for k in range(1, 4):
                                pA1, bA1 = psum()
                                pB1, bB1 = psum()
                                for h in range(4):
                                    OP("pe", "matmul", [b_Ak, b_Bk], [bA1], out=pA1[:, h * 128:(h + 1) * 128], lhsT=Bk[:, h, :], rhs=Ak[:, h, :], start=True, stop=True)
                                for h in range(4):
                                    OP("pe", "matmul", [b_Ak, b_Bk], [bB1], out=pB1[:, h * 128:(h + 1) * 128], lhsT=Ak[:, h, :], rhs=Bk[:, h, :], start=True, stop=True)
                                OP("act", "copy", [bA1], [b_Ak], out=flat(Ak), in_=pA1[:, 0:512])
                                OP("act", "copy", [bB1], [b_Bk], out=flat(Bk), in_=pB1[:, 0:512])
                                pX1, bX1 = psum()
                                pT1, bT1 = psum()
                                for h in range(4):
                                    OP("pe", "matmul", [b_TT, b_Ak], [bX1], out=pX1[:, h * 128:(h + 1) * 128], lhsT=TTb[:, h, :], rhs=Ak[:, h, :], start=True, stop=True)
                                for h in range(4):
                                    OP("pe", "matmul", [b_Ak, b_TT], [bT1], out=pT1[:, h * 128:(h + 1) * 128], lhsT=Ak[:, h, :], rhs=TTb[:, h, :], start=True, stop=True)
                                OP("dve", "tensor_tensor", [bX1, b_X], [b_X], out=flat(Xb), in0=pX1[:, 0:512], in1=flat(Xb), op=ALU.add)
                                OP("dve", "tensor_tensor", [bT1, b_TT], [b_TT], out=flat(TTb), in0=pT1[:, 0:512], in1=flat(TTb), op=ALU.add)
                            for li in range(3):
                                mA_ = (MLm if d == 0 else MUm)[li]
                                mB_ = (MUm if d == 0 else MLm)[li]
                                MA, MB, Qa, Qb = dg, Eb, ETb, Egr
                                OP("pool", "tensor_tensor", [b_Ab[0], b_cst], [b_dg], out=MA[:], in0=Afull[:], in1=mbc(mA_), op=ALU.mult)
                                OP("pool", "tensor_tensor", [b_Bb[0], b_cst], [b_E], out=MB[:], in0=Bfull[:], in1=mbc(mB_), op=ALU.mult)
                                pQa, bQa = psum()
                                pQb, bQb = psum()
                                for h in range(4):
                                    OP("pe", "matmul", [b_E, b_X], [bQa], out=pQa[:, h * 128:(h + 1) * 128], lhsT=MB[:, h, :], rhs=Xb[:, h, :], start=True, stop=True)
                                for h in range(4):
                                    OP("pe", "matmul", [b_dg, b_TT], [bQb], out=pQb[:, h * 128:(h + 1) * 128], lhsT=MA[:, h, :], rhs=TTb[:, h, :], start=True, stop=True)
                                OP("act", "copy", [bQa], [b_ET], out=flat(Qa), in_=pQa[:, 0:512])
                                OP("act", "copy", [bQb], [b_Egr], out=flat(Qb), in_=pQb[:, 0:512])
                                pX1, bX1 = psum()
                                pT1, bT1 = psum()
                                for h in range(4):
                                    OP("pe", "matmul", [b_TT, b_ET], [bX1], out=pX1[:, h * 128:(h + 1) * 128], lhsT=TTb[:, h, :], rhs=Qa[:, h, :], start=True, stop=True)
                                for h in range(4):
                                    OP("pe", "matmul", [b_X, b_Egr], [bT1], out=pT1[:, h * 128:(h + 1) * 128], lhsT=Xb[:, h, :], rhs=Qb[:, h, :], start=True, stop=True)
                                OP("dve", "tensor_tensor", [bX1, b_X], [b_X], out=flat(Xb), in0=pX1[:, 0:512], in1=flat(Xb), op=ALU.add)
                                OP("dve", "tensor_tensor", [bT1, b_TT], [b_TT], out=flat(TTb), in0=pT1[:, 0:512], in1=flat(TTb), op=ALU.add)
                            if gstep < 6:
                                continue
                            pKt, bKt = psum()
                            pVt, bVt = psum()
                            for h in range(4):
                                OP("pe", "transpose", [b_zin[zk], b_cst], [bKt], out=pKt[:, h * 128:(h + 1) * 128], in_=Z[:, 4 + h, :], identity=ident[:])
                            for h in range(4):
                                OP("pe", "transpose", [b_zin[zk], b_cst], [bVt], out=pVt[:, h * 128:(h + 1) * 128], in_=Z[:, 8 + h, :], identity=ident[:])
                            OP("dve", "tensor_tensor", [bKt, b_gt], [b_rk], out=rw[:], in0=pKt[:, 0:512].rearrange("p (a b) -> p a b", a=4), in1=bc4(gt[:, 28:32]), op=ALU.mult)
                            OP("dve", "tensor_tensor", [bKt, b_gt], [b_rk], out=kdc[:], in0=pKt[:, 0:512].rearrange("p (a b) -> p a b", a=4), in1=bc4(gt[:, 20:24]), op=ALU.mult)
                            OP("dve", "tensor_tensor", [bVt, b_gt], [b_rk], out=ru[:], in0=pVt[:, 0:512].rearrange("p (a b) -> p a b", a=4), in1=bc4(gt[:, 4:8]), op=ALU.mult)
                            pWt, bWt = psum()
                            for h in range(4):
                                OP("pe", "matmul", [b_rk, b_TT], [bWt], out=pWt[:, h * 128:(h + 1) * 128], lhsT=rw[:, h, :], rhs=TTb[:, h, :], start=True, stop=True)
                            OP("act", "copy", [bWt], [b_wT], out=flat(wT), in_=pWt[:, 0:512])
                            if gstep < 7:
                                continue
                            if ci == 0:
                                OP("dve", "tensor_copy", [b_S0], [b_S], out=flat(Sd), in_=S0[:])
                            elif slot_first:
                                OP("dve", "tensor_scalar", [b_S, b_cst], [b_S], out=flat(Sd), in0=flat(Sd), scalar1=carry[:, 0:1], scalar2=None, op0=ALU.mult)
                            pU, bU = psum()
                            for h in range(4):
                                OP("pe", "matmul", [b_TT, b_rk], [bU], out=pU[:, h * 128:(h + 1) * 128], lhsT=TTb[:, h, :], rhs=ru[:, h, :], start=True, stop=False)
                                OP("pe", "matmul", [b_wT, b_S], [bU], out=pU[:, h * 128:(h + 1) * 128], lhsT=wT[:, h, :], rhs=Sd[:, h, :], start=False, stop=True)
                            OP("act", "copy", [bU], [b_u], out=flat(u_sb), in_=pU[:, 0:512])
                            pO, bO = psum()
                            for h in range(4):
                                OP("pe", "matmul", [b_qeg, b_S], [bO], out=pO[:, h * 128:(h + 1) * 128], lhsT=qeg[:, h, :], rhs=Sd[:, h, :], start=True, stop=False)
                                OP("pe", "matmul", [b_qkm, b_u], [bO], out=pO[:, h * 128:(h + 1) * 128], lhsT=qkm[:, h, :], rhs=u_sb[:, h, :], start=False, stop=True)
                            pS, bS = psum()
                            for h in range(4):
                                OP("pe", "matmul", [b_rk, b_u], [bS], out=pS[:, h * 128:(h + 1) * 128], lhsT=kdc[:, h, :], rhs=u_sb[:, h, :], start=True, stop=True)
                            OP("dve", "tensor_tensor", [b_S, b_gt], [b_S], out=Sd[:], in0=Sd[:], in1=bc4(gt[:, 24:28]), op=ALU.mult)
                            OP("dve", "tensor_tensor", [b_S, bS], [b_S], out=flat(Sd), in0=flat(Sd), in1=pS[:, 0:512], op=ALU.add)
                            if slot_last:
                                k = stc % 2
                                stc += 1
                                OP("act", "copy", [b_S], [b_sto[k]], out=sto[k][:], in_=flat(Sd))
                                DMA("sp", o_stgdn[l, d, slot], sto[k][:], [b_sto[k]], [], is_out=True)
                            if d == 0:
                                k = oc % 2
                                oc += 1
                                OP("act", "copy", [bO], [b_ost[k]], out=ost[k][:], in_=pO[:, 0:512])
                                DMA("sp", ogs[c], ost[k][:], [b_ost[k]], [b_ogs[c]])
                            else:
                                DMA("sp", oin[:], ogs[c], [b_ogs[c]], [b_oin])
                                OP("dve", "tensor_tensor", [bO, b_oin], [b_f], out=fo[:], in0=pO[:, 0:512], in1=oin[:], op=ALU.add)
                                OP("act", "activation", [b_f], [b_f], out=fg[:], in_=fo[:], func=AF.Square)
                                OP("dve", "tensor_reduce", [b_f], [b_f], out=fs[:], in_=fg[:].rearrange("p (h e) -> p h e", h=4), axis=mybir.AxisListType.X, op=ALU.add)
                                OP("act", "activation", [b_f], [b_f], out=fs[:], in_=fs[:], func=AF.Sqrt, scale=1.0 / 128.0, bias=EPS)
                                OP("dve", "reciprocal", [b_f], [b_f], out=fs[:], in_=fs[:])
                                OP("dve", "tensor_tensor", [b_f], [b_f], out=fo[:].rearrange("p (h e) -> p h e", h=4), in0=fo[:].rearrange("p (h e) -> p h e", h=4),
                                   in1=fs[:].unsqueeze(2).to_broadcast([128, 4, 128]), op=ALU.mult)
                                OP("dve", "tensor_tensor", [b_f, b_gp], [b_f], out=fo[:], in0=fo[:], in1=nwr[:], op=ALU.mult)
                                pD, bD = psum()
                                for kc in range(8):
                                    OP("pe", "matmul", [b_W2, b_hTm[tt]], [bD], out=pD[:, 0:512], lhsT=hTm[:, kc, cs], rhs=Wdog[:, kc, :], start=(kc == 0), stop=(kc == 7))
                                OP("act", "activation", [bD], [b_f], out=fg[:], in_=pD[:, 0:512], func=AF.Silu)
                                OP("dve", "tensor_tensor", [b_f], [b_f], out=fo[:], in0=fo[:], in1=fg[:], op=ALU.mult)
                                pR, bR = psum()
                                for m in range(4):
                                    OP("pe", "transpose", [b_f, b_cst], [bR], out=pR[:, m * 128:(m + 1) * 128], in_=fo[:, m * 128:(m + 1) * 128], identity=ident[:])
                                OP("act", "copy", [bR], [b_om[c]], out=omD[:, :, cs], in_=pR[:, 0:512].rearrange("p (a b) -> p a b", a=4))
                  S.barrier()
                  if gstep >= 99:
                      wout_part(l, 2, 4, omD, b_om)
                  s_om.close()
                  S.barrier()
                sh.close()
                S.barrier()
                mark(f"s5_{l}")
                if "s5" in mixers:
                    s5_part(l, U, b_U, b_om)

        import math
        PI = math.pi

        C1_ = 6.28125
        C2_ = 2 * PI - 6.28125
        I32 = mybir.dt.int32

        def sincos(ang, ki, kf, sn, cs, b_ang, b_ki, b_kf, b_sn, b_cs, e1="dve"):
            OP(e1, "tensor_scalar", [b_ang], [b_ki], out=ki, in0=ang, scalar1=1.0 / (2 * PI), scalar2=None, op0=ALU.mult)
            OP("act", "copy", [b_ki], [b_kf], out=kf, in_=ki)
            OP("dve", "scalar_tensor_tensor", [b_kf, b_ang], [b_ang], out=ang, in0=kf, scalar=-C1_, in1=ang, op0=ALU.mult, op1=ALU.add)
            OP("dve", "scalar_tensor_tensor", [b_kf, b_ang], [b_ang], out=ang, in0=kf, scalar=-C2_, in1=ang, op0=ALU.mult, op1=ALU.add)
            OP("dve", "tensor_scalar", [b_ang], [b_ang], out=ang, in0=ang, scalar1=-3.141592, scalar2=3.141592, op0=ALU.max, op1=ALU.min)
            OP("act", "activation", [b_ang], [b_sn], out=sn, in_=ang, func=AF.Sin)
            OP("act", "activation", [b_ang], [b_cs], out=cs, in_=ang, func=AF.Sin, scale=0.5)
            OP("act", "activation", [b_cs], [b_cs], out=cs, in_=cs, func=AF.Square)
            OP("act", "activation", [b_cs], [b_cs], out=cs, in_=cs, func=AF.Identity, scale=-2.0, bias=1.0)

        def s5_part(l, U, b_U, b_om):
            with ExitStack() as s5:
                G = sb("Gs5", [128, 2, T], stack=s5)
                b_G = [[Buf(f"G{gh}_{tb}") for tb in range(4)] for gh in range(2)]
                pp = sb("pp", [128, 12, 16], stack=s5)
                b_pp = Buf("pp")
                for k_, src in enumerate((s5_lr_p, s5_li_p, s5_ls_p)):
                    DMA("sp", pp[:, k_, :], src[:, l * 16:(l + 1) * 16], [], [b_pp])
                DMA("sp", pp[:, 7, :], s5_h0re[:, l * 16:(l + 1) * 16], [], [b_pp])
                DMA("sp", pp[:, 8, :], s5_h0im[:, l * 16:(l + 1) * 16], [], [b_pp])
                OP("act", "activation", [b_pp], [b_pp], out=pp[:, 9, :], in_=pp[:, 2, :], func=AF.Exp)
                OP("dve", "tensor_tensor", [b_pp], [b_pp], out=pp[:, 3, :], in0=pp[:, 1, :], in1=pp[:, 9, :], op=ALU.mult)
                OP("dve", "tensor_tensor", [b_pp], [b_pp], out=pp[:, 4, :], in0=pp[:, 0, :], in1=pp[:, 9, :], op=ALU.mult)
                OP("act", "activation", [b_pp], [b_pp], out=pp[:, 4, :], in_=pp[:, 4, :], func=AF.Exp)
                ginit = sb("ginit", [128, 2, 16], stack=s5)
                b_gi = Buf("ginit")
                ang0 = sb("ang0", [128, 16], stack=s5)
                OP("dve", "tensor_copy", [b_pp], [b_gi], out=ang0[:, 0:8], in_=pp[:, 3, 0:8])
                OP("dve", "tensor_scalar", [b_pp], [b_gi], out=ang0[:, 8:16], in0=pp[:, 3, 8:16], scalar1=float(T), scalar2=None, op0=ALU.mult)
                ki16 = sb("ki16", [128, 16], I32, stack=s5)
                b_ki = Buf("ki16")
                sincos(ang0[:], ki16[:], pp[:, 9, :], pp[:, 6, :], pp[:, 5, :], b_gi, b_ki, b_pp, b_pp, b_pp)
                OP("dve", "tensor_tensor", [b_pp], [b_pp], out=pp[:, 9, :], in0=pp[:, 7, :], in1=pp[:, 5, :], op=ALU.mult)
                OP("dve", "tensor_tensor", [b_pp], [b_pp], out=pp[:, 10, :], in0=pp[:, 8, :], in1=pp[:, 6, :], op=ALU.mult)
                OP("dve", "tensor_tensor", [b_pp], [b_gi], out=ginit[:, 0, :], in0=pp[:, 9, :], in1=pp[:, 10, :], op=ALU.subtract)
                OP("dve", "tensor_tensor", [b_pp], [b_pp], out=pp[:, 9, :], in0=pp[:, 7, :], in1=pp[:, 6, :], op=ALU.mult)
                OP("dve", "tensor_tensor", [b_pp], [b_pp], out=pp[:, 10, :], in0=pp[:, 8, :], in1=pp[:, 5, :], op=ALU.mult)
                OP("dve", "tensor_tensor", [b_pp], [b_gi], out=ginit[:, 1, :], in0=pp[:, 9, :], in1=pp[:, 10, :], op=ALU.add)
                toff = sb("toff", [128, 16, 4], stack=s5)
                for tb in range(4):
                    OP("dve", "tensor_scalar", [b_pp], [b_gi], out=toff[:, :, tb], in0=pp[:, 3, :], scalar1=float(512 * tb), scalar2=None, op0=ALU.mult)
                tix = sb("tix", [128, 512], stack=s5)
                cmk5 = sb("cmk5", [128, 2, 512], stack=s5)
                b_c5 = Buf("c5")
                DMA("sp", tix[:], c_tix, [], [b_c5])
                DMA("sp", cmk5[:], c_cm5, [], [b_c5])
                dsk = sb("dsk", [128, 2], stack=s5)
                bgl = sb("bgl", [128, 2], stack=s5)
                DMA("sp", dsk[:], s5_D[:, l * 2:(l + 1) * 2], [], [b_c5])
                DMA("sp", bgl[:], s5_bglu[:, l * 2:(l + 1) * 2], [], [b_c5])
                stg = sb("stg", [128, 2, 16, 8], stack=s5)
                b_stg = Buf("stg")
                OP("pool", "memset", [], [b_stg], ap=stg[:], constant=0.0)
                names_ = ["bur", "bui", "sn", "cs", "a1", "a2", "m1", "m2", "m3", "m4", "hr", "hi", "rmk"]
                sets = []
                for si in range(2):
                    Wk_ = {n_: sb(f"w5{si}" + n_, [128, 512], stack=s5) for n_ in names_}
                    Bq_ = {n_: Buf(f"w5{si}" + n_) for n_ in names_}
                    for n_ in ("gr", "gi"):
                        Wk_[n_] = sb(f"w5{si}" + n_, [128, 2], stack=s5)
                        Bq_[n_] = Buf(f"w5{si}" + n_)
                    Wk_["ki"] = sb(f"ki5_{si}", [128, 512], I32, stack=s5)
                    Bq_["ki"] = Buf(f"ki5_{si}")
                    sets.append((Wk_, Bq_))
                Wk, Bk_ = sets[0]
                ps_lo[0] = 4
                for gh in range(2):
                    with ExitStack() as sw5:
                        R_ = {"W2r": sb("r5W2r", [128, 1024], BF16, stack=sw5), "W2i": sb("r5W2i", [128, 1024], BF16, stack=sw5),
                              "Cr": sb("r5Cr", [128, 1024], stack=sw5), "Ci": sb("r5Ci", [128, 1024], stack=sw5)}
                        b_R = {n_: Buf("r5" + n_) for n_ in R_}
                        b_t = Buf("t5")
                        tmap = dict(zip(("lr", "li", "a", "b", "cb", "sb", "nr", "ni", "den", "Br", "Bi", "x1", "x2"), names_))
                        t_ = {k_: Wk[v_] for k_, v_ in tmap.items()}
                        col0 = (l * 2 + gh) * 1024
                        DMA("sp", R_["Cr"][:], s5_Cre[:, col0:col0 + 1024], [], [b_R["Cr"]])
                        DMA("sp", R_["Ci"][:], s5_Cim[:, col0:col0 + 1024], [], [b_R["Ci"]])
                        OP("dve", "tensor_scalar", [b_R["Ci"]], [b_R["Ci"]], out=R_["Ci"][:], in0=R_["Ci"][:], scalar1=-1.0, scalar2=None, op0=ALU.mult)
                        for hf in range(2):
                            col = slice(col0 + hf * 512, col0 + (hf + 1) * 512)
                            oc_ = slice(hf * 512, (hf + 1) * 512)
                            DMA("sp", t_["lr"][:], s5_lr_row[:, col], [b_t], [b_t])
                            DMA("sp", t_["li"][:], s5_li_row[:, col], [b_t], [b_t])
                            DMA("sp", t_["a"][:], s5_ls_row[:, col], [b_t], [b_t])
                            DMA("sp", t_["Br"][:], s5_Bre[:, col], [b_t], [b_t])
                            DMA("sp", t_["Bi"][:], s5_Bim[:, col], [b_t], [b_t])
                            E_ = lambda eng, meth, **kw: OP(eng, meth, [b_t], [b_t], **kw)
                            E_("act", "activation", out=t_["a"][:], in_=t_["a"][:], func=AF.Exp)
                            E_("dve", "tensor_tensor", out=t_["b"][:], in0=t_["li"][:], in1=t_["a"][:], op=ALU.mult)
                            E_("dve", "tensor_tensor", out=t_["a"][:], in0=t_["lr"][:], in1=t_["a"][:], op=ALU.mult)
                            E_("act", "activation", out=t_["a"][:], in_=t_["a"][:], func=AF.Exp)
                            sincos(t_["b"][:], Wk["ki"][:], t_["x1"][:], t_["sb"][:], t_["cb"][:], b_t, b_t, b_t, b_t, b_t)
                            E_("dve", "tensor_tensor", out=t_["nr"][:], in0=t_["a"][:], in1=t_["cb"][:], op=ALU.mult)
                            E_("dve", "tensor_scalar", out=t_["nr"][:], in0=t_["nr"][:], scalar1=-1.0, scalar2=None, op0=ALU.add)
                            E_("dve", "tensor_tensor", out=t_["ni"][:], in0=t_["a"][:], in1=t_["sb"][:], op=ALU.mult)
                            E_("dve", "tensor_tensor", out=t_["den"][:], in0=t_["lr"][:], in1=t_["lr"][:], op=ALU.mult)
                            E_("dve", "tensor_tensor", out=t_["x1"][:], in0=t_["li"][:], in1=t_["li"][:], op=ALU.mult)
                            E_("dve", "tensor_tensor", out=t_["den"][:], in0=t_["den"][:], in1=t_["x1"][:], op=ALU.add)
                            E_("dve", "reciprocal", out=t_["den"][:], in_=t_["den"][:])
                            E_("dve", "tensor_tensor", out=t_["x1"][:], in0=t_["nr"][:], in1=t_["lr"][:], op=ALU.mult)
                            E_("dve", "tensor_tensor", out=t_["x2"][:], in0=t_["ni"][:], in1=t_["li"][:], op=ALU.mult)
                            E_("dve", "tensor_tensor", out=t_["x1"][:], in0=t_["x1"][:], in1=t_["x2"][:], op=ALU.add)
                            E_("dve", "tensor_tensor", out=t_["a"][:], in0=t_["x1"][:], in1=t_["den"][:], op=ALU.mult)
                            E_("dve", "tensor_tensor", out=t_["x1"][:], in0=t_["ni"][:], in1=t_["lr"][:], op=ALU.mult)
                            E_("dve", "tensor_tensor", out=t_["x2"][:], in0=t_["nr"][:], in1=t_["li"][:], op=ALU.mult)
                            E_("dve", "tensor_tensor", out=t_["x1"][:], in0=t_["x1"][:], in1=t_["x2"][:], op=ALU.subtract)
                            E_("dve", "tensor_tensor", out=t_["b"][:], in0=t_["x1"][:], in1=t_["den"][:], op=ALU.mult)
                            E_("dve", "tensor_tensor", out=t_["x1"][:], in0=t_["a"][:], in1=t_["Br"][:], op=ALU.mult)
                            E_("dve", "tensor_tensor", out=t_["x2"][:], in0=t_["b"][:], in1=t_["Bi"][:], op=ALU.mult)
                            OP("dve", "tensor_tensor", [b_t], [b_R["W2r"], b_t], out=R_["W2r"][:, oc_], in0=t_["x1"][:], in1=t_["x2"][:], op=ALU.subtract)
                            E_("dve", "tensor_tensor", out=t_["x1"][:], in0=t_["a"][:], in1=t_["Bi"][:], op=ALU.mult)
                            E_("dve", "tensor_tensor", out=t_["x2"][:], in0=t_["b"][:], in1=t_["Br"][:], op=ALU.mult)
                            OP("dve", "tensor_tensor", [b_t], [b_R["W2i"], b_t], out=R_["W2i"][:, oc_], in0=t_["x1"][:], in1=t_["x2"][:], op=ALU.add)
                        S.barrier()
                        Ybank = [(psb[tb], psB[tb]) for tb in range(4)]
                        nacc = [0] * 4
                        def unit(d, jj, Wk, Bk_):
                            j = gh * 4 + jj
                            dj = d * 8 + j
                            wc = slice((d * 4 + jj) * 128, (d * 4 + jj + 1) * 128)
                            OP("dve", "tensor_scalar", [b_c5, b_pp], [Bk_["rmk"]], out=Wk["rmk"][:], in0=cmk5[:, d, :], scalar1=pp[:, 4, dj:dj + 1], scalar2=None, op0=ALU.mult)
                            prev = None
                            for tb in (range(4) if d == 0 else range(3, -1, -1)):
                                ts = slice(tb * 512, (tb + 1) * 512)
                                pr_, bpr_ = psum()
                                pi_, bpi_ = psum()
                                OP("pe", "matmul", [b_R["W2r"], b_U], [bpr_], out=pr_[:, 0:512], lhsT=R_["W2r"][:, wc], rhs=U[:, gh, ts], start=True, stop=True)
                                OP("pe", "matmul", [b_R["W2i"], b_U], [bpi_], out=pi_[:, 0:512], lhsT=R_["W2i"][:, wc], rhs=U[:, gh, ts], start=True, stop=True)
                                OP("act", "copy", [bpr_], [Bk_["bur"]], out=Wk["bur"][:], in_=pr_[:, 0:512])
                                OP("act", "copy", [bpi_], [Bk_["bui"]], out=Wk["bui"][:], in_=pi_[:, 0:512])
                                yield
                                OP("dve", "tensor_scalar", [b_c5, b_pp, b_gi], [Bk_["a1"]], out=Wk["a1"][:], in0=tix[:], scalar1=pp[:, 3, dj:dj + 1], scalar2=toff[:, dj, tb:tb + 1], op0=ALU.mult, op1=ALU.add)
                                sincos(Wk["a1"][:], Wk["ki"][:], Wk["a2"][:], Wk["sn"][:], Wk["cs"][:], Bk_["a1"], Bk_["ki"], Bk_["a2"], Bk_["sn"], Bk_["cs"])
                                yield
                                OP("dve", "tensor_tensor", [Bk_["bur"], Bk_["cs"]], [Bk_["m1"]], out=Wk["m1"][:], in0=Wk["bur"][:], in1=Wk["cs"][:], op=ALU.mult)
                                OP("pool", "tensor_tensor", [Bk_["bui"], Bk_["sn"]], [Bk_["m2"]], out=Wk["m2"][:], in0=Wk["bui"][:], in1=Wk["sn"][:], op=ALU.mult)
                                OP("dve", "tensor_tensor", [Bk_["bui"], Bk_["cs"]], [Bk_["m3"]], out=Wk["m3"][:], in0=Wk["bui"][:], in1=Wk["cs"][:], op=ALU.mult)
                                OP("dve", "tensor_tensor", [Bk_["bur"], Bk_["sn"]], [Bk_["m4"]], out=Wk["m4"][:], in0=Wk["bur"][:], in1=Wk["sn"][:], op=ALU.mult)
                                OP("dve", "tensor_tensor", [Bk_["m1"], Bk_["m2"]], [Bk_["m1"]], out=Wk["m1"][:], in0=Wk["m1"][:], in1=Wk["m2"][:], op=(ALU.add if d == 0 else ALU.subtract))
                                OP("dve", "tensor_tensor", [Bk_["m3"], Bk_["m4"]], [Bk_["m3"]], out=Wk["m3"][:], in0=Wk["m3"][:], in1=Wk["m4"][:], op=(ALU.subtract if d == 0 else ALU.add))
                                yield
                                if prev is None:
                                    ir, ii = ginit[:, 0, dj:dj + 1], ginit[:, 1, dj:dj + 1]
                                    rd0 = [b_gi]
                                else:
                                    ir, ii = prev
                                    rd0 = [Bk_["gr"], Bk_["gi"]]
                                if d == 0:
                                    OP("dve", "tensor_tensor_scan", [Bk_["m1"], Bk_["rmk"]] + rd0, [Bk_["hr"]], out=Wk["hr"][:], data0=Wk["rmk"][:], data1=Wk["m1"][:], initial=ir, op0=ALU.mult, op1=ALU.add)
                                    OP("dve", "tensor_tensor_scan", [Bk_["m3"], Bk_["rmk"]] + rd0, [Bk_["hi"]], out=Wk["hi"][:], data0=Wk["rmk"][:], data1=Wk["m3"][:], initial=ii, op0=ALU.mult, op1=ALU.add)
                                else:
                                    OP("dve", "tensor_tensor_scan", [Bk_["m1"], Bk_["rmk"]] + rd0, [Bk_["hr"]], out=Wk["hr"][:, ::-1], data0=Wk["rmk"][:, ::-1], data1=Wk["m1"][:, ::-1], initial=ir, op0=ALU.mult, op1=ALU.add)
                                    OP("dve", "tensor_tensor_scan", [Bk_["m3"], Bk_["rmk"]] + rd0, [Bk_["hi"]], out=Wk["hi"][:, ::-1], data0=Wk["rmk"][:, ::-1], data1=Wk["m3"][:, ::-1], initial=ii, op0=ALU.mult, op1=ALU.add)
                                lastc = 511 if d == 0 else 0
                                OP("act", "copy", [Bk_["hr"]], [Bk_["gr"]], out=Wk["gr"][:, 0:1], in_=Wk["hr"][:, lastc:lastc + 1])
                                OP("act", "copy", [Bk_["hi"]], [Bk_["gi"]], out=Wk["gi"][:, 0:1], in_=Wk["hi"][:, lastc:lastc + 1])
                                prev = (Wk["gr"][:, 0:1], Wk["gi"][:, 0:1])
                                yield
                                OP("dve", "tensor_tensor", [Bk_["hr"], Bk_["cs"]], [Bk_["m1"]], out=Wk["m1"][:], in0=Wk["hr"][:], in1=Wk["cs"][:], op=ALU.mult)
                                OP("pool", "tensor_tensor", [Bk_["hi"], Bk_["sn"]], [Bk_["m2"]], out=Wk["m2"][:], in0=Wk["hi"][:], in1=Wk["sn"][:], op=ALU.mult)
                                OP("dve", "tensor_tensor", [Bk_["hr"], Bk_["sn"]], [Bk_["m3"]], out=Wk["m3"][:], in0=Wk["hr"][:], in1=Wk["sn"][:], op=ALU.mult)
                                OP("dve", "tensor_tensor", [Bk_["hi"], Bk_["cs"]], [Bk_["m4"]], out=Wk["m4"][:], in0=Wk["hi"][:], in1=Wk["cs"][:], op=ALU.mult)
                                OP("dve", "tensor_tensor", [Bk_["m1"], Bk_["m2"]], [Bk_["bur"]], out=Wk["bur"][:], in0=Wk["m1"][:], in1=Wk["m2"][:], op=(ALU.subtract if d == 0 else ALU.add))
                                OP("dve", "tensor_tensor", [Bk_["m3"], Bk_["m4"]], [Bk_["bui"]], out=Wk["bui"][:], in0=(Wk["m3"][:] if d == 0 else Wk["m4"][:]), in1=(Wk["m4"][:] if d == 0 else Wk["m3"][:]), op=(ALU.add if d == 0 else ALU.subtract))
                                yield
                                c0_ = 255 if d == 0 else 0
                                OP("act", "copy", [Bk_["bur"]], [b_stg], out=stg[:, 0, dj, 2 * tb:2 * tb + 2], in_=Wk["bur"][:, c0_::256])
                                OP("act", "copy", [Bk_["bui"]], [b_stg], out=stg[:, 1, dj, 2 * tb:2 * tb + 2], in_=Wk["bui"][:, c0_::256])
                                py_, bpy_ = Ybank[tb]
                                first = nacc[tb] == 0
                                nacc[tb] += 2
                                lastm = nacc[tb] == 16
                                OP("pe", "matmul", [b_R["Cr"], Bk_["bur"]], [bpy_], out=py_[:, 0:512], lhsT=R_["Cr"][:, wc], rhs=Wk["bur"][:], start=first, stop=False)
                                OP("pe", "matmul", [b_R["Ci"], Bk_["bui"]], [bpy_], out=py_[:, 0:512], lhsT=R_["Ci"][:, wc], rhs=Wk["bui"][:], start=False, stop=lastm)
                        for jj in range(4):
                            gens = [unit(0, jj, *sets[0]), unit(1, jj, *sets[1])]
                            alive = list(gens)
                            while alive:
                                for g_ in list(alive):
                                    try:
                                        next(g_)
                                    except StopIteration:
                                        alive.remove(g_)
                        for tb in range(4):
                            ts = slice(tb * 512, (tb + 1) * 512)
                            py_, bpy_ = Ybank[tb]
                            y_, y2_, t3_, th_ = Wk["a1"], Wk["a2"], Wk["m1"], Wk["m2"]
                            OP("dve", "scalar_tensor_tensor", [bpy_, b_U, b_c5], [Bk_["a1"]], out=y_[:], in0=U[:, gh, ts], scalar=dsk[:, gh:gh + 1], in1=py_[:, 0:512], op0=ALU.mult, op1=ALU.add)
                            OP("pool", "tensor_tensor", [Bk_["a1"]], [Bk_["a2"]], out=y2_[:], in0=y_[:], in1=y_[:], op=ALU.mult)
                            OP("dve", "tensor_scalar", [Bk_["a2"]], [Bk_["a2"]], out=y2_[:], in0=y2_[:], scalar1=0.044715, scalar2=1.0, op0=ALU.mult, op1=ALU.add)
                            OP("pool", "tensor_tensor", [Bk_["a2"], Bk_["a1"]], [Bk_["m1"]], out=t3_[:], in0=y2_[:], in1=y_[:], op=ALU.mult)
                            OP("act", "activation", [Bk_["m1"]], [Bk_["m2"]], out=th_[:], in_=t3_[:], func=AF.Tanh, scale=0.7978845608028654)
                            OP("dve", "scalar_tensor_tensor", [Bk_["m2"], Bk_["a1"]], [Bk_["m2"]], out=th_[:], in0=th_[:], scalar=1.0, in1=y_[:], op0=ALU.add, op1=ALU.mult)
                            OP("act", "activation", [Bk_["m2"]], [b_G[gh][tb]], out=G[:, gh, ts], in_=th_[:], func=AF.Identity, scale=0.5)
                        S.barrier()
                ps_lo[0] = 0
                DMA("sp", o_s5[l], stg[:].rearrange("p a b c -> p (a b c)"), [b_stg], [], is_out=True)
                with ExitStack() as sgl:
                    Wg5 = sb("Wg5", [128, 2, 256], stack=sgl)
                    b_Wg5 = Buf("Wg5")
                    DMA("sp", Wg5[:], s5_wglu[l].rearrange("(cc p) n -> p cc n", p=128), [], [b_Wg5])
                    omS = sb("omS", [128, 2, T], BF16, stack=sgl)
                    sig = sb("sig", [128, 512], stack=sgl)
                    b_sig = Buf("sig")
                    for co in range(2):
                        for tb in range(4):
                            ts = slice(tb * 512, (tb + 1) * 512)
                            pz, bpz = psum()
                            for cc in range(2):
                                OP("pe", "matmul", [b_Wg5, b_G[cc][tb]], [bpz], out=pz[:, 0:512], lhsT=Wg5[:, cc, co * 128:(co + 1) * 128], rhs=G[:, cc, ts], start=(cc == 0), stop=(cc == 1))
                            OP("act", "activation", [bpz, b_c5], [b_sig], out=sig[:], in_=pz[:, 0:512], func=AF.Sigmoid, bias=bgl[:, co:co + 1])
                            OP("dve", "tensor_tensor", [b_sig, b_G[co][tb]], b_om[tb * 4:(tb + 1) * 4], out=omS[:, co, ts], in0=sig[:], in1=G[:, co, ts], op=ALU.mult)
                    wout_part(l, 6, 2, omS, b_om)
            S.barrier()

        for l in range(DEPTH):
            if dbg_mode == "noffn":
                continue
            ffn_phase(l, 0, 0)
            if enable_mix:
                mixer_phase(l)
            ffn_phase(l, 1, 2)

        mark("final")
        with ExitStack() as so:
            yo = [sb(f"yo{k}", [128, TT], stack=so) for k in range(4)]
            b_yo = [Buf(f"yo{k}") for k in range(4)]
            yc = 0
            for tt in range(NTT):
                ts = slice(tt * TT, (tt + 1) * TT)
                norm_stats(tt)
                for dc in range(8):
                    k = yc % 4
                    yc += 1
                    OP("dve", "scalar_tensor_tensor", [xB[dc][tt], b_rstd, b_fnw], [b_yo[k]], out=yo[k][:], in0=x[:, dc, ts], scalar=fnw[:, dc:dc + 1],
                       in1=rstd[:], op0=ALU.mult, op1=ALU.mult)
                    DMA("sp", yT_out[:, dc, ts], yo[k][:], [b_yo[k]], [], is_out=True)
        S.emit(st)
    return nc


_PROG = {}


def _get_prog(**kw):
    key = tuple(sorted(kw.items()))
    if key not in _PROG:
        _PROG[key] = build_program(**kw)
    return _PROG[key]


def _prep_inputs(inp):
    f = lambda a: np.ascontiguousarray(np.asarray(a, dtype=np.float32))
    xp = f(inp["x_prompt"])
    xs = f(inp["x_sample"])
    c = f(inp["c"])
    c_ctx = f(inp["c_ctx"])
    shared = {
        "w_ada": f(inp["w_ada"]),
        "b_ada": f(np.asarray(inp["b_ada"]).reshape(DEPTH, 72, 128).transpose(0, 2, 1)),
        "norm_w": f(np.asarray(inp["norm_w"]).reshape(DEPTH * 3 * 8, 128).T),
        "fnorm_w": f(np.asarray(inp["final_norm_w"]).reshape(8, 128).T),
        "ffn_w_gate": f(inp["ffn_w_gate"]),
        "ffn_w_up": f(inp["ffn_w_up"]),
        "ffn_w_down": f(inp["ffn_w_down"]),
    }
    idx = np.arange(128)
    shared["w_in"] = f(inp["w_in"])
    shared["w_out"] = f(inp["w_out"])
    shared["gla_up"] = f(np.asarray(inp["gla_gk_up"]).transpose(0, 2, 1, 3).reshape(DEPTH, 16, 256))
    shared["gla_bias"] = f(np.asarray(inp["gla_gk_bias"]).reshape(DEPTH * 2, 128).T)
    shared["gla_nw"] = f(np.broadcast_to(np.tile(np.asarray(inp["gla_norm_w"]), (1, 4))[:, None, :], (DEPTH, 128, 256)))
    cwv = np.asarray(inp["gdn_conv_w"]).reshape(DEPTH, 5, 12, 128)
    shared["gdn_convw"] = f(cwv.transpose(3, 0, 2, 1).reshape(128, DEPTH * 60))
    shared["gdn_dtb"] = f(np.broadcast_to(np.asarray(inp["gdn_dt_bias"]).reshape(1, DEPTH * 8), (128, DEPTH * 8)))
    shared["gdn_alog"] = f(np.broadcast_to(np.asarray(inp["gdn_a_log"]).reshape(1, DEPTH * 8), (128, DEPTH * 8)))
    shared["gdn_nw"] = f(np.broadcast_to(np.tile(np.asarray(inp["gdn_norm_w"]), (1, 4))[:, None, :], (DEPTH, 128, 512)))
    shared["c_mkfs"] = f(idx[:, None] < idx[None, :])
    blk = [(idx[:, None] // 16) == (idx[None, :] // 16)]
    for b_ in (16, 32, 64):
        blk.append(((idx[:, None] // b_) % 2 == 1) & ((idx[None, :] // b_) == (idx[:, None] // b_) - 1))
    for b_ in (16, 32, 64):
        blk.append(((idx[None, :] // b_) % 2 == 1) & ((idx[:, None] // b_) == (idx[None, :] // b_) - 1))
    shared["c_blk"] = f(np.stack(blk, 1))
    shared["c_mkbs"] = f(idx[:, None] > idx[None, :])
    lam_re = np.asarray(inp["s5_lam_re"]); lam_im = np.asarray(inp["s5_lam_im"]); lstep = np.asarray(inp["s5_log_step"])
    def part16(a):
        return f(a.reshape(DEPTH, 2, 8, 2, 64).transpose(3, 4, 0, 1, 2).reshape(128, DEPTH * 16))
    def row2048(a):
        r_ = a.reshape(DEPTH, 2, 2, 4, 2, 64).transpose(0, 2, 1, 3, 4, 5).reshape(1, DEPTH * 2048)
        return f(np.broadcast_to(r_, (128, DEPTH * 2048)))
    ls_full = np.broadcast_to(lstep[..., None], lam_re.shape)
    shared["s5_lr_p"] = part16(lam_re); shared["s5_li_p"] = part16(lam_im); shared["s5_ls_p"] = part16(ls_full)
    shared["s5_lr_row"] = row2048(lam_re); shared["s5_li_row"] = row2048(lam_im); shared["s5_ls_row"] = row2048(ls_full)
    def bpad(b):
        o_ = np.zeros((8, 16, DEPTH, 2, 2, 4, 2, 64), np.float32)
        for g in range(16):
            gh_, j_, gb_ = g // 8, (g // 2) % 4, g % 2
            o_[g % 8, :, :, gh_, :, j_, gb_, :] = b[:, :, g].transpose(3, 0, 1, 2)
        return f(o_.reshape(128, DEPTH * 2048))
    shared["s5_Bre"] = bpad(np.asarray(inp["s5_b_re"])); shared["s5_Bim"] = bpad(np.asarray(inp["s5_b_im"]))
    def cpad(c_):
        o_ = np.zeros((2, 64, DEPTH, 2, 2, 4, 8, 16), np.float32)
        for g in range(16):
            gh_, j_, gb_ = g // 8, (g // 2) % 4, g % 2
            o_[gb_, :, :, gh_, :, j_, g % 8, :] = c_[:, :, g].transpose(3, 0, 1, 2)
        return f(o_.reshape(128, DEPTH * 2048))
    shared["s5_Cre"] = cpad(np.asarray(inp["s5_c_re"])); shared["s5_Cim"] = cpad(np.asarray(inp["s5_c_im"]))
    shared["s5_D"] = f(np.asarray(inp["s5_d"]).reshape(DEPTH * 2, 128).T)
    shared["s5_bglu"] = f(np.asarray(inp["s5_b_glu"]).reshape(DEPTH * 2, 128).T)
    shared["s5_wglu"] = f(inp["s5_w_glu"])
    shared["c_tix"] = f(np.broadcast_to(np.arange(512, dtype=np.float32)[None], (128, 512)))
    shared["c_negpi"] = np.full((128, 1), -np.pi, np.float32)
    shared["c_ident"] = f(np.eye(128))
    shared["c_mkf"] = f(idx[:, None] <= idx[None, :])
    shared["c_mkb"] = f(idx[:, None] >= idx[None, :])
    shared["c_hmask"] = f((idx[:, None] // 32) == np.arange(4)[None, :])
    import os
    if os.environ.get("KDBG", "") == "noffn":
        for k in ("ffn_w_gate", "ffn_w_up", "ffn_w_down"):
            shared.pop(k)
    maps = []
    for r in range(8):
        q = r % 4
        if q < 2:
            xt = xs[q]
            cond = c[q]
        else:
            xt = xp[(q - 2) * 8:(q - 1) * 8].reshape(T, D)
            cond = c_ctx
        m = dict(shared)
        m["xT"] = f(xt.T.reshape(8, 128, T).transpose(1, 0, 2))
        m["cond"] = f(cond.reshape(8, 128).T)
        cy = 1.0 if (q < 2 and os.environ.get("KNOCARRY", "") != "1") else 0.0
        t5 = np.arange(512)
        cm5 = np.stack([np.where(t5 % 256 == 0, cy, 1.0), np.where(t5 % 256 == 255, cy, 1.0)], 0).astype(np.float32)
        m["c_cm5"] = f(np.broadcast_to(cm5[None], (128, 2, 512)))
        if q < 2 and os.environ.get("KNOCARRY", "") != "1":
            m["s5_h0re"] = f(np.asarray(inp["state_s5_re"])[q].reshape(DEPTH, 2, 8, 2, 64).transpose(3, 4, 0, 1, 2).reshape(128, DEPTH * 16))
            m["s5_h0im"] = f(np.asarray(inp["state_s5_im"])[q].reshape(DEPTH, 2, 8, 2, 64).transpose(3, 4, 0, 1, 2).reshape(128, DEPTH * 16))
        else:
            m["s5_h0re"] = np.zeros((128, DEPTH * 16), np.float32)
            m["s5_h0im"] = np.zeros((128, DEPTH * 16), np.float32)
        R_ = 64 if q < 2 else 256
        tpos = np.arange(T) % R_
        cmv = np.stack([((tpos + j) >= 0) & ((tpos + j) < R_) for j in (-2, -1, 1, 2)], 0).astype(np.float32)
        m["c_cmask"] = f(np.broadcast_to(cmv[None], (128, 4, T)))
        if q < 2:
            m["st_gdn"] = f(np.asarray(inp["state_gdn"])[q].transpose(0, 1, 3, 2, 4).reshape(DEPTH, 2, 128, 512))
        else:
            m["st_gdn"] = np.zeros((DEPTH, 2, 128, 512), np.float32)
        if q < 2 and os.environ.get("KNOCARRY", "") == "1":
            m["st_gla"] = np.zeros((DEPTH, 2, 128, 64), np.float32)
            m["c_carry"] = np.zeros((128, 1), np.float32)
            m["st_gdn"] = np.zeros((DEPTH, 2, 128, 512), np.float32)
        elif q < 2:
            m["st_gla"] = f(np.asarray(inp["state_gla"])[q].reshape(DEPTH, 2, 128, 64))
            m["c_carry"] = np.ones((128, 1), np.float32)
        else:
            m["st_gla"] = np.zeros((DEPTH, 2, 128, 64), np.float32)
            m["c_carry"] = np.zeros((128, 1), np.float32)
        maps.append(m)
    return maps


def _run(inp, **kw):
    nc = _get_prog(**kw)
    maps = _prep_inputs(inp)
    res = run_bass_kernel_spmd(nc, maps, core_ids=list(range(8)))
    return res.results


_LAST = {}


def kernel(**inp):
    res = _run(inp)
    _LAST["res"] = res
    ys = []
    for r in range(4):
        yT = np.asarray(res[r]["yT"])
        ys.append(yT.transpose(1, 0, 2).reshape(D, T).T)
    y_sample = np.stack([ys[0], ys[1]], 0).astype(np.float32)
    y_prompt = np.concatenate([ys[2].reshape(8, SL, D), ys[3].reshape(8, SL, D)], 0).astype(np.float32)
    B = 16
    sg = np.concatenate([np.asarray(res[2]["o_stgla"]), np.asarray(res[3]["o_stgla"])], axis=2)
    new_gla = sg.transpose(2, 0, 1, 3, 4).reshape(B, DEPTH, 2, 4, 32, 64).astype(np.float32)
    sd = np.concatenate([np.asarray(res[2]["o_stgdn"]), np.asarray(res[3]["o_stgdn"])], axis=2)
    new_gdn = sd.reshape(DEPTH, 2, B, 128, 4, 128).transpose(2, 0, 1, 4, 3, 5).astype(np.float32)
    if "o_s5" in res[2]:
        s5o = np.stack([np.asarray(res[2]["o_s5"]), np.asarray(res[3]["o_s5"])], 0)
        s5o = s5o.reshape(2, DEPTH, 2, 64, 2, 2, 8, 8)
        s5o = s5o.transpose(4, 0, 7, 1, 5, 6, 2, 3).reshape(2, 16, DEPTH, 2, 16, 64)
        new_re, new_im = np.ascontiguousarray(s5o[0]).astype(np.float32), np.ascontiguousarray(s5o[1]).astype(np.float32)
    else:
        new_re = np.zeros((B, DEPTH, 2, 16, 64), np.float32); new_im = np.zeros((B, DEPTH, 2, 16, 64), np.float32)
    return (y_prompt, y_sample,
            new_gla, np.ascontiguousarray(new_gdn), new_re, new_im)
```

```python
import numpy as np
from contextlib import ExitStack
import concourse.bass as bass
import concourse.mybir as mybir
from concourse.bass_utils import run_bass_kernel_spmd

F32 = mybir.dt.float32
BF16 = mybir.dt.bfloat16
AF = mybir.ActivationFunctionType
ALU = mybir.AluOpType

D = 1024
T = 2048
NS = 8
SL = 256
DEPTH = 2
FFN = 2816
NFC = 22
INW = 3120
TT = 512
NTT = T // TT
EPS = 1e-6
CT = 128
NCH = T // CT

O_GQ, O_GK, O_GV, O_GLF, O_GLB, O_GOG = 0, 128, 256, 512, 528, 544
O_DQ, O_DK, O_DV = 800, 1312, 1824
O_DAF, O_DAB, O_DBF, O_DBB, O_DOG, O_SU = 2336, 2340, 2344, 2348, 2352, 2864


class Buf:
    __slots__ = ("name", "w", "r")

    def __init__(self, name):
        self.name = name
        self.w = None
        self.r = []


class Eng:
    def __init__(self, name, is_pe=False):
        self.name = name
        self.ops = []
        self.count = 0
        self.seen = {}
        self.is_pe = is_pe


class Sched:
    NDMA = 6

    def __init__(self, nc):
        self.nc = nc
        self.engs = {n: Eng(n, n == "pe") for n in ("pe", "act", "dve", "pool", "sp")}
        self.dma_next = {"sp": 0, "pool": 0}
        self.dma_cnt = {}
        self.sem_keys = [("c", n) for n in ("pe", "act", "dve", "pool")]
        for q in ("sp", "pool"):
            for i in range(self.NDMA):
                self.sem_keys.append(("d", q, i))
                self.dma_cnt[("d", q, i)] = 0
        self.out_events = []
        self.nops = 0
        import os
        self.strict = os.environ.get("KSTRICT", "") == "1"

    def _deps(self, eng, reads, writes):
        deps = {}

        def add(ev, same_ok):
            if ev is None:
                return
            key, val, src = ev
            if src == eng.name and key[0] == "c":
                if eng.is_pe or same_ok:
                    return
            if eng.seen.get(key, 0) >= val:
                return
            if deps.get(key, 0) < val:
                deps[key] = val

        for b in reads:
            add(b.w, False)
        for b in writes:
            add(b.w, False)
            for ev in b.r:
                add(ev, False)
        for k, v in deps.items():
            eng.seen[k] = v
        return list(deps.items())

    def op(self, engname, fn, reads=(), writes=()):
        eng = self.engs[engname]
        waits = self._deps(eng, reads, writes)
        if self.strict:
            for m, e2 in self.engs.items():
                if m in ("sp",) or e2.count == 0:
                    continue
                key = ("c", m)
                if eng.seen.get(key, 0) < e2.count and not (m == engname and eng.is_pe):
                    waits.append((key, e2.count))
                    eng.seen[key] = e2.count
        eng.count += 1
        self.nops += 1
        ev = (("c", engname), eng.count, engname)
        eng.ops.append((waits, fn, "c", ev))
        for b in reads:
            b.r.append(ev)
        for b in writes:
            b.w = ev
            b.r = []
        return ev

    def dma(self, q, fn, reads=(), writes=(), is_out=False):
        eng = self.engs[q]
        i = self.dma_next[q]
        self.dma_next[q] = (i + 1) % self.NDMA
        key = ("d", q, i)
        waits = self._deps(eng, reads, writes)
        prev = self.dma_cnt[key] * 16
        if prev > 0 and eng.seen.get(key, 0) < prev:
            waits.append((key, prev))
            eng.seen[key] = prev
        self.dma_cnt[key] += 1
        self.nops += 1
        ev = (key, self.dma_cnt[key] * 16, "dma:" + q)
        eng.ops.append((waits, fn, "d", ev))
        for b in reads:
            b.r.append(ev)
        for b in writes:
            b.w = ev
            b.r = []
        if is_out:
            self.out_events.append(ev)
        return ev

    def barrier(self):
        last = {n: e.count for n, e in self.engs.items() if n != "sp"}
        dl = {k: c * 16 for k, c in self.dma_cnt.items() if c}
        for n, e in self.engs.items():
            waits = []
            for m, c in last.items():
                key = ("c", m)
                if m == n or c == 0:
                    continue
                if e.seen.get(key, 0) < c:
                    waits.append((key, c))
                    e.seen[key] = c
            for key, v in dl.items():
                if e.seen.get(key, 0) < v:
                    waits.append((key, v))
                    e.seen[key] = v
            if waits:
                e.ops.append((waits, None, "w", None))

    def emit(self, stack):
        nc = self.nc
        sems = {}
        for k in self.sem_keys:
            sems[k] = stack.enter_context(nc.semaphore("s_" + "_".join(str(x) for x in k)))
        sp = self.engs["sp"]
        fin = {}
        for n in ("pe", "act", "dve", "pool"):
            c = self.engs[n].count
            if c:
                fin[("c", n)] = c
        for key, cnt in self.dma_cnt.items():
            if cnt:
                fin[key] = cnt * 16
        sp.ops.append((list(fin.items()), None, "w", None))
        block = stack.enter_context(nc.Block())

        def replay(engname):
            def body(e):
                for waits, fn, kind, ev in self.engs[engname].ops:
                    for key, val in waits:
                        e.wait_ge(sems[key], val)
                    if fn is None:
                        continue
                    ins = getattr(e, fn[0])(**fn[1]) if isinstance(fn, tuple) else fn(e)
                    ins.then_inc(sems[ev[0]], 16 if kind == "d" else 1)
            return body

        block.tensor(replay("pe"))
        block.scalar(replay("act"))
        block.vector(replay("dve"))
        block.gpsimd(replay("pool"))
        block.sync(replay("sp"))


def build_program(enable_mix=True, dbg=False):
    nc = bass.Bass("TRN2", target_bir_lowering=False)

    def din(name, shape):
        return nc.dram_tensor(name, list(shape), F32, kind="ExternalInput").ap()

    def dout(name, shape):
        return nc.dram_tensor(name, list(shape), F32, kind="ExternalOutput").ap()

    xT_in = din("xT", [128, 8, T])
    cond_in = din("cond", [128, 8])
    w_ada = din("w_ada", [DEPTH, D, 9 * D])
    b_ada = din("b_ada", [DEPTH, 128, 72])
    norm_w = din("norm_w", [128, DEPTH * 3 * 8])
    fnorm_w = din("fnorm_w", [128, 8])
    import os
    dbg_mode = os.environ.get("KDBG", "")
    if dbg_mode != "noffn":
        w_gate = din("ffn_w_gate", [DEPTH, 2, D, FFN])
        w_up = din("ffn_w_up", [DEPTH, 2, D, FFN])
        w_down = din("ffn_w_down", [DEPTH, 2, FFN, D])
    yT_out = dout("yT", [128, 8, T])
    w_in = din("w_in", [DEPTH, D, INW])
    w_out = din("w_out", [DEPTH, D, D])
    gla_up = din("gla_up", [DEPTH, 16, 256])
    gla_bias = din("gla_bias", [128, DEPTH * 2])
    gla_nw = din("gla_nw", [DEPTH, 128, 256])
    st_gla = din("st_gla", [DEPTH, 2, 128, 64])
    o_stgla = dout("o_stgla", [DEPTH, 2, NS, 128, 64])
    gdn_convw = din("gdn_convw", [128, DEPTH * 60])
    gdn_dtb = din("gdn_dtb", [128, DEPTH * 8])
    gdn_alog = din("gdn_alog", [128, DEPTH * 8])
    gdn_nw = din("gdn_nw", [DEPTH, 128, 512])
    st_gdn = din("st_gdn", [DEPTH, 2, 128, 512])
    o_stgdn = dout("o_stgdn", [DEPTH, 2, NS, 128, 512])
    c_cmask = din("c_cmask", [128, 4, T])
    c_mkfs = din("c_mkfs", [128, 128])
    c_blk = din("c_blk", [128, 7, 128])
    s5_lr_p = din("s5_lr_p", [128, DEPTH * 16])
    s5_li_p = din("s5_li_p", [128, DEPTH * 16])
    s5_ls_p = din("s5_ls_p", [128, DEPTH * 16])
    s5_h0re = din("s5_h0re", [128, DEPTH * 16])
    s5_h0im = din("s5_h0im", [128, DEPTH * 16])
    s5_lr_row = din("s5_lr_row", [128, DEPTH * 2048])
    s5_li_row = din("s5_li_row", [128, DEPTH * 2048])
    s5_ls_row = din("s5_ls_row", [128, DEPTH * 2048])
    s5_Bre = din("s5_Bre", [128, DEPTH * 2048])
    s5_Bim = din("s5_Bim", [128, DEPTH * 2048])
    s5_Cre = din("s5_Cre", [128, DEPTH * 2048])
    s5_Cim = din("s5_Cim", [128, DEPTH * 2048])
    s5_D = din("s5_D", [128, DEPTH * 2])
    s5_bglu = din("s5_bglu", [128, DEPTH * 2])
    s5_wglu = din("s5_wglu", [DEPTH, 256, 256])
    c_tix = din("c_tix", [128, 512])
    c_cm5 = din("c_cm5", [128, 2, 512])
    c_negpi = din("c_negpi", [128, 1])
    o_s5 = dout("o_s5", [DEPTH, 128, 256])
    c_mkbs = din("c_mkbs", [128, 128])
    zq_t = nc.dram_tensor("zq_scr", [NCH, 128, 1536], F32).ap()
    ogs_t = nc.dram_tensor("ogs_scr", [NCH, 128, 512], F32).ap()
    mixers = os.environ.get("KMIX", "gla,gdn,s5").split(",")
    c_ident = din("c_ident", [128, 128])
    c_mkf = din("c_mkf", [128, 128])
    c_mkb = din("c_mkb", [128, 128])
    c_hmask = din("c_hmask", [128, 4])
    c_carry = din("c_carry", [128, 1])

    S = Sched(nc)
    with ExitStack() as st:
        uid = [0]

        def sb(name, shape, dt=F32, stack=st):
            uid[0] += 1
            t_ = stack.enter_context(nc.sbuf_tensor(f"{name}_{uid[0]}", list(shape), dt))
            if os.environ.get("KALLOC", ""):
                a_ = nc.lookup_mloc(t_).addr
                nb_ = int(np.prod(shape[1:])) * (2 if dt == BF16 else 4)
                print("ALLOC", f"{name}_{uid[0]}", a_, a_ + nb_)
            return t_

        x = sb("x", [128, 8, T])
        xB = [[Buf(f"x{dc}_{tt}") for tt in range(NTT)] for dc in range(8)]
        ones = sb("ones", [128, 128])
        b_ones = Buf("ones")
        prm = sb("prm", [128, DEPTH * 72])
        b_prm = Buf("prm")
        nwt = sb("nwt", [128, DEPTH * 24])
        b_nwt = Buf("nwt")
        fnw = sb("fnw", [128, 8])
        b_fnw = Buf("fnw")
        coefA = sb("coefA", [128, DEPTH * 24])
        b_coefA = Buf("coefA")
        coefG = sb("coefG", [128, DEPTH * 24])
        b_coefG = Buf("coefG")
        zero8 = sb("zero8", [128, 8])
        b_zero8 = Buf("zero8")
        psb = [st.enter_context(nc.psum_tensor(f"ps{i}", [128, 512], F32)) for i in range(8)]
        psB = [Buf(f"ps{i}") for i in range(8)]
        ps_i = [0]
        ps_lo = [0]

        def psum():
            i = ps_i[0]
            if i < ps_lo[0]:
                i = ps_lo[0]
            nxt = i + 1
            if nxt >= 8:
                nxt = ps_lo[0]
            ps_i[0] = nxt
            return psb[i], psB[i]

        def OP(eng, meth, r, w, **kw):
            return S.op(eng, (meth, kw), r, w)

        def DMA(q, out, in_, r, w, is_out=False):
            return S.dma(q, ("dma_start", dict(out=out, in_=in_)), r, w, is_out)

        ident = sb("ident", [128, 128])
        mk_f = sb("mk_f", [128, 128])
        mk_b = sb("mk_b", [128, 128])
        hmask = sb("hmask", [128, 4])
        carry = sb("carry", [128, 1])
        b_cst = Buf("cst")
        mk_fs = sb("mk_fs", [128, 128])
        blkm = sb("blkm", [128, 7, 128])
        negpi = sb("negpi", [128, 1])
        bd16 = blkm[:, 0, :]
        MLm = [blkm[:, 1 + i, :] for i in range(3)]
        MUm = [blkm[:, 4 + i, :] for i in range(3)]
        mk_bs = sb("mk_bs", [128, 128])
        zq = [zq_t[c] for c in range(NCH)]
        ogs = [ogs_t[c] for c in range(NCH)]
        b_zq = [Buf(f"zq{c}") for c in range(NCH)]
        b_ogs = [Buf(f"ogs{c}") for c in range(NCH)]
        for dc in range(8):
            DMA("sp", x[:, dc, :], xT_in[:, dc, :], [], [xB[dc][tt] for tt in range(NTT)])
        OP("dve", "memset", [], [b_ones], ap=ones[:], constant=1.0)
        OP("dve", "memset", [], [b_zero8], ap=zero8[:], constant=0.0)
        DMA("sp", nwt[:], norm_w, [], [b_nwt])
        DMA("sp", ident[:], c_ident, [], [b_cst])
        DMA("sp", mk_f[:], c_mkf, [], [b_cst])
        DMA("sp", mk_b[:], c_mkb, [], [b_cst])
        DMA("sp", hmask[:], c_hmask, [], [b_cst])
        DMA("sp", carry[:], c_carry, [], [b_cst])
        DMA("sp", mk_fs[:], c_mkfs, [], [b_cst])
        DMA("sp", blkm[:], c_blk, [], [b_cst])
        DMA("sp", negpi[:], c_negpi, [], [b_cst])
        DMA("sp", mk_bs[:], c_mkbs, [], [b_cst])
        DMA("sp", fnw[:], fnorm_w, [], [b_fnw])

        with ExitStack() as st_ada:
            cnd = sb("cnd", [128, 8], stack=st_ada)
            b_cnd = Buf("cnd")
            sc = sb("sc", [128, 8], stack=st_ada)
            b_sc = Buf("sc")
            bada = sb("bada", [128, DEPTH * 72], stack=st_ada)
            b_bada = Buf("bada")
            wa = [sb(f"wa{i}", [128, 8, 256], stack=st_ada) for i in range(4)]
            b_wa = [Buf(f"wa{i}") for i in range(4)]
            DMA("sp", cnd[:], cond_in, [], [b_cnd])
            for l in range(DEPTH):
                DMA("sp", bada[:, l * 72:(l + 1) * 72], b_ada[l], [], [b_bada])
            OP("act", "activation", [b_cnd], [b_sc], out=sc[:], in_=cnd[:], func=AF.Silu)
            for l in range(DEPTH):
                pm, bpm = psum()
                wsrc = w_ada[l].rearrange("(kc p) n -> p kc n", p=128)
                for blk in range(36):
                    wt, bw = wa[blk % 4], b_wa[blk % 4]
                    DMA("sp", wt[:], wsrc[:, :, blk * 256:(blk + 1) * 256], [], [bw])
                    for m in range(2):
                        j = blk * 2 + m
                        for kc in range(8):
                            OP("pe", "matmul", [bw, b_sc], [bpm], out=pm[:, j:j + 1], lhsT=wt[:, kc, m * 128:(m + 1) * 128], rhs=sc[:, kc:kc + 1],
                               start=(kc == 0), stop=(kc == 7))
                OP("dve", "tensor_tensor", [bpm, b_bada], [b_prm], out=prm[:, l * 72:(l + 1) * 72], in0=pm[:, 0:72], in1=bada[:, l * 72:(l + 1) * 72], op=ALU.add)
                for n in range(3):
                    o = l * 72 + (3 * n + 1) * 8
                    c0 = l * 24 + n * 8
                    OP("dve", "scalar_tensor_tensor", [b_prm, b_nwt], [b_coefA], out=coefA[:, c0:c0 + 8], in0=prm[:, o:o + 8], scalar=1.0, in1=nwt[:, c0:c0 + 8],
                       op0=ALU.add, op1=ALU.mult)
                    og = l * 72 + (3 * n + 2) * 8
                    gs = 1.0 if n == 1 else 0.5
                    OP("dve", "tensor_scalar", [b_prm], [b_coefG], out=coefG[:, c0:c0 + 8], in0=prm[:, og:og + 8], scalar1=gs, scalar2=None, op0=ALU.mult)
            S.barrier()

        sq = [sb(f"sq{i}", [128, TT]) for i in range(2)]
        b_sq = [Buf(f"sq{i}") for i in range(2)]
        rstd = sb("rstd", [128, TT])
        b_rstd = Buf("rstd")
        tn = [sb(f"tn{i}", [128, TT]) for i in range(2)]
        b_tn = [Buf(f"tn{i}") for i in range(2)]
        cnt = {"sq": 0, "tn": 0}

        def norm_stats(tt):
            ts = slice(tt * TT, (tt + 1) * TT)
            pss, bps = psum()
            for dc in range(8):
                i = cnt["sq"] % 2
                cnt["sq"] += 1
                OP("act", "activation", [xB[dc][tt]], [b_sq[i]], out=sq[i][:], in_=x[:, dc, ts], func=AF.Square)
                OP("pe", "matmul", [b_sq[i], b_ones], [bps], out=pss[:, 0:TT], lhsT=ones[:], rhs=sq[i][:], start=(dc == 0), stop=(dc == 7))
            OP("act", "activation", [bps], [b_rstd], out=rstd[:], in_=pss[:, 0:TT], func=AF.Sqrt, scale=1.0 / D, bias=EPS)
            OP("dve", "reciprocal", [b_rstd], [b_rstd], out=rstd[:], in_=rstd[:])

        def norm_tile(tt, A_ap, sh_ap, reads_coef, dst_fn, dst_bufs):
            ts = slice(tt * TT, (tt + 1) * TT)
            norm_stats(tt)
            for dc in range(8):
                i = cnt["tn"] % 2
                cnt["tn"] += 1
                OP("dve", "tensor_tensor", [xB[dc][tt], b_rstd], [b_tn[i]], out=tn[i][:], in0=x[:, dc, ts], in1=rstd[:], op=ALU.mult)
                OP("act", "activation", [b_tn[i]] + reads_coef, dst_bufs(dc), out=dst_fn(dc), in_=tn[i][:], func=AF.Identity,
                   scale=A_ap[:, dc:dc + 1], bias=sh_ap[:, dc:dc + 1])

        def mark(name):
            if os.environ.get("KPHASE", ""):
                print("PHASE", name, S.engs["pe"].count)

        def ffn_phase(l, i, n):
            mark(f"ffn{l}_{i}")
            with ExitStack() as sf:
                hT = sb("hT", [128, 8, TT], BF16, stack=sf)
                b_hT = Buf("hT")
                act = sb("act", [128, NFC, TT], BF16, stack=sf)
                b_act = [Buf(f"act{fc}") for fc in range(NFC)]
                wg = [sb(f"wg{k}", [128, 8, 256], BF16, stack=sf) for k in range(2)]
                wu = [sb(f"wu{k}", [128, 8, 256], BF16, stack=sf) for k in range(2)]
                wd = [sb(f"wd{k}", [128, NFC, 128], BF16, stack=sf) for k in range(2)]
                b_wg = [Buf(f"wg{k}") for k in range(2)]
                b_wu = [Buf(f"wu{k}") for k in range(2)]
                b_wd = [Buf(f"wd{k}") for k in range(2)]
                sg = [sb(f"sg{k}", [128, TT], stack=sf) for k in range(2)]
                b_sg = [Buf(f"sg{k}") for k in range(2)]
                gsrc = w_gate[l, i].rearrange("(kc p) n -> p kc n", p=128)
                usrc = w_up[l, i].rearrange("(kc p) n -> p kc n", p=128)
                dsrc = w_down[l, i].rearrange("(fc p) n -> p fc n", p=128)
                c0 = l * 24 + n * 8
                osh = l * 72 + (3 * n) * 8
                A_ap = coefA[:, c0:c0 + 8]
                sh_ap = prm[:, osh:osh + 8]
                G_ap = coefG[:, c0:c0 + 8]
                wcnt = 0
                dcnt = 0
                scnt = 0
                for tt in range(NTT):
                    ts = slice(tt * TT, (tt + 1) * TT)
                    norm_tile(tt, A_ap, sh_ap, [b_coefA, b_prm], lambda dc: hT[:, dc, :], lambda dc: [b_hT])
                    for fb in range(11):
                        k = wcnt % 2
                        wcnt += 1
                        DMA("pool", wg[k][:], gsrc[:, :, fb * 256:(fb + 1) * 256], [], [b_wg[k]])
                        DMA("pool", wu[k][:], usrc[:, :, fb * 256:(fb + 1) * 256], [], [b_wu[k]])
                        for sub in range(2):
                            fc = fb * 2 + sub
                            pg, bpg = psum()
                            pu, bpu = psum()
                            for kc in range(8):
                                OP("pe", "matmul", [b_wg[k], b_hT], [bpg], out=pg[:, 0:TT], lhsT=wg[k][:, kc, sub * 128:(sub + 1) * 128], rhs=hT[:, kc, :],
                                   start=(kc == 0), stop=(kc == 7))
                            for kc in range(8):
                                OP("pe", "matmul", [b_wu[k], b_hT], [bpu], out=pu[:, 0:TT], lhsT=wu[k][:, kc, sub * 128:(sub + 1) * 128], rhs=hT[:, kc, :],
                                   start=(kc == 0), stop=(kc == 7))
                            s_ = scnt % 2
                            scnt += 1
                            OP("act", "activation", [bpg], [b_sg[s_]], out=sg[s_][:], in_=pg[:, 0:TT], func=AF.Silu)
                            OP("dve", "tensor_tensor", [b_sg[s_], bpu], [b_act[fc]], out=act[:, fc, :], in0=sg[s_][:], in1=pu[:, 0:TT], op=ALU.mult)
                    for dc in range(8):
                        k = dcnt % 2
                        dcnt += 1
                        DMA("pool", wd[k][:], dsrc[:, :, dc * 128:(dc + 1) * 128], [], [b_wd[k]])
                        po, bpo = psum()
                        for fc in range(NFC):
                            OP("pe", "matmul", [b_wd[k], b_act[fc]], [bpo], out=po[:, 0:TT], lhsT=wd[k][:, fc, :], rhs=act[:, fc, :],
                               start=(fc == 0), stop=(fc == NFC - 1))
                        OP("dve", "scalar_tensor_tensor", [bpo, b_coefG, xB[dc][tt]], [xB[dc][tt]], out=x[:, dc, ts], in0=po[:, 0:TT], scalar=G_ap[:, dc:dc + 1],
                           in1=x[:, dc, ts], op0=ALU.mult, op1=ALU.add)
                S.barrier()

        def wout_part(l, mc0, nmc, om, b_om):
            with ExitStack() as sw:
                Wo = sb("Wo", [128, nmc, D], BF16, stack=sw)
                b_Wo = Buf("Wo")
                osrc = w_out[l].rearrange("(mc p) n -> p mc n", p=128)
                DMA("pool", Wo[:], osrc[:, mc0:mc0 + nmc, :], [], [b_Wo])
                c0 = l * 24 + 8
                for tt in range(NTT):
                    ts = slice(tt * TT, (tt + 1) * TT)
                    for dc in range(8):
                        pW, bW = psum()
                        for mc in range(nmc):
                            OP("pe", "matmul", [b_Wo] + b_om[tt * 4:(tt + 1) * 4], [bW], out=pW[:, 0:TT], lhsT=Wo[:, mc, dc * 128:(dc + 1) * 128], rhs=om[:, mc, ts],
                               start=(mc == 0), stop=(mc == nmc - 1))
                        OP("dve", "scalar_tensor_tensor", [bW, b_coefG, xB[dc][tt]], [xB[dc][tt]], out=x[:, dc, ts], in0=pW[:, 0:TT], scalar=coefG[:, c0 + dc:c0 + dc + 1],
                           in1=x[:, dc, ts], op0=ALU.mult, op1=ALU.add)
                S.barrier()

        def mixer_phase(l):
            with ExitStack() as sm:
                U = sb("U", [128, 2, T], BF16, stack=sm)
                b_U = Buf("U")
                b_om = [Buf(f"om{c}") for c in range(NCH)]
                wsrc = w_in[l].rearrange("(kc p) n -> p kc n", p=128)
                sh = ExitStack()
                hTm = sb("hTm", [128, 8, T], BF16, stack=sh)
                b_hTm = [Buf(f"hTm{tt}") for tt in range(NTT)]
                c0 = l * 24 + 8
                osh = l * 72 + 3 * 8
                for tt in range(NTT):
                    ts = slice(tt * TT, (tt + 1) * TT)
                    norm_tile(tt, coefA[:, c0:c0 + 8], prm[:, osh:osh + 8], [b_coefA, b_prm], lambda dc: hTm[:, dc, ts], lambda dc: [b_hTm[tt]])
                with ExitStack() as su:
                    Ws5 = sb("Ws5", [128, 8, 256], BF16, stack=su)
                    b_Ws = Buf("Ws5")
                    DMA("pool", Ws5[:], wsrc[:, :, O_SU:O_SU + 256], [], [b_Ws])
                    for gh in range(2):
                        for tb in range(4):
                            ts = slice(tb * 512, (tb + 1) * 512)
                            pu_, bpu_ = psum()
                            for kc in range(8):
                                OP("pe", "matmul", [b_Ws, b_hTm[tb]], [bpu_], out=pu_[:, 0:512], lhsT=Ws5[:, kc, gh * 128:(gh + 1) * 128], rhs=hTm[:, kc, ts], start=(kc == 0), stop=(kc == 7))
                            OP("act", "copy", [bpu_], [b_U], out=U[:, gh, ts], in_=pu_[:, 0:512])
                S.barrier()
                mark(f"gla{l}")
                with ExitStack() as sg_:
                    omG = sb("omG", [128, 2, T], BF16, stack=sg_)
                    Wqk = sb("Wqk", [128, 8, 256], BF16, stack=sg_)
                    Wv = sb("Wv", [128, 8, 256], BF16, stack=sg_)
                    Wlr = sb("Wlr", [128, 8, 32], BF16, stack=sg_)
                    Wog = sb("Wog", [128, 8, 256], BF16, stack=sg_)
                    b_W = Buf("glaW")
                    DMA("pool", Wqk[:], wsrc[:, :, O_GQ:O_GQ + 256], [], [b_W])
                    DMA("pool", Wv[:], wsrc[:, :, O_GV:O_GV + 256], [], [b_W])
                    DMA("pool", Wlr[:], wsrc[:, :, O_GLF:O_GLF + 32], [], [b_W])
                    DMA("pool", Wog[:], wsrc[:, :, O_GOG:O_GOG + 256], [], [b_W])
                    up = sb("up", [16, 256], stack=sg_)
                    nb = sb("nb", [128, 2], stack=sg_)
                    nwr = sb("nwr", [128, 256], stack=sg_)
                    b_gp = Buf("glap")
                    DMA("sp", up[:], gla_up[l], [], [b_gp])
                    DMA("sp", nb[:], gla_bias[:, l * 2:l * 2 + 2], [], [b_gp])
                    DMA("sp", nwr[:], gla_nw[l], [], [b_gp])
                    OP("dve", "tensor_scalar", [b_gp], [b_gp], out=nb[:], in0=nb[:], scalar1=-1.0, scalar2=None, op0=ALU.mult)
                    ogla = sb("ogla", [128, NCH, 256], stack=sg_)
                    b_og = [Buf(f"og{c}") for c in range(NCH)]
                    S0 = sb("S0", [128, 2, 64], stack=sg_)
                    b_S0 = Buf("S0")
                    DMA("sp", S0[:, 0, :], st_gla[l, 0], [], [b_S0])
                    DMA("sp", S0[:, 1, :], st_gla[l, 1], [], [b_S0])
                    gsets = []
                    for si in range(2):
                        t2 = lambda n_, shp: sb(f"g{si}{n_}", shp, stack=sg_)
                        Sg = t2("Sg", [128, 64]); b_S = Buf(f"Sg{si}")
                        lr_sb = t2("lr", [16, 128]); b_lr = Buf(f"lr{si}")
                        spt = t2("spt", [128, 128]); cst = t2("cst", [128, 128]); eb = t2("eb", [128, 128]); einv = t2("einv", [128, 128]); edec = t2("edec", [128, 128])
                        b_e = Buf(f"e{si}")
                        sm1 = t2("sm1", [128, 2]); b_sm1 = Buf(f"sm1{si}")
                        qd = t2("qd", [128, 128]); ki = t2("ki", [128, 128]); kd = t2("kd", [128, 128]); b_qk = Buf(f"qk{si}")
                        rb = t2("rb", [128, 4, 128]); b_rb = Buf(f"rb{si}")
                        attm = t2("attm", [128, 4, 128]); b_att = Buf(f"att{si}")
                        v_sb = t2("v_sb", [128, 256]); b_v = Buf(f"v{si}")
                        kdm = t2("kdm", [128, 4, 128]); b_kdm = Buf(f"kdm{si}")
                        OP("pool", "memset", [], [b_kdm], ap=kdm[:], constant=0.0)
                        gsets.append((Sg, b_S, lr_sb, b_lr, spt, cst, eb, einv, edec, b_e, sm1, b_sm1, qd, ki, kd, b_qk, rb, b_rb, attm, b_att, v_sb, b_v, kdm, b_kdm))
                    sto = [sb(f"sto{k}", [128, 64], stack=sg_) for k in range(2)]
                    b_sto = [Buf(f"sto{k}") for k in range(2)]
                    stc = [0]
                    ogw = [False] * NCH

                    def gla_dir(d, B):
                        (Sg, b_S, lr_sb, b_lr, spt, cst, eb, einv, edec, b_e, sm1, b_sm1, qd, ki, kd, b_qk, rb, b_rb, attm, b_att, v_sb, b_v, kdm, b_kdm) = B
                        order = list(range(NCH)) if d == 0 else list(range(NCH - 1, -1, -1))
                        mk = mk_f if d == 0 else mk_b
                        for ci, c in enumerate(order):
                            cs = slice(c * CT, (c + 1) * CT)
                            tt = c // 4
                            slot_first = (c % 2 == 0) if d == 0 else (c % 2 == 1)
                            slot_last = not slot_first
                            slot = c // 2
                            pA, bA = psum()
                            for kc in range(8):
                                OP("pe", "matmul", [b_W, b_hTm[tt]], [bA], out=pA[:, 0:128], lhsT=Wqk[:, kc, 0:128], rhs=hTm[:, kc, cs], start=(kc == 0), stop=(kc == 7))
                            for kc in range(8):
                                OP("pe", "matmul", [b_W, b_hTm[tt]], [bA], out=pA[:, 128:256], lhsT=Wqk[:, kc, 128:256], rhs=hTm[:, kc, cs], start=(kc == 0), stop=(kc == 7))
                            for kc in range(8):
                                OP("pe", "matmul", [b_W, b_hTm[tt]], [bA], out=pA[0:16, 256:384], lhsT=Wlr[:, kc, 16 * d:16 * d + 16], rhs=hTm[:, kc, cs], start=(kc == 0), stop=(kc == 7))
                            OP("act", "copy", [bA], [b_lr], out=lr_sb[:], in_=pA[0:16, 256:384])
                            yield
                            pZ, bZ = psum()
                            OP("pe", "matmul", [b_lr, b_gp], [bZ], out=pZ[:, 0:128], lhsT=up[:, d * 128:(d + 1) * 128], rhs=lr_sb[:], start=True, stop=True)
                            OP("act", "activation", [bZ, b_gp], [b_e], out=spt[:], in_=pZ[:, 0:128], func=AF.Exp, scale=-1.0, bias=nb[:, d:d + 1])
                            OP("act", "activation", [b_e], [b_e], out=spt[:], in_=spt[:], func=AF.Ln, bias=1.0)
                            if d == 0:
                                OP("dve", "tensor_tensor_scan", [b_e, b_ones], [b_e], out=cst[:], data0=ones[:], data1=spt[:], initial=0.0, op0=ALU.mult, op1=ALU.add)
                                last = cst[:, 127:128]
                            else:
                                OP("dve", "tensor_tensor_scan", [b_e, b_ones], [b_e], out=cst[:, ::-1], data0=ones[:], data1=spt[:, ::-1], initial=0.0, op0=ALU.mult, op1=ALU.add)
                                last = cst[:, 0:1]
                            OP("dve", "tensor_scalar", [b_e], [b_sm1], out=sm1[:, 0:1], in0=last, scalar1=-1.0 / 16.0, scalar2=None, op0=ALU.mult)
                            OP("act", "activation", [b_sm1], [b_sm1], out=sm1[:, 1:2], in_=sm1[:, 0:1], func=AF.Exp)
                            OP("act", "activation", [b_e], [b_e], out=eb[:], in_=cst[:], func=AF.Exp, scale=-1.0 / 16.0)
                            OP("act", "activation", [b_e], [b_e], out=einv[:], in_=cst[:], func=AF.Exp, scale=1.0 / 16.0)
                            OP("act", "activation", [b_e, b_sm1], [b_e], out=edec[:], in_=cst[:], func=AF.Exp, scale=1.0 / 16.0, bias=sm1[:, 0:1])
                            OP("dve", "scalar_tensor_tensor", [bA, b_e], [b_qk], out=qd[:], in0=pA[:, 0:128], scalar=32.0 ** -0.5, in1=eb[:], op0=ALU.mult, op1=ALU.mult)
                            OP("dve", "tensor_tensor", [bA, b_e], [b_qk], out=ki[:], in0=pA[:, 128:256], in1=einv[:], op=ALU.mult)
                            OP("dve", "tensor_tensor", [bA, b_e], [b_qk], out=kd[:], in0=pA[:, 128:256], in1=edec[:], op=ALU.mult)
                            OP("dve", "tensor_tensor", [b_qk, b_cst], [b_rb], out=rb[:], in0=qd[:].unsqueeze(1).to_broadcast([128, 4, 128]),
                               in1=hmask[:].unsqueeze(2).to_broadcast([128, 4, 128]), op=ALU.mult)
                            yield
                            pT, bT = psum()
                            OP("pe", "matmul", [b_qk, b_rb], [bT], out=pT[:, 0:512], lhsT=ki[:], rhs=rb[:].rearrange("p h t -> p (h t)"), start=True, stop=True)
                            OP("dve", "tensor_tensor", [bT, b_cst], [b_att], out=attm[:], in0=pT[:, 0:512].rearrange("p (h t) -> p h t", h=4),
                               in1=mk[:].unsqueeze(1).to_broadcast([128, 4, 128]), op=ALU.mult)
                            yield
                            pV, bV = psum()
                            for kc in range(8):
                                OP("pe", "matmul", [b_W, b_hTm[tt]], [bV], out=pV[:, 0:256], lhsT=hTm[:, kc, cs], rhs=Wv[:, kc, :], start=(kc == 0), stop=(kc == 7))
                            OP("act", "copy", [bV], [b_v], out=v_sb[:], in_=pV[:, 0:256])
                            OP("pe", "transpose", [b_qk, b_cst], [bV], out=pV[:, 256:384], in_=kd[:], identity=ident[:])
                            for h in range(4):
                                OP("act", "copy", [bV], [b_kdm], out=kdm[:, h, h * 32:(h + 1) * 32], in_=pV[:, 256 + h * 32:256 + (h + 1) * 32])
                            yield
                            if ci == 0:
                                OP("dve", "tensor_copy", [b_S0], [b_S], out=Sg[:], in_=S0[:, d, :])
                            elif slot_first:
                                OP("dve", "tensor_scalar", [b_S, b_cst], [b_S], out=Sg[:], in0=Sg[:], scalar1=carry[:, 0:1], scalar2=None, op0=ALU.mult)
                            pO, bO = psum()
                            for h in range(4):
                                OP("pe", "matmul", [b_att, b_v], [bO], out=pO[:, h * 64:(h + 1) * 64], lhsT=attm[:, h, :], rhs=v_sb[:, h * 64:(h + 1) * 64], start=True, stop=False)
                                OP("pe", "matmul", [b_rb, b_S], [bO], out=pO[:, h * 64:(h + 1) * 64], lhsT=rb[:, h, :], rhs=Sg[:], start=False, stop=True)
                            if not ogw[c]:
                                ogw[c] = True
                                OP("act", "copy", [bO], [b_og[c]], out=ogla[:, c, :], in_=pO[:, 0:256])
                            else:
                                OP("dve", "tensor_tensor", [bO, b_og[c]], [b_og[c]], out=ogla[:, c, :], in0=pO[:, 0:256], in1=ogla[:, c, :], op=ALU.add)
                            yield
                            pS, bS = psum()
                            for h in range(4):
                                OP("pe", "matmul", [b_kdm, b_v], [bS], out=pS[:, 0:64], lhsT=kdm[:, h, :], rhs=v_sb[:, h * 64:(h + 1) * 64], start=(h == 0), stop=(h == 3))
                            OP("dve", "scalar_tensor_tensor", [bS, b_S, b_sm1], [b_S], out=Sg[:], in0=Sg[:], scalar=sm1[:, 1:2], in1=pS[:, 0:64], op0=ALU.mult, op1=ALU.add)
                            if slot_last:
                                k = stc[0] % 2
                                stc[0] += 1
                                OP("act", "copy", [b_S], [b_sto[k]], out=sto[k][:], in_=Sg[:])
                                DMA("sp", o_stgla[l, d, slot], sto[k][:], [b_sto[k]], [], is_out=True)
                    gens = [gla_dir(0, gsets[0]), gla_dir(1, gsets[1])]
                    alive = list(gens)
                    while alive:
                        for g_ in list(alive):
                            try:
                                next(g_)
                            except StopIteration:
                                alive.remove(g_)
                    sqo = sb("sqo", [128, 256], stack=sg_)
                    ssq = sb("ssq", [128, 4], stack=sg_)
                    gts = sb("gts", [128, 256], stack=sg_)
                    b_fin = Buf("fin")
                    for c in range(NCH):
                        cs = slice(c * CT, (c + 1) * CT)
                        tt = c // 4
                        OP("dve", "tensor_tensor", [b_og[c]], [b_fin], out=sqo[:], in0=ogla[:, c, :], in1=ogla[:, c, :], op=ALU.mult)
                        OP("dve", "tensor_reduce", [b_fin], [b_fin], out=ssq[:], in_=sqo[:].rearrange("p (h e) -> p h e", h=4), axis=mybir.AxisListType.X, op=ALU.add)
                        OP("act", "activation", [b_fin], [b_fin], out=ssq[:], in_=ssq[:], func=AF.Sqrt, scale=1.0 / 64.0, bias=EPS)
                        OP("dve", "reciprocal", [b_fin], [b_fin], out=ssq[:], in_=ssq[:])
                        OP("dve", "tensor_tensor", [b_fin, b_og[c]], [b_fin], out=sqo[:].rearrange("p (h e) -> p h e", h=4), in0=ogla[:, c, :].rearrange("p (h e) -> p h e", h=4),
                           in1=ssq[:].unsqueeze(2).to_broadcast([128, 4, 64]), op=ALU.mult)
                        OP("dve", "tensor_tensor", [b_fin, b_gp], [b_fin], out=sqo[:], in0=sqo[:], in1=nwr[:], op=ALU.mult)
                        pG, bG = psum()
                        for kc in range(8):
                            OP("pe", "matmul", [b_W, b_hTm[tt]], [bG], out=pG[:, 0:256], lhsT=hTm[:, kc, cs], rhs=Wog[:, kc, :], start=(kc == 0), stop=(kc == 7))
                        OP("act", "activation", [bG], [b_fin], out=gts[:], in_=pG[:, 0:256], func=AF.Silu)
                        OP("dve", "tensor_tensor", [b_fin], [b_fin], out=sqo[:], in0=sqo[:], in1=gts[:], op=ALU.mult)
                        for m in range(2):
                            OP("pe", "transpose", [b_fin, b_cst], [bG], out=pG[:, 256 + m * 128:256 + (m + 1) * 128], in_=sqo[:, m * 128:(m + 1) * 128], identity=ident[:])
                            OP("act", "copy", [bG], [b_om[c]], out=omG[:, m, cs], in_=pG[:, 256 + m * 128:256 + (m + 1) * 128])
                    wout_part(l, 0, 2, omG, b_om)
                S.barrier()
                mark(f"gdn1_{l}")
                if "gdn" in mixers:
                  with ExitStack() as s1:
                    Wqkv = sb("Wqkv", [128, 8, 1536], BF16, stack=s1)
                    b_Wq = Buf("Wqkv")
                    for q3 in range(3):
                        DMA("pool", Wqkv[:, :, q3 * 512:(q3 + 1) * 512], wsrc[:, :, O_DQ + q3 * 512:O_DQ + (q3 + 1) * 512], [], [b_Wq])
                    cw = sb("cw", [128, 12, 5], stack=s1)
                    b_cw = Buf("cw")
                    DMA("sp", cw[:], gdn_convw[:, l * 60:(l + 1) * 60].rearrange("p (c j) -> p c j", j=5), [], [b_cw])
                    win = sb("win", [128, 12, 132], stack=s1)
                    b_win = Buf("win")
                    cmk = [sb(f"cmk{k}", [128, 4, 128], stack=s1) for k in range(2)]
                    b_cmk = [Buf(f"cmk{k}") for k in range(2)]
                    acc = sb("acc", [128, 12, 128], stack=s1)
                    tmpc = [sb(f"tmpc{k}", [128, 12, 128], stack=s1) for k in range(2)]
                    b_acc = Buf("acc")
                    b_tmpc = [Buf(f"tmpc{k}") for k in range(2)]
                    zs = [sb(f"zs{k}", [128, 12, 128], stack=s1) for k in range(2)]
                    b_zs = [Buf(f"zs{k}") for k in range(2)]
                    sq8 = sb("sq8", [128, 8, 128], stack=s1)
                    rs8 = sb("rs8", [128, 8, 128], stack=s1)
                    b_s8 = Buf("s8")
                    OP("pool", "memset", [], [b_win], ap=win[:], constant=0.0)
                    for c in range(NCH):
                        tt = c // 4
                        lo = max(c * CT - 2, 0)
                        hi = min(c * CT + CT + 2, T)
                        o0 = lo - (c * CT - 2)
                        n = hi - lo
                        k2 = c % 2
                        DMA("sp", cmk[k2][:], c_cmask[:, :, c * CT:(c + 1) * CT], [], [b_cmk[k2]])
                        if c == NCH - 1:
                            OP("pool", "memset", [], [b_win], ap=win[:], constant=0.0)
                        rd_h = [b_hTm[max(lo // TT, 0)], b_hTm[min((hi - 1) // TT, NTT - 1)]]
                        for g3 in range(4):
                            pw, bw = psum()
                            for j3 in range(3):
                                ct = g3 * 3 + j3
                                for kc in range(8):
                                    OP("pe", "matmul", [b_Wq] + rd_h, [bw], out=pw[:, j3 * 132:j3 * 132 + n], lhsT=Wqkv[:, kc, ct * 128:(ct + 1) * 128], rhs=hTm[:, kc, lo:hi],
                                       start=(kc == 0), stop=(kc == 7))
                            OP("act", "copy", [bw], [b_win], out=win[:, g3 * 3:(g3 + 1) * 3, o0:o0 + n], in_=pw[:, 0:396].rearrange("p (a b) -> p a b", a=3)[:, :, 0:n])
                        OP("dve", "tensor_tensor", [b_win, b_cw], [b_acc], out=acc[:], in0=win[:, :, 2:130], in1=cw[:, :, 2:3].to_broadcast([128, 12, 128]), op=ALU.mult)
                        for ji, j in enumerate((-2, -1, 1, 2)):
                            e_ = "pool" if ji % 2 == 0 else "dve"
                            k3 = ji % 2
                            OP(e_, "tensor_tensor", [b_win, b_cw], [b_tmpc[k3]], out=tmpc[k3][:], in0=win[:, :, 2 + j:130 + j],
                               in1=cw[:, :, 2 + j:3 + j].to_broadcast([128, 12, 128]), op=ALU.mult)
                            OP(e_, "tensor_tensor", [b_tmpc[k3], b_cmk[k2]], [b_tmpc[k3]], out=tmpc[k3][:], in0=tmpc[k3][:],
                               in1=cmk[k2][:, ji:ji + 1, :].to_broadcast([128, 12, 128]), op=ALU.mult)
                            OP("dve", "tensor_tensor", [b_tmpc[k3], b_acc], [b_acc], out=acc[:], in0=acc[:], in1=tmpc[k3][:], op=ALU.add)
                        z = zs[k2]
                        OP("act", "activation", [b_acc], [b_zs[k2]], out=z[:], in_=acc[:], func=AF.Silu)
                        OP("act", "activation", [b_zs[k2]], [b_s8], out=sq8[:], in_=z[:, 0:8, :], func=AF.Square)
                        for hf in range(2):
                            pn, bn = psum()
                            OP("pe", "matmul", [b_s8, b_ones], [bn], out=pn[:, 0:512], lhsT=ones[:], rhs=sq8[:, hf * 4:(hf + 1) * 4, :].rearrange("p a b -> p (a b)"), start=True, stop=True)
                            OP("act", "activation", [bn], [b_s8], out=rs8[:, hf * 4:(hf + 1) * 4, :].rearrange("p a b -> p (a b)"), in_=pn[:, 0:512], func=AF.Sqrt, bias=EPS)
                        OP("dve", "reciprocal", [b_s8], [b_s8], out=rs8[:], in_=rs8[:])
                        OP("dve", "scalar_tensor_tensor", [b_s8, b_zs[k2]], [b_zs[k2]], out=z[:, 0:4, :], in0=z[:, 0:4, :], scalar=128.0 ** -0.5, in1=rs8[:, 0:4, :], op0=ALU.mult, op1=ALU.mult)
                        OP("dve", "tensor_tensor", [b_s8, b_zs[k2]], [b_zs[k2]], out=z[:, 4:8, :], in0=z[:, 4:8, :], in1=rs8[:, 4:8, :], op=ALU.mult)
                        DMA("sp", zq[c], z[:].rearrange("p a b -> p (a b)"), [b_zs[k2]], [b_zq[c]])
                  S.barrier()
                  mark(f"gdn2_{l}")
                  gstep = float(os.environ.get("KGDN", "99"))
                  s_om = ExitStack()
                  omD = sb("omD", [128, 4, T], BF16, stack=s_om)
                  with ExitStack() as s2:
                    Wab = sb("Wab", [128, 8, 16], BF16, stack=s2)
                    Wdog = sb("Wdog", [128, 8, 512], BF16, stack=s2)
                    b_W2 = Buf("gdnW2")
                    DMA("pool", Wab[:], wsrc[:, :, O_DAF:O_DAF + 16], [], [b_W2])
                    DMA("pool", Wdog[:], wsrc[:, :, O_DOG:O_DOG + 512], [], [b_W2])
                    dtb = sb("dtb", [128, 8], stack=s2)
                    nA = sb("nA", [128, 8], stack=s2)
                    nwr = sb("nwrd", [128, 512], stack=s2)
                    b_gp = Buf("gdnp")
                    DMA("sp", dtb[:], gdn_dtb[:, l * 8:(l + 1) * 8], [], [b_gp])
                    DMA("sp", nA[:], gdn_alog[:, l * 8:(l + 1) * 8], [], [b_gp])
                    DMA("sp", nwr[:], gdn_nw[l], [], [b_gp])
                    OP("act", "activation", [b_gp], [b_gp], out=nA[:], in_=nA[:], func=AF.Exp)
                    OP("dve", "tensor_scalar", [b_gp], [b_gp], out=nA[:], in0=nA[:], scalar1=-1.0, scalar2=None, op0=ALU.mult)
                    TTb = sb("TTb", [128, 4, 128], stack=s2)
                    Sd = sb("Sd", [128, 4, 128], stack=s2)
                    b_S = Buf("Sd")
                    S0 = sb("S0d", [128, 512], stack=s2)
                    b_S0 = Buf("S0d")
                    zin = [sb(f"zin{k}", [128, 12, 128], stack=s2) for k in range(2)]
                    b_zin = [Buf(f"zin{k}") for k in range(2)]
                    gt = sb("gt", [128, 40], stack=s2)
                    b_gt = Buf("gt")
                    dg = sb("dg", [128, 4, 128], stack=s2)
                    b_dg = Buf("dg")
                    Eb = sb("Eb", [128, 4, 128], stack=s2)
                    ETb = sb("ETb", [128, 4, 128], stack=s2)
                    Egr = sb("Egr", [128, 4, 128], stack=s2)
                    b_E = Buf("E")
                    b_ET = Buf("ET")
                    b_Egr = Buf("Egr")
                    Ab = [sb(f"Ab{k}", [128, 4, 128], stack=s2) for k in range(2)]
                    Bb = [sb(f"Bb{k}", [128, 4, 128], stack=s2) for k in range(2)]
                    b_Ab = [Buf(f"Ab{k}") for k in range(2)]
                    b_Bb = [Buf(f"Bb{k}") for k in range(2)]
                    b_TT = Buf("TT")
                    Xb = sb("Xb", [128, 4, 128], stack=s2)
                    b_X = Buf("Xb")
                    rw = sb("rw", [128, 4, 128], stack=s2)
                    ru = sb("ru", [128, 4, 128], stack=s2)
                    kdc = sb("kdc", [128, 4, 128], stack=s2)
                    b_rk = Buf("rk")
                    wT = sb("wT", [128, 4, 128], stack=s2)
                    b_wT = Buf("wT")
                    u_sb = sb("u_sb", [128, 4, 128], stack=s2)
                    b_u = Buf("u")
                    qeg = sb("qeg", [128, 4, 128], stack=s2)
                    b_qeg = Buf("qeg")
                    qkm = sb("qkm", [128, 4, 128], stack=s2)
                    b_qkm = Buf("qkm")
                    ost = [sb(f"ost{k}", [128, 512], stack=s2) for k in range(1)] * 2
                    b_ost = [Buf(f"ost{k}") for k in range(1)] * 2
                    oin = sb("oin", [128, 512], stack=s2)
                    b_oin = Buf("oin")
                    fo = sb("fo", [128, 512], stack=s2)
                    fg = sb("fg", [128, 512], stack=s2)
                    fs = sb("fs", [128, 4], stack=s2)
                    b_f = Buf("fin")
                    sto = ost
                    b_sto = b_ost
                    stc = 0
                    oc = 0
                    zc = 0
                    bc4 = lambda ap: ap.unsqueeze(2).to_broadcast([128, 4, 128])
                    flat = lambda t_: t_[:].rearrange("p a b -> p (a b)")
                    for d in range(2 if gstep >= 2 else 0):
                        DMA("sp", S0[:], st_gdn[l, d], [], [b_S0])
                        order = list(range(NCH)) if d == 0 else list(range(NCH - 1, -1, -1))
                        mk_i = mk_f if d == 0 else mk_b
                        mk_s = mk_bs if d == 0 else mk_fs
                        for ci, c in enumerate(order):
                            cs = slice(c * CT, (c + 1) * CT)
                            tt = c // 4
                            slot_first = (c % 2 == 0) if d == 0 else (c % 2 == 1)
                            slot_last = not slot_first
                            slot = c // 2
                            zk = zc % 2
                            zc += 1
                            Z = zin[zk]
                            DMA("sp", flat(Z), zq[c], [b_zq[c]], [b_zin[zk]])
                            pg, bg = psum()
                            for kc in range(8):
                                OP("pe", "matmul", [b_W2, b_hTm[tt]], [bg], out=pg[:, 0:16], lhsT=hTm[:, kc, cs], rhs=Wab[:, kc, :], start=(kc == 0), stop=(kc == 7))
                            OP("dve", "tensor_tensor", [bg, b_gp], [b_gt], out=gt[:, 32:36], in0=pg[:, 4 * d:4 * d + 4], in1=dtb[:, 4 * d:4 * d + 4], op=ALU.add)
                            OP("act", "activation", [b_gt], [b_gt], out=gt[:, 32:36], in_=gt[:, 32:36], func=AF.Exp)
                            OP("act", "activation", [b_gt], [b_gt], out=gt[:, 32:36], in_=gt[:, 32:36], func=AF.Ln, bias=1.0)
                            OP("dve", "tensor_tensor", [b_gt, b_gp], [b_gt], out=gt[:, 0:4], in0=gt[:, 32:36], in1=nA[:, 4 * d:4 * d + 4], op=ALU.mult)
                            OP("act", "activation", [bg], [b_gt], out=gt[:, 4:8], in_=pg[:, 8 + 4 * d:12 + 4 * d], func=AF.Exp, scale=-1.0)
                            OP("dve", "tensor_scalar", [b_gt], [b_gt], out=gt[:, 4:8], in0=gt[:, 4:8], scalar1=1.0, scalar2=None, op0=ALU.add)
                            OP("dve", "reciprocal", [b_gt], [b_gt], out=gt[:, 4:8], in_=gt[:, 4:8])
                            pc, bpc = psum()
                            OP("pe", "matmul", [b_gt, b_cst], [bpc], out=pc[:, 0:4], lhsT=mk_i[:], rhs=gt[:, 0:4], start=True, stop=True)
                            OP("pe", "matmul", [b_gt, b_ones], [bpc], out=pc[:, 4:8], lhsT=ones[:], rhs=gt[:, 0:4], start=True, stop=True)
                            OP("act", "copy", [bpc], [b_gt], out=gt[:, 8:12], in_=pc[:, 0:4])
                            OP("act", "activation", [bpc], [b_gt], out=gt[:, 12:16], in_=pc[:, 0:4], func=AF.Exp)
                            OP("act", "activation", [bpc], [b_gt], out=gt[:, 24:28], in_=pc[:, 4:8], func=AF.Exp)
                            OP("dve", "tensor_tensor", [bpc, b_gt], [b_gt], out=gt[:, 20:24], in0=pc[:, 4:8], in1=gt[:, 8:12], op=ALU.subtract)
                            OP("act", "activation", [b_gt], [b_gt], out=gt[:, 20:24], in_=gt[:, 20:24], func=AF.Exp)
                            OP("dve", "scalar_tensor_tensor", [b_gt], [b_gt], out=gt[:, 28:32], in0=gt[:, 4:8], scalar=-1.0, in1=gt[:, 12:16], op0=ALU.mult, op1=ALU.mult)
                            if gstep < 3:
                                continue
                            OP("dve", "tensor_tensor", [b_gt, b_cst], [b_dg], out=dg[:], in0=ident[:].unsqueeze(1).to_broadcast([128, 4, 128]), in1=bc4(gt[:, 8:12]), op=ALU.mult)
                            pG, bG = psum()
                            OP("pe", "matmul", [b_dg, b_ones], [bG], out=pG[:, 0:512], lhsT=ones[:], rhs=flat(dg), start=True, stop=True)
                            OP("dve", "scalar_tensor_tensor", [bG, b_gt], [b_E], out=Eb[:], in0=pG[:, 0:512].rearrange("p (a b) -> p a b", a=4), scalar=-1.0, in1=bc4(gt[:, 8:12]),
                               op0=ALU.mult, op1=ALU.add)
                            OP("dve", "tensor_scalar", [b_E], [b_E], out=Eb[:], in0=Eb[:], scalar1=0.0, scalar2=None, op0=ALU.min)
                            OP("act", "activation", [b_E], [b_E], out=Eb[:], in_=Eb[:], func=AF.Exp)
                            OP("dve", "tensor_tensor", [bG, b_gt], [b_ET], out=ETb[:], in0=pG[:, 0:512].rearrange("p (a b) -> p a b", a=4), in1=bc4(gt[:, 8:12]), op=ALU.subtract)
                            OP("dve", "tensor_scalar", [b_ET], [b_ET], out=ETb[:], in0=ETb[:], scalar1=0.0, scalar2=None, op0=ALU.min)
                            OP("act", "activation", [b_ET], [b_ET], out=ETb[:], in_=ETb[:], func=AF.Exp)
                            OP("pool", "tensor_tensor", [b_ET, b_cst], [b_ET], out=ETb[:], in0=ETb[:], in1=mk_i[:].unsqueeze(1).to_broadcast([128, 4, 128]), op=ALU.mult)
                            OP("act", "activation", [bG], [b_Egr], out=flat(Egr), in_=pG[:, 0:512], func=AF.Exp)
                            if gstep < 3.2:
                                continue
                            pK, bK = psum()
                            pQ, bQ = psum()
                            for h in range(4):
                                OP("pe", "matmul", [b_zin[zk]], [bK], out=pK[:, h * 128:(h + 1) * 128], lhsT=Z[:, 4 + h, :], rhs=Z[:, 4 + h, :], start=True, stop=True)
                            for h in range(4):
                                OP("pe", "matmul", [b_zin[zk]], [bQ], out=pQ[:, h * 128:(h + 1) * 128], lhsT=Z[:, 4 + h, :], rhs=Z[:, h, :], start=True, stop=True)
                            if gstep < 3.5:
                                continue
                            A0 = Ab[0]
                            OP("pool", "tensor_tensor", [b_E, b_cst], [b_E], out=Eb[:], in0=Eb[:], in1=mk_s[:].unsqueeze(1).to_broadcast([128, 4, 128]), op=ALU.mult)
                            OP("dve", "scalar_tensor_tensor", [bK, b_E], [b_Ab[0]], out=flat(A0), in0=pK[:, 0:512], scalar=-1.0, in1=flat(Eb), op0=ALU.mult, op1=ALU.mult)
                            OP("dve", "tensor_tensor", [b_Ab[0], b_gt], [b_Ab[0]], out=A0[:], in0=A0[:], in1=bc4(gt[:, 4:8]), op=ALU.mult)
                            OP("dve", "tensor_tensor", [bQ, b_ET], [b_qkm], out=flat(qkm), in0=pQ[:, 0:512], in1=flat(ETb), op=ALU.mult)
                            OP("pool", "tensor_tensor", [b_zin[zk], b_Egr], [b_qeg], out=qeg[:], in0=Z[:, 0:4, :], in1=Egr[:], op=ALU.mult)
                            if gstep < 3.8:
                                continue
                            pB, bB = psum()
                            for h in range(4):
                                OP("pe", "matmul", [b_Ab[0], b_cst], [bB], out=pB[:, h * 128:(h + 1) * 128], lhsT=A0[:, h, :], rhs=ident[:], start=True, stop=True)
                            if os.environ.get("KX", "") != "1":
                                OP("act", "copy", [bB], [b_Bb[0]], out=flat(Bb[0]), in_=pB[:, 0:512])
                            if gstep < 5:
                                continue
                            Afull, Bfull = Ab[0], Bb[0]
                            Ak, Bk = Ab[1], Bb[1]
                            b_Ak, b_Bk = b_Ab[1], b_Bb[1]
                            mbc = lambda m_: m_[:].unsqueeze(1).to_broadcast([128, 4, 128])
                            OP("pool", "tensor_tensor", [b_Ab[0], b_cst], [b_Ak], out=Ak[:], in0=Afull[:], in1=mbc(bd16), op=ALU.mult)
                            OP("pool", "tensor_tensor", [b_Bb[0], b_cst], [b_Bk], out=Bk[:], in0=Bfull[:], in1=mbc(bd16), op=ALU.mult)
                            OP("dve", "tensor_tensor", [b_Ak, b_cst], [b_X], out=Xb[:], in0=Ak[:], in1=mbc(ident), op=ALU.add)
                            OP("dve", "tensor_tensor", [b_Bk, b_cst], [b_TT], out=TTb[:], in0=Bk[:], in1=mbc(ident), op=ALU.add)
                            for k in range(1, 4):
                                pA1, bA1 = psum()
                                pB1, bB1 = psum()
                                for h in range(4):
                                    OP("pe", "matmul", [b_Ak, b_Bk], [bA1], out=pA1[:, h * 128:(h + 1) * 128], lhsT=Bk[:, h, :], rhs=Ak[:, h, :], start=True, stop=True)
                                for h in range(4):
                                    OP("pe", "matmul", [b_Ak, b_Bk], [bB1], out=pB1[:, h * 128:(h + 1) * 128], lhsT=Ak[:, h, :], rhs=Bk[:, h, :], start=True, stop=True)
                                OP("act", "copy", [bA1], [b_Ak], out=flat(Ak), in_=pA1[:, 0:512])
                                OP("act", "copy", [bB1], [b_Bk], out=flat(Bk), in_=pB1[:, 0:512])
                                pX1, bX1 = psum()
                                pT1, bT1 = psum()
                                for h in range(4):
                                    OP("pe", "matmul", [b_TT, b_Ak], [bX1], out=pX1[:, h * 128:(h + 1) * 128], lhsT=TTb[:, h, :], rhs=Ak[:, h, :], start=True, stop=True)
                                for h in range(4):
                                    OP("pe", "matmul", [b_Ak, b_TT], [bT1], out=pT1[:, h * 128:(h + 1) * 128], lhsT=Ak[:, h, :], rhs=TTb[:, h, :], start=True, stop=True)
                                OP("dve", "tensor_tensor", [bX1, b_X], [b_X], out=flat(Xb), in0=pX1[:, 0:512], in1=flat(Xb), op=ALU.add)
                                OP("dve", "tensor_tensor", [bT1, b_TT], [b_TT], out=flat(TTb), in0=pT1[:, 0:512], in1=flat(TTb), op=ALU.add)
                            for li in range(3):
                                mA_ = (MLm if d == 0 else MUm)[li]
                                mB_ = (MUm if d == 0 else MLm)[li]
                                MA, MB, Qa, Qb = dg, Eb, ETb, Egr
                                OP("pool", "tensor_tensor", [b_Ab[0], b_cst], [b_dg], out=MA[:], in0=Afull[:], in1=mbc(mA_), op=ALU.mult)
                                OP("pool", "tensor_tensor", [b_Bb[0], b_cst], [b_E], out=MB[:], in0=Bfull[:], in1=mbc(mB_), op=ALU.mult)
                                pQa, bQa = psum()
                                pQb, bQb = psum()
                                for h in range(4):
                                    OP("pe", "matmul", [b_E, b_X], [bQa], out=pQa[:, h * 128:(h + 1) * 128], lhsT=MB[:, h, :], rhs=Xb[:, h, :], start=True, stop=True)
                                for h in range(4):
                                    OP("pe", "matmul", [b_dg, b_TT], [bQb], out=pQb[:, h * 128:(h + 1) * 128], lhsT=MA[:, h, :], rhs=TTb[:, h, :], start=True, stop=True)
                                OP("act", "copy", [bQa], [b_ET], out=flat(Qa), in_=pQa[:, 0:512])
                                OP("act", "copy", [bQb], [b_Egr], out=flat(Qb), in_=pQb[:, 0:512])
                                pX1, bX1 = psum()
                                pT1, bT1 = psum()
                                for h in range(4):
                                    OP("pe", "matmul", [b_TT, b_ET], [bX1], out=pX1[:, h * 128:(h + 1) * 128], lhsT=TTb[:, h, :], rhs=Qa[:, h, :], start=True, stop=True)
                                for h in range(4):
                                    OP("pe", "matmul", [b_X, b_Egr], [bT1], out=pT1[:, h * 128:(h + 1) * 128], lhsT=Xb[:, h, :], rhs=Qb[:, h, :], start=True, stop=True)
                                OP("dve", "tensor_tensor", [bX1, b_X], [b_X], out=flat(Xb), in0=pX1[:, 0:512], in1=flat(Xb), op=ALU.add)
                                OP("dve", "tensor_tensor", [bT1, b_TT], [b_TT], out=flat(TTb), in0=pT1[:, 0:512], in1=flat(TTb), op=ALU.add)
                            if gstep < 6:
                                continue
                            pKt, bKt = psum()
                            pVt, bVt = psum()
                            for h in range(4):
                                OP("pe", "transpose", [b_zin[zk], b_cst], [bKt], out=pKt[:, h * 128:(h + 1) * 128], in_=Z[:, 4 + h, :], identity=ident[:])
                            for h in range(4):
                                OP("pe", "transpose", [b_zin[zk], b_cst], [bVt], out=pVt[:, h * 128:(h + 1) * 128], in_=Z[:, 8 + h, :], identity=ident[:])
                            OP("dve", "tensor_tensor", [bKt, b_gt], [b_rk], out=rw[:], in0=pKt[:, 0:512].rearrange("p (a b) -> p a b", a=4), in1=bc4(gt[:, 28:32]), op=ALU.mult)
                            OP("dve", "tensor_tensor", [bKt, b_gt], [b_rk], out=kdc[:], in0=pKt[:, 0:512].rearrange("p (a b) -> p a b", a=4), in1=bc4(gt[:, 20:24]), op=ALU.mult)
                            OP("dve", "tensor_tensor", [bVt, b_gt], [b_rk], out=ru[:], in0=pVt[:, 0:512].rearrange("p (a b) -> p a b", a=4), in1=bc4(gt[:, 4:8]), op=ALU.mult)
                            pWt, bWt = psum()
                            for h in range(4):
                                OP("pe", "matmul", [b_rk, b_TT], [bWt], out=pWt[:, h * 128:(h + 1) * 128], lhsT=rw[:, h, :], rhs=TTb[:, h, :], start=True, stop=True)
                            OP("act", "copy", [bWt], [b_wT], out=flat(wT), in_=pWt[:, 0:512])
                            if gstep < 7:
                                continue
                            if ci == 0:
                                OP("dve", "tensor_copy", [b_S0], [b_S], out=flat(Sd), in_=S0[:])
                            elif slot_first:
                                OP("dve", "tensor_scalar", [b_S, b_cst], [b_S], out=flat(Sd), in0=flat(Sd), scalar1=carry[:, 0:1], scalar2=None, op0=ALU.mult)
                            pU, bU = psum()
                            for h in range(4):
                                OP("pe", "matmul", [b_TT, b_rk], [bU], out=pU[:, h * 128:(h + 1) * 128], lhsT=TTb[:, h, :], rhs=ru[:, h, :], start=True, stop=False)
                                OP("pe", "matmul", [b_wT, b_S], [bU], out=pU[:, h * 128:(h + 1) * 128], lhsT=wT[:, h, :], rhs=Sd[:, h, :], start=False, stop=True)
                            OP("act", "copy", [bU], [b_u], out=flat(u_sb), in_=pU[:, 0:512])
                            pO, bO = psum()
                            for h in range(4):
                                OP("pe", "matmul", [b_qeg, b_S], [bO], out=pO[:, h * 128:(h + 1) * 128], lhsT=qeg[:, h, :], rhs=Sd[:, h, :], start=True, stop=False)
                                OP("pe", "matmul", [b_qkm, b_u], [bO], out=pO[:, h * 128:(h + 1) * 128], lhsT=qkm[:, h, :], rhs=u_sb[:, h, :], start=False, stop=True)
                            pS, bS = psum()
                            for h in range(4):
                                OP("pe", "matmul", [b_rk, b_u], [bS], out=pS[:, h * 128:(h + 1) * 128], lhsT=kdc[:, h, :], rhs=u_sb[:, h, :], start=True, stop=True)
                            OP("dve", "tensor_tensor", [b_S, b_gt], [b_S], out=Sd[:], in0=Sd[:], in1=bc4(gt[:, 24:28]), op=ALU.mult)
                            OP("dve", "tensor_tensor", [b_S, bS], [b_S], out=flat(Sd), in0=flat(Sd), in1=pS[:, 0:512], op=ALU.add)
                            if slot_last:
                                k = stc % 2
                                stc += 1
                                OP("act", "copy", [b_S], [b_sto[k]], out=sto[k][:], in_=flat(Sd))
                                DMA("sp", o_stgdn[l, d, slot], sto[k][:], [b_sto[k]], [], is_out=True)
                            if d == 0:
                                k = oc % 2
                                oc += 1
                                OP("act", "copy", [bO], [b_ost[k]], out=ost[k][:], in_=pO[:, 0:512])
                                DMA("sp", ogs[c], ost[k][:], [b_ost[k]], [b_ogs[c]])
                            else:
                                DMA("sp", oin[:], ogs[c], [b_ogs[c]], [b_oin])
                                OP("dve", "tensor_tensor", [bO, b_oin], [b_f], out=fo[:], in0=pO[:, 0:512], in1=oin[:], op=ALU.add)
                                OP("act", "activation", [b_f], [b_f], out=fg[:], in_=fo[:], func=AF.Square)
                                OP("dve", "tensor_reduce", [b_f], [b_f], out=fs[:], in_=fg[:].rearrange("p (h e) -> p h e", h=4), axis=mybir.AxisListType.X, op=ALU.add)
                                OP("act", "activation", [b_f], [b_f], out=fs[:], in_=fs[:], func=AF.Sqrt, scale=1.0 / 128.0, bias=EPS)
                                OP("dve", "reciprocal", [b_f], [b_f], out=fs[:], in_=fs[:])
                                OP("dve", "tensor_tensor", [b_f], [b_f], out=fo[:].rearrange("p (h e) -> p h e", h=4), in0=fo[:].rearrange("p (h e) -> p h e", h=4),
                                   in1=fs[:].unsqueeze(2).to_broadcast([128, 4, 128]), op=ALU.mult)
                                OP("dve", "tensor_tensor", [b_f, b_gp], [b_f], out=fo[:], in0=fo[:], in1=nwr[:], op=ALU.mult)
                                pD, bD = psum()
                                for kc in range(8):
                                    OP("pe", "matmul", [b_W2, b_hTm[tt]], [bD], out=pD[:, 0:512], lhsT=hTm[:, kc, cs], rhs=Wdog[:, kc, :], start=(kc == 0), stop=(kc == 7))
                                OP("act", "activation", [bD], [b_f], out=fg[:], in_=pD[:, 0:512], func=AF.Silu)
                                OP("dve", "tensor_tensor", [b_f], [b_f], out=fo[:], in0=fo[:], in1=fg[:], op=ALU.mult)
                                pR, bR = psum()
                                for m in range(4):
                                    OP("pe", "transpose", [b_f, b_cst], [bR], out=pR[:, m * 128:(m + 1) * 128], in_=fo[:, m * 128:(m + 1) * 128], identity=ident[:])
                                OP("act", "copy", [bR], [b_om[c]], out=omD[:, :, cs], in_=pR[:, 0:512].rearrange("p (a b) -> p a b", a=4))
                  S.barrier()
                  if gstep >= 99:
                      wout_part(l, 2, 4, omD, b_om)
                  s_om.close()
                  S.barrier()
                sh.close()
                S.barrier()
                mark(f"s5_{l}")
                if "s5" in mixers:
                    s5_part(l, U, b_U, b_om)

        import math
        PI = math.pi

        C1_ = 6.28125
        C2_ = 2 * PI - 6.28125
        I32 = mybir.dt.int32

        def sincos(ang, ki, kf, sn, cs, b_ang, b_ki, b_kf, b_sn, b_cs, e1="dve"):
            OP(e1, "tensor_scalar", [b_ang], [b_ki], out=ki, in0=ang, scalar1=1.0 / (2 * PI), scalar2=None, op0=ALU.mult)
            OP(e1, "tensor_copy", [b_ki], [b_kf], out=kf, in_=ki)
            OP("dve", "scalar_tensor_tensor", [b_kf, b_ang], [b_ang], out=ang, in0=kf, scalar=-C1_, in1=ang, op0=ALU.mult, op1=ALU.add)
            OP("dve", "scalar_tensor_tensor", [b_kf, b_ang], [b_ang], out=ang, in0=kf, scalar=-C2_, in1=ang, op0=ALU.mult, op1=ALU.add)
            OP("dve", "tensor_scalar", [b_ang], [b_ang], out=ang, in0=ang, scalar1=-3.141592, scalar2=3.141592, op0=ALU.max, op1=ALU.min)
            OP("act", "activation", [b_ang], [b_sn], out=sn, in_=ang, func=AF.Sin)
            OP("act", "activation", [b_ang], [b_cs], out=cs, in_=ang, func=AF.Sin, scale=0.5)
            OP("act", "activation", [b_cs], [b_cs], out=cs, in_=cs, func=AF.Square)
            OP("act", "activation", [b_cs], [b_cs], out=cs, in_=cs, func=AF.Identity, scale=-2.0, bias=1.0)

        def s5_part(l, U, b_U, b_om):
            with ExitStack() as s5:
                G = sb("Gs5", [128, 2, T], stack=s5)
                b_G = [[Buf(f"G{gh}_{tb}") for tb in range(4)] for gh in range(2)]
                pp = sb("pp", [128, 12, 16], stack=s5)
                b_pp = Buf("pp")
                for k_, src in enumerate((s5_lr_p, s5_li_p, s5_ls_p)):
                    DMA("sp", pp[:, k_, :], src[:, l * 16:(l + 1) * 16], [], [b_pp])
                DMA("sp", pp[:, 7, :], s5_h0re[:, l * 16:(l + 1) * 16], [], [b_pp])
                DMA("sp", pp[:, 8, :], s5_h0im[:, l * 16:(l + 1) * 16], [], [b_pp])
                OP("act", "activation", [b_pp], [b_pp], out=pp[:, 9, :], in_=pp[:, 2, :], func=AF.Exp)
                OP("dve", "tensor_tensor", [b_pp], [b_pp], out=pp[:, 3, :], in0=pp[:, 1, :], in1=pp[:, 9, :], op=ALU.mult)
                OP("dve", "tensor_tensor", [b_pp], [b_pp], out=pp[:, 4, :], in0=pp[:, 0, :], in1=pp[:, 9, :], op=ALU.mult)
                OP("act", "activation", [b_pp], [b_pp], out=pp[:, 4, :], in_=pp[:, 4, :], func=AF.Exp)
                ginit = sb("ginit", [128, 2, 16], stack=s5)
                b_gi = Buf("ginit")
                ang0 = sb("ang0", [128, 16], stack=s5)
                OP("dve", "tensor_copy", [b_pp], [b_gi], out=ang0[:, 0:8], in_=pp[:, 3, 0:8])
                OP("dve", "tensor_scalar", [b_pp], [b_gi], out=ang0[:, 8:16], in0=pp[:, 3, 8:16], scalar1=float(T), scalar2=None, op0=ALU.mult)
                ki16 = sb("ki16", [128, 16], I32, stack=s5)
                b_ki = Buf("ki16")
                sincos(ang0[:], ki16[:], pp[:, 9, :], pp[:, 6, :], pp[:, 5, :], b_gi, b_ki, b_pp, b_pp, b_pp)
                OP("dve", "tensor_tensor", [b_pp], [b_pp], out=pp[:, 9, :], in0=pp[:, 7, :], in1=pp[:, 5, :], op=ALU.mult)
                OP("dve", "tensor_tensor", [b_pp], [b_pp], out=pp[:, 10, :], in0=pp[:, 8, :], in1=pp[:, 6, :], op=ALU.mult)
                OP("dve", "tensor_tensor", [b_pp], [b_gi], out=ginit[:, 0, :], in0=pp[:, 9, :], in1=pp[:, 10, :], op=ALU.subtract)
                OP("dve", "tensor_tensor", [b_pp], [b_pp], out=pp[:, 9, :], in0=pp[:, 7, :], in1=pp[:, 6, :], op=ALU.mult)
                OP("dve", "tensor_tensor", [b_pp], [b_pp], out=pp[:, 10, :], in0=pp[:, 8, :], in1=pp[:, 5, :], op=ALU.mult)
                OP("dve", "tensor_tensor", [b_pp], [b_gi], out=ginit[:, 1, :], in0=pp[:, 9, :], in1=pp[:, 10, :], op=ALU.add)
                toff = sb("toff", [128, 16, 4], stack=s5)
                for tb in range(4):
                    OP("dve", "tensor_scalar", [b_pp], [b_gi], out=toff[:, :, tb], in0=pp[:, 3, :], scalar1=float(512 * tb), scalar2=None, op0=ALU.mult)
                tix = sb("tix", [128, 512], stack=s5)
                cmk5 = sb("cmk5", [128, 2, 512], stack=s5)
                b_c5 = Buf("c5")
                DMA("sp", tix[:], c_tix, [], [b_c5])
                DMA("sp", cmk5[:], c_cm5, [], [b_c5])
                dsk = sb("dsk", [128, 2], stack=s5)
                bgl = sb("bgl", [128, 2], stack=s5)
                DMA("sp", dsk[:], s5_D[:, l * 2:(l + 1) * 2], [], [b_c5])
                DMA("sp", bgl[:], s5_bglu[:, l * 2:(l + 1) * 2], [], [b_c5])
                stg = sb("stg", [128, 2, 16, 8], stack=s5)
                b_stg = Buf("stg")
                OP("pool", "memset", [], [b_stg], ap=stg[:], constant=0.0)
                names_ = ["bur", "bui", "sn", "cs", "a1", "a2", "m1", "m2", "m3", "m4", "hr", "hi", "rmk"]
                sets = []
                for si in range(2):
                    Wk_ = {n_: sb(f"w5{si}" + n_, [128, 512], stack=s5) for n_ in names_}
                    Bq_ = {n_: Buf(f"w5{si}" + n_) for n_ in names_}
                    for n_ in ("gr", "gi"):
                        Wk_[n_] = sb(f"w5{si}" + n_, [128, 2], stack=s5)
                        Bq_[n_] = Buf(f"w5{si}" + n_)
                    Wk_["ki"] = sb(f"ki5_{si}", [128, 512], I32, stack=s5)
                    Bq_["ki"] = Buf(f"ki5_{si}")
                    sets.append((Wk_, Bq_))
                Wk, Bk_ = sets[0]
                ps_lo[0] = 4
                for gh in range(2):
                    with ExitStack() as sw5:
                        R_ = {"W2r": sb("r5W2r", [128, 1024], BF16, stack=sw5), "W2i": sb("r5W2i", [128, 1024], BF16, stack=sw5),
                              "Cr": sb("r5Cr", [128, 1024], stack=sw5), "Ci": sb("r5Ci", [128, 1024], stack=sw5)}
                        b_R = {n_: Buf("r5" + n_) for n_ in R_}
                        b_t = Buf("t5")
                        tmap = dict(zip(("lr", "li", "a", "b", "cb", "sb", "nr", "ni", "den", "Br", "Bi", "x1", "x2"), names_))
                        t_ = {k_: Wk[v_] for k_, v_ in tmap.items()}
                        col0 = (l * 2 + gh) * 1024
                        DMA("sp", R_["Cr"][:], s5_Cre[:, col0:col0 + 1024], [], [b_R["Cr"]])
                        DMA("sp", R_["Ci"][:], s5_Cim[:, col0:col0 + 1024], [], [b_R["Ci"]])
                        OP("dve", "tensor_scalar", [b_R["Ci"]], [b_R["Ci"]], out=R_["Ci"][:], in0=R_["Ci"][:], scalar1=-1.0, scalar2=None, op0=ALU.mult)
                        for hf in range(2):
                            col = slice(col0 + hf * 512, col0 + (hf + 1) * 512)
                            oc_ = slice(hf * 512, (hf + 1) * 512)
                            DMA("sp", t_["lr"][:], s5_lr_row[:, col], [b_t], [b_t])
                            DMA("sp", t_["li"][:], s5_li_row[:, col], [b_t], [b_t])
                            DMA("sp", t_["a"][:], s5_ls_row[:, col], [b_t], [b_t])
                            DMA("sp", t_["Br"][:], s5_Bre[:, col], [b_t], [b_t])
                            DMA("sp", t_["Bi"][:], s5_Bim[:, col], [b_t], [b_t])
                            E_ = lambda eng, meth, **kw: OP(eng, meth, [b_t], [b_t], **kw)
                            E_("act", "activation", out=t_["a"][:], in_=t_["a"][:], func=AF.Exp)
                            E_("dve", "tensor_tensor", out=t_["b"][:], in0=t_["li"][:], in1=t_["a"][:], op=ALU.mult)
                            E_("dve", "tensor_tensor", out=t_["a"][:], in0=t_["lr"][:], in1=t_["a"][:], op=ALU.mult)
                            E_("act", "activation", out=t_["a"][:], in_=t_["a"][:], func=AF.Exp)
                            sincos(t_["b"][:], Wk["ki"][:], t_["x1"][:], t_["sb"][:], t_["cb"][:], b_t, b_t, b_t, b_t, b_t)
                            E_("dve", "tensor_tensor", out=t_["nr"][:], in0=t_["a"][:], in1=t_["cb"][:], op=ALU.mult)
                            E_("dve", "tensor_scalar", out=t_["nr"][:], in0=t_["nr"][:], scalar1=-1.0, scalar2=None, op0=ALU.add)
                            E_("dve", "tensor_tensor", out=t_["ni"][:], in0=t_["a"][:], in1=t_["sb"][:], op=ALU.mult)
                            E_("dve", "tensor_tensor", out=t_["den"][:], in0=t_["lr"][:], in1=t_["lr"][:], op=ALU.mult)
                            E_("dve", "tensor_tensor", out=t_["x1"][:], in0=t_["li"][:], in1=t_["li"][:], op=ALU.mult)
                            E_("dve", "tensor_tensor", out=t_["den"][:], in0=t_["den"][:], in1=t_["x1"][:], op=ALU.add)
                            E_("dve", "reciprocal", out=t_["den"][:], in_=t_["den"][:])
                            E_("dve", "tensor_tensor", out=t_["x1"][:], in0=t_["nr"][:], in1=t_["lr"][:], op=ALU.mult)
                            E_("dve", "tensor_tensor", out=t_["x2"][:], in0=t_["ni"][:], in1=t_["li"][:], op=ALU.mult)
                            E_("dve", "tensor_tensor", out=t_["x1"][:], in0=t_["x1"][:], in1=t_["x2"][:], op=ALU.add)
                            E_("dve", "tensor_tensor", out=t_["a"][:], in0=t_["x1"][:], in1=t_["den"][:], op=ALU.mult)
                            E_("dve", "tensor_tensor", out=t_["x1"][:], in0=t_["ni"][:], in1=t_["lr"][:], op=ALU.mult)
                            E_("dve", "tensor_tensor", out=t_["x2"][:], in0=t_["nr"][:], in1=t_["li"][:], op=ALU.mult)
                            E_("dve", "tensor_tensor", out=t_["x1"][:], in0=t_["x1"][:], in1=t_["x2"][:], op=ALU.subtract)
                            E_("dve", "tensor_tensor", out=t_["b"][:], in0=t_["x1"][:], in1=t_["den"][:], op=ALU.mult)
                            E_("dve", "tensor_tensor", out=t_["x1"][:], in0=t_["a"][:], in1=t_["Br"][:], op=ALU.mult)
                            E_("dve", "tensor_tensor", out=t_["x2"][:], in0=t_["b"][:], in1=t_["Bi"][:], op=ALU.mult)
                            OP("dve", "tensor_tensor", [b_t], [b_R["W2r"], b_t], out=R_["W2r"][:, oc_], in0=t_["x1"][:], in1=t_["x2"][:], op=ALU.subtract)
                            E_("dve", "tensor_tensor", out=t_["x1"][:], in0=t_["a"][:], in1=t_["Bi"][:], op=ALU.mult)
                            E_("dve", "tensor_tensor", out=t_["x2"][:], in0=t_["b"][:], in1=t_["Br"][:], op=ALU.mult)
                            OP("dve", "tensor_tensor", [b_t], [b_R["W2i"], b_t], out=R_["W2i"][:, oc_], in0=t_["x1"][:], in1=t_["x2"][:], op=ALU.add)
                        S.barrier()
                        Ybank = [(psb[tb], psB[tb]) for tb in range(4)]
                        nacc = [0] * 4
                        def unit(d, jj, Wk, Bk_):
                            j = gh * 4 + jj
                            dj = d * 8 + j
                            wc = slice((d * 4 + jj) * 128, (d * 4 + jj + 1) * 128)
                            OP("dve", "tensor_scalar", [b_c5, b_pp], [Bk_["rmk"]], out=Wk["rmk"][:], in0=cmk5[:, d, :], scalar1=pp[:, 4, dj:dj + 1], scalar2=None, op0=ALU.mult)
                            prev = None
                            for tb in (range(4) if d == 0 else range(3, -1, -1)):
                                ts = slice(tb * 512, (tb + 1) * 512)
                                pr_, bpr_ = psum()
                                pi_, bpi_ = psum()
                                OP("pe", "matmul", [b_R["W2r"], b_U], [bpr_], out=pr_[:, 0:512], lhsT=R_["W2r"][:, wc], rhs=U[:, gh, ts], start=True, stop=True)
                                OP("pe", "matmul", [b_R["W2i"], b_U], [bpi_], out=pi_[:, 0:512], lhsT=R_["W2i"][:, wc], rhs=U[:, gh, ts], start=True, stop=True)
                                OP("act", "copy", [bpr_], [Bk_["bur"]], out=Wk["bur"][:], in_=pr_[:, 0:512])
                                OP("act", "copy", [bpi_], [Bk_["bui"]], out=Wk["bui"][:], in_=pi_[:, 0:512])
                                yield
                                OP("dve", "tensor_scalar", [b_c5, b_pp, b_gi], [Bk_["a1"]], out=Wk["a1"][:], in0=tix[:], scalar1=pp[:, 3, dj:dj + 1], scalar2=toff[:, dj, tb:tb + 1], op0=ALU.mult, op1=ALU.add)
                                sincos(Wk["a1"][:], Wk["ki"][:], Wk["a2"][:], Wk["sn"][:], Wk["cs"][:], Bk_["a1"], Bk_["ki"], Bk_["a2"], Bk_["sn"], Bk_["cs"])
                                yield
                                OP("dve", "tensor_tensor", [Bk_["bur"], Bk_["cs"]], [Bk_["m1"]], out=Wk["m1"][:], in0=Wk["bur"][:], in1=Wk["cs"][:], op=ALU.mult)
                                OP("pool", "tensor_tensor", [Bk_["bui"], Bk_["sn"]], [Bk_["m2"]], out=Wk["m2"][:], in0=Wk["bui"][:], in1=Wk["sn"][:], op=ALU.mult)
                                OP("dve", "tensor_tensor", [Bk_["bui"], Bk_["cs"]], [Bk_["m3"]], out=Wk["m3"][:], in0=Wk["bui"][:], in1=Wk["cs"][:], op=ALU.mult)
                                OP("dve", "tensor_tensor", [Bk_["bur"], Bk_["sn"]], [Bk_["m4"]], out=Wk["m4"][:], in0=Wk["bur"][:], in1=Wk["sn"][:], op=ALU.mult)
                                OP("dve", "tensor_tensor", [Bk_["m1"], Bk_["m2"]], [Bk_["m1"]], out=Wk["m1"][:], in0=Wk["m1"][:], in1=Wk["m2"][:], op=(ALU.add if d == 0 else ALU.subtract))
                                OP("dve", "tensor_tensor", [Bk_["m3"], Bk_["m4"]], [Bk_["m3"]], out=Wk["m3"][:], in0=Wk["m3"][:], in1=Wk["m4"][:], op=(ALU.subtract if d == 0 else ALU.add))
                                yield
                                if prev is None:
                                    ir, ii = ginit[:, 0, dj:dj + 1], ginit[:, 1, dj:dj + 1]
                                    rd0 = [b_gi]
                                else:
                                    ir, ii = prev
                                    rd0 = [Bk_["gr"], Bk_["gi"]]
                                if d == 0:
                                    OP("dve", "tensor_tensor_scan", [Bk_["m1"], Bk_["rmk"]] + rd0, [Bk_["hr"]], out=Wk["hr"][:], data0=Wk["rmk"][:], data1=Wk["m1"][:], initial=ir, op0=ALU.mult, op1=ALU.add)
                                    OP("dve", "tensor_tensor_scan", [Bk_["m3"], Bk_["rmk"]] + rd0, [Bk_["hi"]], out=Wk["hi"][:], data0=Wk["rmk"][:], data1=Wk["m3"][:], initial=ii, op0=ALU.mult, op1=ALU.add)
                                else:
                                    OP("dve", "tensor_tensor_scan", [Bk_["m1"], Bk_["rmk"]] + rd0, [Bk_["hr"]], out=Wk["hr"][:, ::-1], data0=Wk["rmk"][:, ::-1], data1=Wk["m1"][:, ::-1], initial=ir, op0=ALU.mult, op1=ALU.add)
                                    OP("dve", "tensor_tensor_scan", [Bk_["m3"], Bk_["rmk"]] + rd0, [Bk_["hi"]], out=Wk["hi"][:, ::-1], data0=Wk["rmk"][:, ::-1], data1=Wk["m3"][:, ::-1], initial=ii, op0=ALU.mult, op1=ALU.add)
                                lastc = 511 if d == 0 else 0
                                OP("act", "copy", [Bk_["hr"]], [Bk_["gr"]], out=Wk["gr"][:, 0:1], in_=Wk["hr"][:, lastc:lastc + 1])
                                OP("act", "copy", [Bk_["hi"]], [Bk_["gi"]], out=Wk["gi"][:, 0:1], in_=Wk["hi"][:, lastc:lastc + 1])
                                prev = (Wk["gr"][:, 0:1], Wk["gi"][:, 0:1])
                                yield
                                OP("dve", "tensor_tensor", [Bk_["hr"], Bk_["cs"]], [Bk_["m1"]], out=Wk["m1"][:], in0=Wk["hr"][:], in1=Wk["cs"][:], op=ALU.mult)
                                OP("pool", "tensor_tensor", [Bk_["hi"], Bk_["sn"]], [Bk_["m2"]], out=Wk["m2"][:], in0=Wk["hi"][:], in1=Wk["sn"][:], op=ALU.mult)
                                OP("dve", "tensor_tensor", [Bk_["hr"], Bk_["sn"]], [Bk_["m3"]], out=Wk["m3"][:], in0=Wk["hr"][:], in1=Wk["sn"][:], op=ALU.mult)
                                OP("dve", "tensor_tensor", [Bk_["hi"], Bk_["cs"]], [Bk_["m4"]], out=Wk["m4"][:], in0=Wk["hi"][:], in1=Wk["cs"][:], op=ALU.mult)
                                OP("dve", "tensor_tensor", [Bk_["m1"], Bk_["m2"]], [Bk_["bur"]], out=Wk["bur"][:], in0=Wk["m1"][:], in1=Wk["m2"][:], op=(ALU.subtract if d == 0 else ALU.add))
                                OP("dve", "tensor_tensor", [Bk_["m3"], Bk_["m4"]], [Bk_["bui"]], out=Wk["bui"][:], in0=(Wk["m3"][:] if d == 0 else Wk["m4"][:]), in1=(Wk["m4"][:] if d == 0 else Wk["m3"][:]), op=(ALU.add if d == 0 else ALU.subtract))
                                yield
                                c0_ = 255 if d == 0 else 0
                                OP("act", "copy", [Bk_["bur"]], [b_stg], out=stg[:, 0, dj, 2 * tb:2 * tb + 2], in_=Wk["bur"][:, c0_::256])
                                OP("act", "copy", [Bk_["bui"]], [b_stg], out=stg[:, 1, dj, 2 * tb:2 * tb + 2], in_=Wk["bui"][:, c0_::256])
                                py_, bpy_ = Ybank[tb]
                                first = nacc[tb] == 0
                                nacc[tb] += 2
                                lastm = nacc[tb] == 16
                                OP("pe", "matmul", [b_R["Cr"], Bk_["bur"]], [bpy_], out=py_[:, 0:512], lhsT=R_["Cr"][:, wc], rhs=Wk["bur"][:], start=first, stop=False)
                                OP("pe", "matmul", [b_R["Ci"], Bk_["bui"]], [bpy_], out=py_[:, 0:512], lhsT=R_["Ci"][:, wc], rhs=Wk["bui"][:], start=False, stop=lastm)
                        for jj in range(4):
                            gens = [unit(0, jj, *sets[0]), unit(1, jj, *sets[1])]
                            alive = list(gens)
                            while alive:
                                for g_ in list(alive):
                                    try:
                                        next(g_)
                                    except StopIteration:
                                        alive.remove(g_)
                        for tb in range(4):
                            ts = slice(tb * 512, (tb + 1) * 512)
                            py_, bpy_ = Ybank[tb]
                            y_, y2_, t3_, th_ = Wk["a1"], Wk["a2"], Wk["m1"], Wk["m2"]
                            OP("dve", "scalar_tensor_tensor", [bpy_, b_U, b_c5], [Bk_["a1"]], out=y_[:], in0=U[:, gh, ts], scalar=dsk[:, gh:gh + 1], in1=py_[:, 0:512], op0=ALU.mult, op1=ALU.add)
                            OP("pool", "tensor_tensor", [Bk_["a1"]], [Bk_["a2"]], out=y2_[:], in0=y_[:], in1=y_[:], op=ALU.mult)
                            OP("dve", "tensor_scalar", [Bk_["a2"]], [Bk_["a2"]], out=y2_[:], in0=y2_[:], scalar1=0.044715, scalar2=1.0, op0=ALU.mult, op1=ALU.add)
                            OP("pool", "tensor_tensor", [Bk_["a2"], Bk_["a1"]], [Bk_["m1"]], out=t3_[:], in0=y2_[:], in1=y_[:], op=ALU.mult)
                            OP("act", "activation", [Bk_["m1"]], [Bk_["m2"]], out=th_[:], in_=t3_[:], func=AF.Tanh, scale=0.7978845608028654)
                            OP("dve", "scalar_tensor_tensor", [Bk_["m2"], Bk_["a1"]], [Bk_["m2"]], out=th_[:], in0=th_[:], scalar=1.0, in1=y_[:], op0=ALU.add, op1=ALU.mult)
                            OP("act", "activation", [Bk_["m2"]], [b_G[gh][tb]], out=G[:, gh, ts], in_=th_[:], func=AF.Identity, scale=0.5)
                        S.barrier()
                ps_lo[0] = 0
                DMA("sp", o_s5[l], stg[:].rearrange("p a b c -> p (a b c)"), [b_stg], [], is_out=True)
                with ExitStack() as sgl:
                    Wg5 = sb("Wg5", [128, 2, 256], stack=sgl)
                    b_Wg5 = Buf("Wg5")
                    DMA("sp", Wg5[:], s5_wglu[l].rearrange("(cc p) n -> p cc n", p=128), [], [b_Wg5])
                    omS = sb("omS", [128, 2, T], BF16, stack=sgl)
                    sig = sb("sig", [128, 512], stack=sgl)
                    b_sig = Buf("sig")
                    for co in range(2):
                        for tb in range(4):
                            ts = slice(tb * 512, (tb + 1) * 512)
                            pz, bpz = psum()
                            for cc in range(2):
                                OP("pe", "matmul", [b_Wg5, b_G[cc][tb]], [bpz], out=pz[:, 0:512], lhsT=Wg5[:, cc, co * 128:(co + 1) * 128], rhs=G[:, cc, ts], start=(cc == 0), stop=(cc == 1))
                            OP("act", "activation", [bpz, b_c5], [b_sig], out=sig[:], in_=pz[:, 0:512], func=AF.Sigmoid, bias=bgl[:, co:co + 1])
                            OP("dve", "tensor_tensor", [b_sig, b_G[co][tb]], b_om[tb * 4:(tb + 1) * 4], out=omS[:, co, ts], in0=sig[:], in1=G[:, co, ts], op=ALU.mult)
                    wout_part(l, 6, 2, omS, b_om)
            S.barrier()

        for l in range(DEPTH):
            if dbg_mode == "noffn":
                continue
            ffn_phase(l, 0, 0)
            if enable_mix:
                mixer_phase(l)
            ffn_phase(l, 1, 2)

        mark("final")
        with ExitStack() as so:
            yo = [sb(f"yo{k}", [128, TT], stack=so) for k in range(4)]
            b_yo = [Buf(f"yo{k}") for k in range(4)]
            yc = 0
            for tt in range(NTT):
                ts = slice(tt * TT, (tt + 1) * TT)
                norm_stats(tt)
                for dc in range(8):
                    k = yc % 4
                    yc += 1
                    OP("dve", "scalar_tensor_tensor", [xB[dc][tt], b_rstd, b_fnw], [b_yo[k]], out=yo[k][:], in0=x[:, dc, ts], scalar=fnw[:, dc:dc + 1],
                       in1=rstd[:], op0=ALU.mult, op1=ALU.mult)
                    DMA("sp", yT_out[:, dc, ts], yo[k][:], [b_yo[k]], [], is_out=True)
        S.emit(st)
    return nc


_PROG = {}


def _get_prog(**kw):
    key = tuple(sorted(kw.items()))
    if key not in _PROG:
        _PROG[key] = build_program(**kw)
    return _PROG[key]


def _prep_inputs(inp):
    f = lambda a: np.ascontiguousarray(np.asarray(a, dtype=np.float32))
    xp = f(inp["x_prompt"])
    xs = f(inp["x_sample"])
    c = f(inp["c"])
    c_ctx = f(inp["c_ctx"])
    shared = {
        "w_ada": f(inp["w_ada"]),
        "b_ada": f(np.asarray(inp["b_ada"]).reshape(DEPTH, 72, 128).transpose(0, 2, 1)),
        "norm_w": f(np.asarray(inp["norm_w"]).reshape(DEPTH * 3 * 8, 128).T),
        "fnorm_w": f(np.asarray(inp["final_norm_w"]).reshape(8, 128).T),
        "ffn_w_gate": f(inp["ffn_w_gate"]),
        "ffn_w_up": f(inp["ffn_w_up"]),
        "ffn_w_down": f(inp["ffn_w_down"]),
    }
    idx = np.arange(128)
    shared["w_in"] = f(inp["w_in"])
    shared["w_out"] = f(inp["w_out"])
    shared["gla_up"] = f(np.asarray(inp["gla_gk_up"]).transpose(0, 2, 1, 3).reshape(DEPTH, 16, 256))
    shared["gla_bias"] = f(np.asarray(inp["gla_gk_bias"]).reshape(DEPTH * 2, 128).T)
    shared["gla_nw"] = f(np.broadcast_to(np.tile(np.asarray(inp["gla_norm_w"]), (1, 4))[:, None, :], (DEPTH, 128, 256)))
    cwv = np.asarray(inp["gdn_conv_w"]).reshape(DEPTH, 5, 12, 128)
    shared["gdn_convw"] = f(cwv.transpose(3, 0, 2, 1).reshape(128, DEPTH * 60))
    shared["gdn_dtb"] = f(np.broadcast_to(np.asarray(inp["gdn_dt_bias"]).reshape(1, DEPTH * 8), (128, DEPTH * 8)))
    shared["gdn_alog"] = f(np.broadcast_to(np.asarray(inp["gdn_a_log"]).reshape(1, DEPTH * 8), (128, DEPTH * 8)))
    shared["gdn_nw"] = f(np.broadcast_to(np.tile(np.asarray(inp["gdn_norm_w"]), (1, 4))[:, None, :], (DEPTH, 128, 512)))
    shared["c_mkfs"] = f(idx[:, None] < idx[None, :])
    blk = [(idx[:, None] // 16) == (idx[None, :] // 16)]
    for b_ in (16, 32, 64):
        blk.append(((idx[:, None] // b_) % 2 == 1) & ((idx[None, :] // b_) == (idx[:, None] // b_) - 1))
    for b_ in (16, 32, 64):
        blk.append(((idx[None, :] // b_) % 2 == 1) & ((idx[:, None] // b_) == (idx[None, :] // b_) - 1))
    shared["c_blk"] = f(np.stack(blk, 1))
    shared["c_mkbs"] = f(idx[:, None] > idx[None, :])
    lam_re = np.asarray(inp["s5_lam_re"]); lam_im = np.asarray(inp["s5_lam_im"]); lstep = np.asarray(inp["s5_log_step"])
    def part16(a):
        return f(a.reshape(DEPTH, 2, 8, 2, 64).transpose(3, 4, 0, 1, 2).reshape(128, DEPTH * 16))
    def row2048(a):
        r_ = a.reshape(DEPTH, 2, 2, 4, 2, 64).transpose(0, 2, 1, 3, 4, 5).reshape(1, DEPTH * 2048)
        return f(np.broadcast_to(r_, (128, DEPTH * 2048)))
    ls_full = np.broadcast_to(lstep[..., None], lam_re.shape)
    shared["s5_lr_p"] = part16(lam_re); shared["s5_li_p"] = part16(lam_im); shared["s5_ls_p"] = part16(ls_full)
    shared["s5_lr_row"] = row2048(lam_re); shared["s5_li_row"] = row2048(lam_im); shared["s5_ls_row"] = row2048(ls_full)
    def bpad(b):
        o_ = np.zeros((8, 16, DEPTH, 2, 2, 4, 2, 64), np.float32)
        for g in range(16):
            gh_, j_, gb_ = g // 8, (g // 2) % 4, g % 2
            o_[g % 8, :, :, gh_, :, j_, gb_, :] = b[:, :, g].transpose(3, 0, 1, 2)
        return f(o_.reshape(128, DEPTH * 2048))
    shared["s5_Bre"] = bpad(np.asarray(inp["s5_b_re"])); shared["s5_Bim"] = bpad(np.asarray(inp["s5_b_im"]))
    def cpad(c_):
        o_ = np.zeros((2, 64, DEPTH, 2, 2, 4, 8, 16), np.float32)
        for g in range(16):
            gh_, j_, gb_ = g // 8, (g // 2) % 4, g % 2
            o_[gb_, :, :, gh_, :, j_, g % 8, :] = c_[:, :, g].transpose(3, 0, 1, 2)
        return f(o_.reshape(128, DEPTH * 2048))
    shared["s5_Cre"] = cpad(np.asarray(inp["s5_c_re"])); shared["s5_Cim"] = cpad(np.asarray(inp["s5_c_im"]))
    shared["s5_D"] = f(np.asarray(inp["s5_d"]).reshape(DEPTH * 2, 128).T)
    shared["s5_bglu"] = f(np.asarray(inp["s5_b_glu"]).reshape(DEPTH * 2, 128).T)
    shared["s5_wglu"] = f(inp["s5_w_glu"])
    shared["c_tix"] = f(np.broadcast_to(np.arange(512, dtype=np.float32)[None], (128, 512)))
    shared["c_negpi"] = np.full((128, 1), -np.pi, np.float32)
    shared["c_ident"] = f(np.eye(128))
    shared["c_mkf"] = f(idx[:, None] <= idx[None, :])
    shared["c_mkb"] = f(idx[:, None] >= idx[None, :])
    shared["c_hmask"] = f((idx[:, None] // 32) == np.arange(4)[None, :])
    import os
    if os.environ.get("KDBG", "") == "noffn":
        for k in ("ffn_w_gate", "ffn_w_up", "ffn_w_down"):
            shared.pop(k)
    maps = []
    for r in range(8):
        q = r % 4
        if q < 2:
            xt = xs[q]
            cond = c[q]
        else:
            xt = xp[(q - 2) * 8:(q - 1) * 8].reshape(T, D)
            cond = c_ctx
        m = dict(shared)
        m["xT"] = f(xt.T.reshape(8, 128, T).transpose(1, 0, 2))
        m["cond"] = f(cond.reshape(8, 128).T)
        cy = 1.0 if (q < 2 and os.environ.get("KNOCARRY", "") != "1") else 0.0
        t5 = np.arange(512)
        cm5 = np.stack([np.where(t5 % 256 == 0, cy, 1.0), np.where(t5 % 256 == 255, cy, 1.0)], 0).astype(np.float32)
        m["c_cm5"] = f(np.broadcast_to(cm5[None], (128, 2, 512)))
        if q < 2 and os.environ.get("KNOCARRY", "") != "1":
            m["s5_h0re"] = f(np.asarray(inp["state_s5_re"])[q].reshape(DEPTH, 2, 8, 2, 64).transpose(3, 4, 0, 1, 2).reshape(128, DEPTH * 16))
            m["s5_h0im"] = f(np.asarray(inp["state_s5_im"])[q].reshape(DEPTH, 2, 8, 2, 64).transpose(3, 4, 0, 1, 2).reshape(128, DEPTH * 16))
        else:
            m["s5_h0re"] = np.zeros((128, DEPTH * 16), np.float32)
            m["s5_h0im"] = np.zeros((128, DEPTH * 16), np.float32)
        R_ = 64 if q < 2 else 256
        tpos = np.arange(T) % R_
        cmv = np.stack([((tpos + j) >= 0) & ((tpos + j) < R_) for j in (-2, -1, 1, 2)], 0).astype(np.float32)
        m["c_cmask"] = f(np.broadcast_to(cmv[None], (128, 4, T)))
        if q < 2:
            m["st_gdn"] = f(np.asarray(inp["state_gdn"])[q].transpose(0, 1, 3, 2, 4).reshape(DEPTH, 2, 128, 512))
        else:
            m["st_gdn"] = np.zeros((DEPTH, 2, 128, 512), np.float32)
        if q < 2 and os.environ.get("KNOCARRY", "") == "1":
            m["st_gla"] = np.zeros((DEPTH, 2, 128, 64), np.float32)
            m["c_carry"] = np.zeros((128, 1), np.float32)
            m["st_gdn"] = np.zeros((DEPTH, 2, 128, 512), np.float32)
        elif q < 2:
            m["st_gla"] = f(np.asarray(inp["state_gla"])[q].reshape(DEPTH, 2, 128, 64))
            m["c_carry"] = np.ones((128, 1), np.float32)
        else:
            m["st_gla"] = np.zeros((DEPTH, 2, 128, 64), np.float32)
            m["c_carry"] = np.zeros((128, 1), np.float32)
        maps.append(m)
    return maps


def _run(inp, **kw):
    nc = _get_prog(**kw)
    maps = _prep_inputs(inp)
    res = run_bass_kernel_spmd(nc, maps, core_ids=list(range(8)))
    return res.results


_LAST = {}


def kernel(**inp):
    res = _run(inp)
    _LAST["res"] = res
    ys = []
    for r in range(4):
        yT = np.asarray(res[r]["yT"])
        ys.append(yT.transpose(1, 0, 2).reshape(D, T).T)
    y_sample = np.stack([ys[0], ys[1]], 0).astype(np.float32)
    y_prompt = np.concatenate([ys[2].reshape(8, SL, D), ys[3].reshape(8, SL, D)], 0).astype(np.float32)
    B = 16
    sg = np.concatenate([np.asarray(res[2]["o_stgla"]), np.asarray(res[3]["o_stgla"])], axis=2)
    new_gla = sg.transpose(2, 0, 1, 3, 4).reshape(B, DEPTH, 2, 4, 32, 64).astype(np.float32)
    sd = np.concatenate([np.asarray(res[2]["o_stgdn"]), np.asarray(res[3]["o_stgdn"])], axis=2)
    new_gdn = sd.reshape(DEPTH, 2, B, 128, 4, 128).transpose(2, 0, 1, 4, 3, 5).astype(np.float32)
    if "o_s5" in res[2]:
        s5o = np.stack([np.asarray(res[2]["o_s5"]), np.asarray(res[3]["o_s5"])], 0)
        s5o = s5o.reshape(2, DEPTH, 2, 64, 2, 2, 8, 8)
        s5o = s5o.transpose(4, 0, 7, 1, 5, 6, 2, 3).reshape(2, 16, DEPTH, 2, 16, 64)
        new_re, new_im = np.ascontiguousarray(s5o[0]).astype(np.float32), np.ascontiguousarray(s5o[1]).astype(np.float32)
    else:
        new_re = np.zeros((B, DEPTH, 2, 16, 64), np.float32); new_im = np.zeros((B, DEPTH, 2, 16, 64), np.float32)
    return (y_prompt, y_sample,
            new_gla, np.ascontiguousarray(new_gdn), new_re, new_im)
```
